# Optimizing a Trainium2 kernel written in Bass

```python
import math
import jax, jax.numpy as jnp
from jax import lax
import numpy as np

D_MODEL = 1024
BATCH = 4
SEQ = 4096
DEPTH = 4

N_MIXERS = 3
D_FF = 2816
LN_EPS = 1e-5
DEEPNORM_ALPHA = (2 * DEPTH) ** 0.25
DEEPNORM_BETA = (8 * DEPTH) ** -0.25
A_HEADS = 8
A_HEAD_DIM = D_MODEL // A_HEADS
MOBA_BLOCK = 256
MOBA_TOPK = 3
MOBA_Q_CHUNK = 32
REL_BUCKETS = 32
REL_MAX_EXACT = REL_BUCKETS // 2
REL_MAX_DIST = 128
POOL_WINDOWS = (2, 4, 8, 16)
POOL_GROUP = D_MODEL // len(POOL_WINDOWS)
C_HEADS = 4
C_HEAD_DIM = D_MODEL // C_HEADS
C_CONV = 4
C_CHUNK = 64
N_A = len(range(0, DEPTH, N_MIXERS))
N_B = len(range(1, DEPTH, N_MIXERS))
N_C = len(range(2, DEPTH, N_MIXERS))

kernel_name = "hybrid_moba_pool_mlstm_macaron_deepnorm"


def layer_norm(x, g, b):
    xf = x.astype(jnp.float32)
    mu = jnp.mean(xf, axis=-1, keepdims=True)
    var = jnp.mean(jnp.square(xf - mu), axis=-1, keepdims=True)
    y = (xf - mu) * lax.rsqrt(var + LN_EPS)
    return (y * g.astype(jnp.float32) + b.astype(jnp.float32)).astype(x.dtype)


def swiglu(x, w_gu, w_down):
    g, u = jnp.split(x @ w_gu, 2, axis=-1)
    return (jax.nn.silu(g) * u) @ w_down


def t5_bucket(dist):
    n = jnp.maximum(dist, 0)
    is_small = n < REL_MAX_EXACT
    nf = jnp.maximum(n, 1).astype(jnp.float32)
    large = REL_MAX_EXACT + (jnp.log(nf / REL_MAX_EXACT) / math.log(REL_MAX_DIST / REL_MAX_EXACT)
                             * (REL_BUCKETS - REL_MAX_EXACT)).astype(jnp.int32)
    large = jnp.minimum(large, REL_BUCKETS - 1)
    return jnp.where(is_small, n, large)


def moba_attention(h, w_in, w_out, rel_bias):
    B, S, _ = h.shape
    H, dh, blk, qcs = A_HEADS, A_HEAD_DIM, MOBA_BLOCK, MOBA_Q_CHUNK
    qkv = (h @ w_in).reshape(B, S, 3, H, dh)
    q, k, v = [jnp.transpose(qkv[:, :, j], (0, 2, 1, 3)) for j in range(3)]
    nb = -(-S // blk)
    pad = nb * blk - S
    kb = jnp.pad(k, ((0, 0), (0, 0), (0, pad), (0, 0))).reshape(B, H, nb, blk, dh)
    vb = jnp.pad(v, ((0, 0), (0, 0), (0, pad), (0, 0))).reshape(B, H, nb, blk, dh)
    q_block = jnp.arange(S) // blk
    n_sel = min(MOBA_TOPK, nb - 1)
    scale = dh ** -0.5
    rel_t = rel_bias.T
    bi = jnp.arange(B)[:, None, None, None]
    hi = jnp.arange(H)[None, :, None, None]
    if n_sel > 0:
        k_mean = jnp.mean(kb.astype(jnp.float32), axis=3)
        gate = jnp.einsum('bhsd,bhnd->bhsn', q.astype(jnp.float32), k_mean)
        past = jnp.arange(nb)[None, :] < q_block[:, None]
        gate = jnp.where(past, gate, -jnp.inf)
        _, top_idx = lax.top_k(gate, n_sel)
        top_valid = top_idx < q_block[:, None]

    def chunk_attend(c):
        start = c * qcs
        qc = lax.dynamic_slice_in_dim(q, start, qcs, axis=2)
        qpos = start + jnp.arange(qcs)
        j = start // blk
        ko = lax.dynamic_index_in_dim(kb, j, axis=2, keepdims=False)
        vo = lax.dynamic_index_in_dim(vb, j, axis=2, keepdims=False)
        kpos = j * blk + jnp.arange(blk)
        dist = qpos[:, None] - kpos[None, :]
        bias_own = jnp.transpose(rel_bias[t5_bucket(dist)], (2, 0, 1))
        s_own = jnp.einsum('bhqd,bhkd->bhqk', qc, ko) * scale + bias_own
        s_own = jnp.where(dist >= 0, s_own, -jnp.inf).astype(jnp.float32)
        if n_sel > 0:
            idx = lax.dynamic_slice_in_dim(top_idx, start, qcs, axis=2)
            valid = lax.dynamic_slice_in_dim(top_valid, start, qcs, axis=2)
            kg = kb[bi, hi, idx]
            vg = vb[bi, hi, idx]
            kpos_sel = idx[..., None] * blk + jnp.arange(blk)
            bucket = t5_bucket(qpos[:, None, None] - kpos_sel)
            s_sel = jnp.einsum('bhqd,bhqnkd->bhqnk', qc, kg) * scale + rel_t[hi[..., None], bucket]
            s_sel = jnp.where(valid[..., None], s_sel, -jnp.inf).astype(jnp.float32)
            logits = jnp.concatenate([s_own, s_sel.reshape(B, H, qcs, n_sel * blk)], axis=-1)
            p = jax.nn.softmax(logits, axis=-1).astype(v.dtype)
            out = (jnp.einsum('bhqk,bhkd->bhqd', p[..., :blk], vo)
                   + jnp.einsum('bhqnk,bhqnkd->bhqd', p[..., blk:].reshape(B, H, qcs, n_sel, blk), vg))
        else:
            p = jax.nn.softmax(s_own, axis=-1).astype(v.dtype)
            out = jnp.einsum('bhqk,bhkd->bhqd', p, vo)
        return out

    outs = lax.map(chunk_attend, jnp.arange(S // qcs))
    o = jnp.transpose(outs, (1, 0, 3, 2, 4)).reshape(B, S, H * dh)
    return o @ w_out


def pool_mixer(h, w_in, w_group, scale, w_out):
    B, S, _ = h.shape
    u = h @ w_in
    counts = jnp.arange(1, S + 1, dtype=jnp.float32)
    pooled = []
    for g, w in enumerate(POOL_WINDOWS):
        ug = u[..., g * POOL_GROUP:(g + 1) * POOL_GROUP].astype(jnp.float32)
        cs = jnp.cumsum(ug, axis=1)
        cs_lag = jnp.pad(cs, ((0, 0), (w, 0), (0, 0)))[:, :S]
        mean = (cs - cs_lag) / jnp.minimum(counts, float(w))[:, None]
        pooled.append(mean - ug)
    p = jnp.stack(pooled, axis=2).astype(h.dtype)
    y = jnp.einsum('bsgc,gcd->bsgd', p, w_group).reshape(B, S, D_MODEL) * scale
    return y @ w_out


def causal_conv(x, w):
    S = x.shape[1]
    xp = jnp.pad(x, ((0, 0), (C_CONV - 1, 0), (0, 0)))
    y = w[0] * xp[:, 0:S]
    for j in range(1, C_CONV):
        y = y + w[j] * xp[:, j:j + S]
    return y


def mlstm_chunkwise(q, k, v, i_pre, log_f):
    B, H, S, dh = q.shape
    L = C_CHUNK
    nc = S // L

    def to_chunks(t):
        return jnp.moveaxis(t.reshape((B, H, nc, L) + t.shape[3:]), 2, 0)

    qc, kc, vc, ic = to_chunks(q), to_chunks(k), to_chunks(v), to_chunks(i_pre)
    bc = jnp.cumsum(to_chunks(log_f), axis=-1)
    causal = jnp.tril(jnp.ones((L, L), dtype=bool))

    def step(carry, xs):
        C, n, m = carry
        qx, kx, vx, ix, bx = xs
        d_intra = jnp.where(causal, bx[..., :, None] - bx[..., None, :] + ix[..., None, :], -jnp.inf)
        m_inter = bx + m[..., None]
        m_t = jnp.maximum(m_inter, jnp.max(d_intra, axis=-1))
        w = jnp.exp(d_intra - m_t[..., None]) * jnp.einsum('bhtd,bhsd->bhts', qx, kx)
        s_inter = jnp.exp(m_inter - m_t)
        num = s_inter[..., None] * jnp.einsum('bhtk,bhkv->bhtv', qx, C) + jnp.einsum('bhts,bhsv->bhtv', w, vx)
        den = s_inter * jnp.einsum('bhtk,bhk->bht', qx, n) + jnp.sum(w, axis=-1)
        h = num / jnp.maximum(jnp.abs(den), jnp.exp(-m_t))[..., None]
        b_last = bx[..., -1]
        g = b_last[..., None] - bx + ix
        m_new = jnp.maximum(b_last + m, jnp.max(g, axis=-1))
        decay = jnp.exp(b_last + m - m_new)
        wk = jnp.exp(g - m_new[..., None])
        C = decay[..., None, None] * C + jnp.einsum('bhsk,bhsv->bhkv', kx * wk[..., None], vx)
        n = decay[..., None] * n + jnp.einsum('bhs,bhsk->bhk', wk, kx)
        return (C, n, m_new), h

    init = (jnp.zeros((B, H, dh, dh), jnp.float32), jnp.zeros((B, H, dh), jnp.float32),
            jnp.zeros((B, H), jnp.float32))
    _, hs = lax.scan(step, init, (qc, kc, vc, ic, bc))
    return jnp.moveaxis(hs, 0, 2).reshape(B, H, S, dh)


def mlstm_mixer(h, w_in, b_gates, conv_w, norm_g, w_out):
    B, S, _ = h.shape
    H, dh, D = C_HEADS, C_HEAD_DIM, D_MODEL
    proj = h @ w_in
    qk = jax.nn.silu(causal_conv(proj[..., :2 * D], conv_w))
    v = proj[..., 2 * D:3 * D]
    o_pre = proj[..., 3 * D:4 * D]
    gates = (proj[..., 4 * D:] + b_gates).astype(jnp.float32)

    def heads(t):
        return jnp.transpose(t.reshape(B, S, H, dh).astype(jnp.float32), (0, 2, 1, 3))

    i_pre = jnp.transpose(gates[..., :H], (0, 2, 1))
    log_f = jnp.transpose(jax.nn.log_sigmoid(gates[..., H:]), (0, 2, 1))
    ht = mlstm_chunkwise(heads(qk[..., :D]), heads(qk[..., D:]) * dh ** -0.5, heads(v), i_pre, log_f)
    ht = jnp.transpose(ht, (0, 2, 1, 3))
    hc = jax.nn.sigmoid(o_pre.astype(jnp.float32)).reshape(B, S, H, dh) * ht
    mu = jnp.mean(hc, axis=-1, keepdims=True)
    var = jnp.mean(jnp.square(hc - mu), axis=-1, keepdims=True)
    hn = ((hc - mu) * lax.rsqrt(var + LN_EPS)).reshape(B, S, D) * norm_g.astype(jnp.float32)
    return hn.astype(h.dtype) @ w_out


def setup_inputs(seed: int = 0) -> dict:
    key = jax.random.key(seed)
    ks = jax.random.split(key, 18)
    D, H, GC = D_MODEL, C_HEADS, POOL_GROUP

    def nrm(k, shape, std):
        return jax.random.normal(k, shape, jnp.float32) * std

    x = nrm(ks[0], (BATCH, SEQ, D), 1.0)
    rel_bias = nrm(ks[1], (REL_BUCKETS, A_HEADS), 0.2)
    ln_g = 1.0 + nrm(ks[2], (DEPTH, 3, D), 0.02)
    ln_b = nrm(ks[3], (DEPTH, 3, D), 0.02)
    ffn_w_gu = nrm(ks[4], (DEPTH, 2, D, 2 * D_FF), D ** -0.5)
    ffn_w_down = nrm(ks[5], (DEPTH, 2, D_FF, D), D_FF ** -0.5 * DEEPNORM_BETA)
    a_w_in = nrm(ks[6], (N_A, D, 3 * D), D ** -0.5)
    a_w_out = nrm(ks[7], (N_A, D, D), D ** -0.5 * DEEPNORM_BETA)
    b_w_in = nrm(ks[8], (N_B, D, D), D ** -0.5)
    b_w_group = nrm(ks[9], (N_B, len(POOL_WINDOWS), GC, GC), GC ** -0.5)
    b_scale = 1.0 + nrm(ks[10], (N_B, D), 0.1)
    b_w_out = nrm(ks[11], (N_B, D, D), D ** -0.5 * DEEPNORM_BETA)
    c_w_in = nrm(ks[12], (N_C, D, 4 * D + 2 * H), D ** -0.5)
    c_b_gates = jnp.concatenate([nrm(ks[13], (N_C, H), 0.1),
                                 jnp.linspace(3.0, 6.0, H, dtype=jnp.float32)[None, :] + nrm(ks[14], (N_C, H), 0.1)],
                                axis=-1)
    c_conv_w = nrm(ks[15], (N_C, C_CONV, 2 * D), C_CONV ** -0.5)
    c_norm_g = 1.0 + nrm(ks[16], (N_C, D), 0.02)
    c_w_out = nrm(ks[17], (N_C, D, D), D ** -0.5 * DEEPNORM_BETA)
    return {"x": x, "rel_bias": rel_bias, "ln_g": ln_g, "ln_b": ln_b,
            "ffn_w_gu": ffn_w_gu, "ffn_w_down": ffn_w_down,
            "a_w_in": a_w_in, "a_w_out": a_w_out,
            "b_w_in": b_w_in, "b_w_group": b_w_group, "b_scale": b_scale, "b_w_out": b_w_out,
            "c_w_in": c_w_in, "c_b_gates": c_b_gates, "c_conv_w": c_conv_w, "c_norm_g": c_norm_g,
            "c_w_out": c_w_out}


def reference(x, rel_bias, ln_g, ln_b, ffn_w_gu, ffn_w_down, a_w_in, a_w_out,
              b_w_in, b_w_group, b_scale, b_w_out,
              c_w_in, c_b_gates, c_conv_w, c_norm_g, c_w_out):
    for i in range(DEPTH):
        x = layer_norm(DEEPNORM_ALPHA * x + 0.5 * swiglu(x, ffn_w_gu[i, 0], ffn_w_down[i, 0]),
                       ln_g[i, 0], ln_b[i, 0])
        kind, j = i % N_MIXERS, i // N_MIXERS
        if kind == 0:
            y = moba_attention(x, a_w_in[j], a_w_out[j], rel_bias)
        elif kind == 1:
            y = pool_mixer(x, b_w_in[j], b_w_group[j], b_scale[j], b_w_out[j])
        else:
            y = mlstm_mixer(x, c_w_in[j], c_b_gates[j], c_conv_w[j], c_norm_g[j], c_w_out[j])
        x = layer_norm(DEEPNORM_ALPHA * x + y, ln_g[i, 1], ln_b[i, 1])
        x = layer_norm(DEEPNORM_ALPHA * x + 0.5 * swiglu(x, ffn_w_gu[i, 1], ffn_w_down[i, 1]),
                       ln_g[i, 2], ln_b[i, 2])
    return x
```

```python
import math
from contextlib import ExitStack

import numpy as np
import concourse.bass as bass
import concourse.mybir as mybir
from concourse.bass_utils import run_bass_kernel_spmd

F32 = mybir.dt.float32
BF16 = mybir.dt.bfloat16
AF = mybir.ActivationFunctionType
ALU = mybir.AluOpType
AX = mybir.AxisListType

D = 1024
KC = 8
DFF = 2816
NJ = 22
DEPTH = 4
SEQ = 4096
BATCH = 4
TOK = 2048
TT = 512
NT = TOK // TT
ALPHA = (2 * DEPTH) ** 0.25
LN_EPS = 1e-5
JGROUPS = [8, 7, 7]
TOUT = 3
SEM_LIMIT = 30000


class Buf:
    __slots__ = ("w", "r", "name", "excl")

    def __init__(self, name="", excl=False):
        self.w = []
        self.r = []
        self.name = name
        self.excl = excl


class Eng:
    def __init__(self, name, same_sync):
        self.name = name
        self.prog = []
        self.sem = None
        self.count = 0
        self.waited = {}
        self.same_sync = same_sync
        self.ring = []
        self.ring_pos = 0


class Sched:
    def __init__(self, nc, stack):
        self.nc = nc
        self.stack = stack
        self.nsem = 0
        self.E = {
            "pe": Eng("pe", False),
            "act": Eng("act", True),
            "dve": Eng("dve", True),
            "pool": Eng("pool", True),
            "sp": Eng("sp", False),
        }
        for e in self.E.values():
            e.sem = self.new_sem(e.name)
        self.out_deps = []

    def new_sem(self, name):
        self.nsem += 1
        return self.stack.enter_context(self.nc.semaphore("s%d_%s" % (self.nsem, name)))

    def _ensure(self, e, dep):
        sem, val = dep
        if sem is e.sem and not e.same_sync:
            return
        key = id(sem)
        if e.waited.get(key, (None, 0))[1] >= val:
            return
        e.waited[key] = (sem, val)
        e.prog.append(("wait", sem, val))

    def _deps(self, e, reads, writes):
        for b in reads:
            for d in b.w:
                self._ensure(e, d)
            if b.excl:
                for d in b.r:
                    if d[0] is not e.sem:
                        self._ensure(e, d)
        for b in writes:
            for d in b.w:
                self._ensure(e, d)
            for d in b.r:
                self._ensure(e, d)

    def _commit(self, dep, reads, writes):
        for b in reads:
            b.r.append(dep)
            if len(b.r) > 64:
                b.r = b.r[-64:]
        for b in writes:
            b.w = [dep]
            b.r = []

    def op(self, eng, fn, reads=(), writes=()):
        e = self.E[eng]
        self._deps(e, reads, writes)
        if e.count >= SEM_LIMIT:
            e.sem = self.new_sem(e.name)
            e.count = 0
        e.count += 1
        dep = (e.sem, e.count)
        e.prog.append(("op", fn, e.sem, 1))
        self._commit(dep, reads, writes)
        return dep

    def dma(self, eng, out, in_, reads=(), writes=(), is_output=False, nring=8):
        e = self.E[eng]
        self._deps(e, reads, writes)
        if not e.ring:
            e.ring = [[self.new_sem(e.name + "d"), 0] for _ in range(nring)]
        slot = e.ring[e.ring_pos % len(e.ring)]
        e.ring_pos += 1
        if slot[1] > 0:
            self._ensure(e, (slot[0], slot[1]))
        slot[1] += 16
        dep = (slot[0], slot[1])
        e.prog.append(("op", lambda q, o=out, i=in_: q.dma_start(out=o, in_=i), slot[0], 16))
        self._commit(dep, reads, writes)
        if is_output:
            self.out_deps.append(dep)
        return dep

    def barrier(self, bufs):
        pass

    def finish(self):
        e = self.E["sp"]
        for d in self.out_deps:
            self._ensure(e, d)

    def emit(self, block):
        def run(prog):
            def f(q):
                for it in prog:
                    if it[0] == "wait":
                        q.wait_ge(it[1], it[2])
                    else:
                        it[1](q).then_inc(it[2], it[3])
            return f
        block.tensor(run(self.E["pe"].prog))
        block.scalar(run(self.E["act"].prog))
        block.vector(run(self.E["dve"].prog))
        block.gpsimd(run(self.E["pool"].prog))
        block.sync(run(self.E["sp"].prog))


class Arena:
    def __init__(self, nc, words):
        self.t = nc.alloc_sbuf_tensor("arena", [128, words], F32)
        self.words = words
        self.off = 0

    def f32(self, n):
        a = self.t[:, self.off:self.off + n]
        self.off += n
        assert self.off <= self.words, "SBUF arena overflow %d" % self.off
        return a

    def bf16(self, n):
        assert n % 2 == 0
        return self.f32(n // 2).bitcast(BF16)


NCORES = 4
NSEG = 2
HALO = 16
NDELTA = 1152


class Stream:
    def __init__(self, S, eng, dram, slots, bufs, order, view, idmap):
        self.S, self.eng, self.dram, self.slots, self.bufs = S, eng, dram, slots, bufs
        self.idmap = idmap
        self.order, self.view = order, view
        self.N = len(slots)
        self.issued = 0
        self.done = 0
        self.cursor = 0

    def prefetch(self):
        while self.issued < len(self.order) and self.issued < self.done + self.N:
            i = self.issued
            s = i % self.N
            self.S.dma(self.eng, self.slots[s], self.view(self.dram[self.idmap[self.order[i]]]), writes=[self.bufs[s]])
            self.issued += 1

    def take(self, piece):
        pos = self.cursor
        assert self.order[pos] == piece, (pos, self.order[pos], piece)
        self.cursor += 1
        self.prefetch()
        assert pos < self.issued
        return pos % self.N, pos

    def release(self, pos):
        assert pos == self.done, (pos, self.done)
        self.done = pos + 1
        self.prefetch()


def piece_tables():
    n = DEPTH * 2 * NJ
    gu_mix = {}
    for layer in range(DEPTH):
        gu_mix[layer] = n
        n += {0: 12, 1: 4, 2: 16}[layer % 3]
    m = DEPTH * 2 * NJ
    wd_mix = {}
    for layer in range(DEPTH):
        wd_mix[layer] = m
        m += 8
    return gu_mix, n, wd_mix, m


GU_MIX, NP_GU, WD_MIX, NP_WD = piece_tables()


def phase_pieces(ph):
    kind, layer, f = ph
    if kind == "ffn":
        ids = [(layer * 2 + f) * NJ + j for j in range(NJ)]
        return ids, list(ids)
    mk = layer % 3
    ngu = {0: 12, 1: 4, 2: 16}[mk]
    return [GU_MIX[layer] + i for i in range(ngu)], [WD_MIX[layer] + i for i in range(8)]


ALL_PHASES = []
for _l in range(DEPTH):
    ALL_PHASES += [("ffn", _l, 0), ("mix", _l, 0), ("ffn", _l, 1)]


def used_pieces(phases):
    gu, wdp = set(), set()
    for ph in phases:
        a, b = phase_pieces(ph)
        gu.update(a)
        wdp.update(b)
    return sorted(gu), sorted(wdp)


def build_program(phases):
    nc = bass.Bass("TRN2", target_bir_lowering=False)
    gu_ids, wd_ids = used_pieces(phases)
    gu_map = {p: i for i, p in enumerate(gu_ids)}
    wd_map = {p: i for i, p in enumerate(wd_ids)}
    xT = nc.dram_tensor("xT", [NSEG, 128, KC * TOK], F32, kind="ExternalInput").ap()
    wgu = nc.dram_tensor("wgu", [len(gu_ids), 128, KC * 256], F32, kind="ExternalInput").ap()
    wd = nc.dram_tensor("wd", [len(wd_ids), 128, D], F32, kind="ExternalInput").ap()
    lnp = nc.dram_tensor("lnp", [128, DEPTH * 3 * 2 * KC], F32, kind="ExternalInput").ap()
    pwg = nc.dram_tensor("pwg", [128, 2048], F32, kind="ExternalInput").ap()
    pmisc = nc.dram_tensor("pmisc", [128, 72], F32, kind="ExternalInput").ap()
    yT = nc.dram_tensor("yT", [NSEG, 128, KC * TOK], F32, kind="ExternalOutput").ap()
    uctx = nc.dram_tensor("uctx", [KC, 128, HALO], F32).ap()
    cgw = nc.dram_tensor("cgw", [128, 64], F32, kind="ExternalInput").ap()
    cfrep = nc.dram_tensor("cfrep", [4, 128, KC * 128], F32, kind="ExternalInput").ap()
    cmisc = nc.dram_tensor("cmisc", [128, 80], F32, kind="ExternalInput").ap()
    cconst = nc.dram_tensor("cconst", [128, 384], F32, kind="ExternalInput").ap()
    cctx = nc.dram_tensor("cctx", [16, 128, 4], F32).ap()
    asel = nc.dram_tensor("asel", [16, 16 * 128], F32, kind="ExternalInput").ap()
    amask = nc.dram_tensor("amask", [128, 512], F32, kind="ExternalInput").ap()
    aconst = nc.dram_tensor("aconst", [128, 256], F32, kind="ExternalInput").ap()
    arb31 = nc.dram_tensor("arb31", [128, 8], F32, kind="ExternalInput").ap()
    arbx = nc.dram_tensor("arbx", [33, 8], F32, kind="ExternalInput").ap()
    aoh = nc.dram_tensor("aoh", [33, NDELTA], F32, kind="ExternalInput").ap()
    tvd = nc.dram_tensor("tvd", [8, NDELTA], BF16).ap()
    ktctx_all = [nc.dram_tensor("ktctx%d" % j, [8, 128, TOK], BF16).ap() for j in range(2)]
    vctx_all = [nc.dram_tensor("vctx%d" % j, [4, 128, 16 * 256], BF16).ap() for j in range(2)]
    kmctx_all = [nc.dram_tensor("kmctx%d" % j, [128, 64], F32).ap() for j in range(2)]
    tv_state = {}
    DB = {}

    def db(name):
        if name not in DB:
            DB[name] = Buf(name)
        return DB[name]
    sctx = nc.dram_tensor("sctx", [4, 128, 768], F32).ap()

    gu_order, wd_order = [], []
    for sg in range(NSEG):
        for ph in phases:
            a, b = phase_pieces(ph)
            gu_order += a
            wd_order += b

    stack = ExitStack()
    with stack:
        S = Sched(nc, stack)
        A = Arena(nc, 53000)
        X = A.f32(KC * TOK).rearrange("p (k t) -> p k t", k=KC)
        Xb = A.bf16(KC * TOK).rearrange("p (k t) -> p k t", k=KC)
        bX = [[Buf("X%d_%d" % (k, t)) for t in range(NT)] for k in range(KC)]
        bXb = [[Buf("Xb%d_%d" % (k, t)) for t in range(NT)] for k in range(KC)]
        LNP = A.f32(DEPTH * 3 * 2 * KC)
        bLNP = Buf("lnp")
        ones = A.bf16(128)
        bones = Buf("ones")
        NGU = 4
        WGU = [A.bf16(KC * 256).rearrange("p (k f) -> p k f", k=KC) for _ in range(NGU)]
        bWGU = [Buf("wgu%d" % i) for i in range(NGU)]
        NWD = 8
        WD = [A.bf16(D) for _ in range(NWD)]
        bWD = [Buf("wd%d" % i) for i in range(NWD)]
        MEAN = A.f32(TT)
        RSTD = A.f32(TT)
        NMR = A.f32(TT)
        bST = Buf("stats")
        YT = [A.f32(TT) for _ in range(2)]
        bYT = [Buf("yt%d" % i) for i in range(2)]
        scratch_base = A.off
        ZB = A.bf16(KC * TT).rearrange("p (k t) -> p k t", k=KC)
        bZB = Buf("zb")
        ZQ = A.bf16(KC * TT).rearrange("p (k t) -> p k t", k=KC)
        bZQ = Buf("zq")
        HG = A.bf16(8 * TOK).rearrange("p (j t) -> p j t", j=8)
        bHG = [[Buf("h%d_%d" % (j, t)) for t in range(NT)] for j in range(8)]
        SIL = [A.f32(TT) for _ in range(2)]
        bSIL = [Buf("sil%d" % i) for i in range(2)]
        PS = [nc.alloc_psum_tensor("ps%d" % i, [128, 512], F32)[:, :] for i in range(8)]
        bPS = [Buf("ps%d" % i, excl=True) for i in range(8)]

        guS = Stream(S, "pool", wgu, WGU, bWGU, gu_order, lambda a: a.rearrange("p (k f) -> p k f", k=KC), gu_map)
        wdS = Stream(S, "pool", wd, WD, bWD, wd_order, lambda a: a, wd_map)

        def barrier():
            deps = []
            for e in S.E.values():
                if e.count > 0:
                    deps.append((e.sem, e.count))
                for slot in e.ring:
                    if slot[1] > 0:
                        deps.append((slot[0], slot[1]))
            for e in S.E.values():
                for d in deps:
                    S._ensure(e, d)

        S.dma("sp", LNP, lnp, writes=[bLNP])
        S.op("dve", lambda q: q.memset(ones, 1.0 / 1024.0), writes=[bones])

        def ln_col(layer, which, gb, k):
            c = ((layer * 3 + which) * 2 + gb) * KC + k
            return LNP[:, c:c + 1]

        def layer_norm(layer, which):
            for t in range(NT):
                ln_tile(layer, which, t)

        def ln_tile(layer, which, t):
            if True:
                ts = slice(t * TT, (t + 1) * TT)
                allx = [bX[k][t] for k in range(KC)]
                S.op("act", lambda q, ts=ts: q.copy(ZB, X[:, :, ts]), reads=allx, writes=[bZB])
                S.op("act", lambda q, ts=ts: q.activation(ZQ, X[:, :, ts], AF.Square), reads=allx, writes=[bZQ])
                for k in range(KC):
                    S.op("pe", lambda q, k=k: q.matmul(PS[6], ones, ZB[:, k, :], start=(k == 0), stop=(k == KC - 1)),
                         reads=[bones, bZB], writes=[bPS[6]])
                for k in range(KC):
                    S.op("pe", lambda q, k=k: q.matmul(PS[7], ones, ZQ[:, k, :], start=(k == 0), stop=(k == KC - 1)),
                         reads=[bones, bZQ], writes=[bPS[7]])
                S.op("act", lambda q: q.copy(MEAN, PS[6]), reads=[bPS[6]], writes=[bST])
                S.op("dve", lambda q: q.tensor_tensor(NMR, MEAN, MEAN, ALU.mult), reads=[bST], writes=[bST])
                S.op("dve", lambda q: q.tensor_tensor(RSTD, PS[7], NMR, ALU.subtract), reads=[bPS[7], bST], writes=[bST])
                S.op("dve", lambda q: q.tensor_scalar_add(RSTD, RSTD, LN_EPS), reads=[bST], writes=[bST])
                S.op("act", lambda q: q.sqrt(RSTD, RSTD), reads=[bST], writes=[bST])
                S.op("dve", lambda q: q.reciprocal(RSTD, RSTD), reads=[bST], writes=[bST])
                S.op("dve", lambda q: q.scalar_tensor_tensor(NMR, MEAN, -1.0, RSTD, ALU.mult, ALU.mult), reads=[bST], writes=[bST])
                for k in range(KC):
                    y = YT[k % 2]
                    by = bYT[k % 2]
                    S.op("dve", lambda q, k=k, ts=ts, y=y: q.tensor_tensor(y, X[:, k, ts], RSTD, ALU.mult),
                         reads=[bX[k][t], bST], writes=[by])
                    S.op("dve", lambda q, y=y: q.tensor_tensor(y, y, NMR, ALU.add), reads=[by, bST], writes=[by])
                    g = ln_col(layer, which, 0, k)
                    b = ln_col(layer, which, 1, k)
                    S.op("act", lambda q, k=k, ts=ts, y=y, g=g, b=b: q.activation(X[:, k, ts], y, AF.Identity, bias=b, scale=g),
                         reads=[by, bLNP], writes=[bX[k][t]])
                    S.op("act", lambda q, k=k, ts=ts, y=y, g=g, b=b: q.activation(Xb[:, k, ts], y, AF.Identity, bias=b, scale=g),
                         reads=[by, bLNP], writes=[bXb[k][t]])

        def add_into_x(i, t, ps_idx, first):
            ts = slice(t * TT, (t + 1) * TT)
            if first:
                S.op("dve", lambda q: q.scalar_tensor_tensor(X[:, i, ts], X[:, i, ts], ALPHA, PS[ps_idx], ALU.mult, ALU.add),
                     reads=[bPS[ps_idx], bX[i][t]], writes=[bX[i][t]])
            else:
                S.op("dve", lambda q: q.tensor_tensor(X[:, i, ts], X[:, i, ts], PS[ps_idx], ALU.add),
                     reads=[bPS[ps_idx], bX[i][t]], writes=[bX[i][t]])

        gu_cnt = [0]

        def gu_tile(s, jl, t):
            ts = slice(t * TT, (t + 1) * TT)
            par = gu_cnt[0] % 2
            gu_cnt[0] += 1
            pg = 2 * par
            pu = pg + 1
            for k in range(KC):
                S.op("pe", lambda q, k=k: q.matmul(PS[pg], WGU[s][:, k, 0:128], Xb[:, k, ts], start=(k == 0), stop=(k == KC - 1)),
                     reads=[bWGU[s], bXb[k][t]], writes=[bPS[pg]])
            for k in range(KC):
                S.op("pe", lambda q, k=k: q.matmul(PS[pu], WGU[s][:, k, 128:256], Xb[:, k, ts], start=(k == 0), stop=(k == KC - 1)),
                     reads=[bWGU[s], bXb[k][t]], writes=[bPS[pu]])
            sl = SIL[par]
            bsl = bSIL[par]
            S.op("act", lambda q: q.activation(sl, PS[pg], AF.Silu), reads=[bPS[pg]], writes=[bsl])
            S.op("dve", lambda q: q.scalar_tensor_tensor(HG[:, jl, ts], PS[pu], 0.5, sl, ALU.mult, ALU.mult),
                 reads=[bPS[pu], bsl], writes=[bHG[jl][t]])

        def ffn(layer, f):
            fi = layer * 2 + f
            which = 0 if f == 0 else 2
            jbase = 0
            ngrp = len(JGROUPS)
            for gi, gsz in enumerate(JGROUPS):
                jstart = 0
                if gi == 0:
                    held = [guS.take(fi * NJ + jbase + jl) for jl in range(TOUT)]
                    for t in range(NT):
                        for jl in range(TOUT):
                            gu_tile(held[jl][0], jl, t)
                    for (_, gpos) in held:
                        guS.release(gpos)
                    jstart = TOUT
                if True:
                    for jl in range(jstart, gsz):
                        s, gpos = guS.take(fi * NJ + jbase + jl)
                        for t in range(NT):
                            gu_tile(s, jl, t)
                        guS.release(gpos)
                slots = [wdS.take(fi * NJ + jbase + jl) for jl in range(gsz)]
                for t in range(NT):
                    ts = slice(t * TT, (t + 1) * TT)
                    for i in range(KC):
                        po = 4 + (i % 2)
                        for jl in range(gsz):
                            sw = slots[jl][0]
                            S.op("pe", lambda q, i=i, jl=jl, ts=ts, sw=sw, po=po, gsz=gsz: q.matmul(PS[po], WD[sw][:, i * 128:(i + 1) * 128], HG[:, jl, ts], start=(jl == 0), stop=(jl == gsz - 1)),
                                 reads=[bWD[sw], bHG[jl][t]], writes=[bPS[po]])
                        add_into_x(i, t, po, gi == 0)
                    if gi == ngrp - 1:
                        ln_tile(layer, which, t)
                for (_, wpos) in slots:
                    wdS.release(wpos)
                jbase += gsz

        def pool_mixer(layer, sg):
            barrier()
            A.off = scratch_base
            W = HALO + TOK
            UF = A.f32(W)
            T1 = A.f32(W)
            T2 = A.f32(W)
            TMP = A.f32(HALO)
            Pm = A.bf16(2 * TOK).rearrange("p (c t) -> p c t", c=2)
            Y1 = A.bf16(2 * TOK).rearrange("p (c t) -> p c t", c=2)
            WGS = A.bf16(2048)
            WGS4 = WGS.rearrange("p (m i f) -> p m i f", m=4, i=2)
            PMI = A.f32(72)
            bUF, bT1, bT2, bTMP, bWGS, bPMI = Buf(), Buf(), Buf(), Buf(), Buf(), Buf()
            bPm = [[Buf() for _ in range(NT)] for _ in range(2)]
            bY1 = [[Buf() for _ in range(NT)] for _ in range(2)]
            S.dma("pool", WGS, pwg, writes=[bWGS])
            S.dma("sp", PMI, pmisc, writes=[bPMI])
            for m in range(4):
                s, gpos = guS.take(GU_MIX[layer] + m)
                w = 2 ** (m + 1)
                for oc in range(2):
                    c = 2 * m + oc
                    if sg == 0:
                        S.op("dve", lambda q: q.memset(UF[:, 0:HALO], 0.0), writes=[bUF])
                    else:
                        S.dma("sp", UF[:, 0:HALO], uctx[c], reads=[db("u%d" % c)], writes=[bUF])
                    for t in range(NT):
                        ts = slice(t * TT, (t + 1) * TT)
                        p = t % 4
                        for k in range(KC):
                            S.op("pe", lambda q, k=k, ts=ts, s=s, p=p, oc=oc: q.matmul(PS[p], WGU[s][:, k, oc * 128:(oc + 1) * 128], Xb[:, k, ts], start=(k == 0), stop=(k == KC - 1)),
                                 reads=[bWGU[s], bXb[k][t]], writes=[bPS[p]])
                        S.op("act", lambda q, t=t, p=p: q.copy(UF[:, HALO + t * TT:HALO + (t + 1) * TT], PS[p]), reads=[bPS[p]], writes=[bUF])
                    if sg == 0:
                        S.dma("sp", uctx[c], UF[:, TOK:TOK + HALO], reads=[bUF], writes=[db("u%d" % c)])
                    cur, bcur = UF, bUF
                    tmps = [(T1, bT1), (T2, bT2)]
                    for i in range(m + 1):
                        sh = 1 << i
                        lo = (2 << i) - 1
                        nxt, bn = tmps[i % 2]
                        S.op("dve", lambda q, nxt=nxt, cur=cur, lo=lo, sh=sh: q.tensor_tensor(nxt[:, lo:W], cur[:, lo:W], cur[:, lo - sh:W - sh], ALU.add),
                             reads=[bcur], writes=[bn])
                        cur, bcur = nxt, bn
                    S.op("dve", lambda q, cur=cur, oc=oc, w=w: q.scalar_tensor_tensor(Pm[:, oc, :], cur[:, HALO:W], 1.0 / w, UF[:, HALO:W], ALU.mult, ALU.subtract),
                         reads=[bcur, bUF], writes=bPm[oc])
                    if sg == 0:
                        S.op("dve", lambda q, cur=cur, m=m: q.tensor_tensor(TMP, cur[:, HALO:2 * HALO], PMI[:, m * HALO:(m + 1) * HALO], ALU.mult),
                             reads=[bcur, bPMI], writes=[bTMP])
                        S.op("dve", lambda q, oc=oc: q.tensor_tensor(Pm[:, oc, 0:HALO], TMP, UF[:, HALO:2 * HALO], ALU.subtract),
                             reads=[bTMP, bUF], writes=[bPm[oc][0]])
                guS.release(gpos)
                for oc in range(2):
                    c = 2 * m + oc
                    for t in range(NT):
                        ts = slice(t * TT, (t + 1) * TT)
                        p = (2 * oc + t) % 4
                        for ic in range(2):
                            S.op("pe", lambda q, ic=ic, oc=oc, ts=ts, p=p, m=m: q.matmul(PS[p], WGS4[:, m, ic, oc * 128:(oc + 1) * 128], Pm[:, ic, ts], start=(ic == 0), stop=(ic == 1)),
                                 reads=[bWGS, bPm[ic][t]], writes=[bPS[p]])
                        S.op("act", lambda q, oc=oc, ts=ts, p=p, c=c: q.activation(Y1[:, oc, ts], PS[p], AF.Identity, scale=PMI[:, 64 + c:65 + c]),
                             reads=[bPS[p], bPMI], writes=[bY1[oc][t]])
                for oc in range(2):
                    c = 2 * m + oc
                    sw, wpos = wdS.take(WD_MIX[layer] + c)
                    for t in range(NT):
                        ts = slice(t * TT, (t + 1) * TT)
                        for i in range(KC):
                            po = 4 + (i % 2)
                            S.op("pe", lambda q, i=i, oc=oc, ts=ts, sw=sw, po=po: q.matmul(PS[po], WD[sw][:, i * 128:(i + 1) * 128], Y1[:, oc, ts], start=True, stop=True),
                                 reads=[bWD[sw], bY1[oc][t]], writes=[bPS[po]])
                            add_into_x(i, t, po, c == 0)
                    wdS.release(wpos)
            barrier()

        def moba_mixer(layer, sg):
            barrier()
            lj = layer // 3
            ktctx, vctx, kmctx = ktctx_all[lj], vctx_all[lj], kmctx_all[lj]
            A.off = scratch_base
            QH = A.bf16(TOK)
            QL = A.bf16(TOK)
            QB = A.bf16(TOK)
            KMH = A.bf16(16)
            KML = A.bf16(16)
            KTA = A.bf16(2 * TOK)
            VP = A.bf16(32 * 256).rearrange("p (a f) -> p a f", a=32)
            KMA = A.f32(16)
            SELT = A.bf16(TOK)
            BTP = A.bf16(5 * TT).rearrange("p (r t) -> p r t", r=5)
            SEL = A.bf16(16 * 128)
            MK = A.f32(512)
            CJ = A.f32(256)
            JB = A.bf16(128)
            IDB = A.bf16(128)
            ONEB = A.bf16(128)
            RB31 = A.f32(8)
            GMA = A.f32(256)
            G2 = A.f32(256)
            GE = A.f32(256)
            MX = A.f32(16)
            SELBA = A.bf16(256)
            base0 = A.off
            if sg == 0:
                RBX = A.f32(8)
                RBH = A.bf16(8)
                RBL = A.bf16(8)
                OH = A.f32(NDELTA)
                OHB = A.bf16(NDELTA)
                TVS = A.bf16(NDELTA)
                assert A.off <= A.words, A.off
            A.off = base0
            NPT = 4
            PT = [A.bf16(TT) for _ in range(NPT)]
            OT = A.bf16(TT)
            REC = A.f32(TT)
            SUMP = A.f32(TT)
            SH = A.bf16(TT)
            SL = A.bf16(TT)
            assert A.off <= A.words, A.off
            bQ, bKT, bVP, bKM, bSELT, bBTP, bOT, bREC, bK, bGM, bSB = Buf(), Buf(), Buf(), Buf(), Buf(), Buf(), Buf(), Buf(), Buf(), Buf(), Buf()
            bPT = [Buf() for _ in range(NPT)]
            bSUM = Buf()
            PMASK = MK[:, 0:256].rearrange("p (g n) -> p g n", g=16)
            NOTOWN = MK[:, 256:512].rearrange("p (g n) -> p g n", g=16)
            S.dma("pool", SEL[0:16, :], asel, writes=[bK])
            S.dma("sp", MK, amask, writes=[bK])
            S.dma("sp", CJ, aconst, writes=[bK])
            S.dma("sp", RB31, arb31, writes=[bK])
            S.op("act", lambda q: q.copy(JB, CJ[:, 0:128]), reads=[bK], writes=[bK])
            S.op("act", lambda q: q.copy(IDB, CJ[:, 128:256]), reads=[bK], writes=[bK])
            S.op("dve", lambda q: q.memset(ONEB, 1.0), writes=[bK])
            if sg == 0:
                S.dma("sp", RBX[0:33, :], arbx, writes=[bK])
                S.dma("sp", OH[0:33, :], aoh, writes=[bK])
                S.op("act", lambda q: q.copy(RBH[0:33, :], RBX[0:33, :]), reads=[bK], writes=[bK])
                S.op("dve", lambda q: q.tensor_tensor(RBL[0:33, :], RBX[0:33, :], RBH[0:33, :], ALU.subtract), reads=[bK], writes=[bK])
                S.op("act", lambda q: q.copy(OHB[0:33, :], OH[0:33, :]), reads=[bK], writes=[bK])
                for c3 in range(3):
                    S.op("pe", lambda q, c3=c3: q.matmul(PS[c3][0:8, 0:384], RBH[0:33, 0:8], OHB[0:33, c3 * 384:(c3 + 1) * 384], start=True, stop=False),
                         reads=[bK], writes=[bPS[c3]])
                    S.op("pe", lambda q, c3=c3: q.matmul(PS[c3][0:8, 0:384], RBL[0:33, 0:8], OHB[0:33, c3 * 384:(c3 + 1) * 384], start=False, stop=True),
                         reads=[bK], writes=[bPS[c3]])
                    S.op("act", lambda q, c3=c3: q.copy(TVS[0:8, c3 * 384:(c3 + 1) * 384], PS[c3][0:8, 0:384]), reads=[bPS[c3]], writes=[bK])
                S.dma("sp", tvd, TVS[0:8, :], reads=[bK], writes=[db("tvd")])
                barrier()
            for hp in range(4):
                sv, pv = guS.take(GU_MIX[layer] + 3 * hp)
                if sg == 1:
                    S.dma("sp", VP[:, 0:16, :], vctx[hp].rearrange("p (a f) -> p a f", a=16), reads=[db("L%dvc%d" % (lj, hp))], writes=[bVP])
                for tt in range(16):
                    p = 6 + (tt % 2)
                    for k in range(KC):
                        S.op("pe", lambda q, tt=tt, k=k, p=p, sv=sv: q.matmul(PS[p][:, 0:256], Xb[:, k, tt * 128:(tt + 1) * 128], WGU[sv][:, k, :], start=(k == 0), stop=(k == KC - 1)),
                             reads=[bWGU[sv], bXb[k][tt // 4]], writes=[bPS[p]])
                    S.op("act", lambda q, tt=tt, p=p: q.copy(VP[:, sg * 16 + tt, :], PS[p][:, 0:256]), reads=[bPS[p]], writes=[bVP])
                guS.release(pv)
                if sg == 0:
                    S.dma("sp", vctx[hp].rearrange("p (a f) -> p a f", a=16), VP[:, 0:16, :], reads=[bVP], writes=[db("L%dvc%d" % (lj, hp))])
                for hh in range(2):
                    h = 2 * hp + hh
                    sa, pa = guS.take(GU_MIX[layer] + 3 * hp + 1 + hh)
                    sw, wpos = wdS.take(WD_MIX[layer] + h)
                    S.op("dve", lambda q: q.memset(KMA, 0.0), writes=[bKM])
                    if sg == 1:
                        S.dma("sp", KTA[:, 0:TOK], ktctx[h], reads=[db("L%dkt%d" % (lj, h))], writes=[bKT])
                        S.dma("sp", KMA[:, 0:8], kmctx[:, h * 8:(h + 1) * 8], reads=[db("L%dkm%d" % (lj, h))], writes=[bKM])
                    for t in range(NT):
                        ts = slice(t * TT, (t + 1) * TT)
                        for k in range(KC):
                            S.op("pe", lambda q, k=k, ts=ts, sa=sa: q.matmul(PS[6], WGU[sa][:, k, 0:128], Xb[:, k, ts], start=(k == 0), stop=(k == KC - 1)),
                                 reads=[bWGU[sa], bXb[k][t]], writes=[bPS[6]])
                        S.op("act", lambda q, ts=ts: q.copy(QH[:, ts], PS[6]), reads=[bPS[6]], writes=[bQ])
                        S.op("dve", lambda q, ts=ts: q.tensor_tensor(QL[:, ts], PS[6], QH[:, ts], ALU.subtract), reads=[bPS[6], bQ], writes=[bQ])
                        S.op("act", lambda q, ts=ts: q.activation(QB[:, ts], PS[6], AF.Identity, scale=float(128 ** -0.5)), reads=[bPS[6]], writes=[bQ])
                        for k in range(KC):
                            S.op("pe", lambda q, k=k, ts=ts, sa=sa: q.matmul(PS[7], WGU[sa][:, k, 128:256], Xb[:, k, ts], start=(k == 0), stop=(k == KC - 1)),
                                 reads=[bWGU[sa], bXb[k][t]], writes=[bPS[7]])
                        S.op("act", lambda q, t=t: q.copy(KTA[:, sg * TOK + t * TT:sg * TOK + (t + 1) * TT], PS[7]), reads=[bPS[7]], writes=[bKT])
                        S.op("dve", lambda q, t=t: q.tensor_reduce(KMA[:, sg * 8 + 2 * t:sg * 8 + 2 * t + 2], PS[7].rearrange("p (b l) -> p b l", b=2), AX.X, ALU.add),
                             reads=[bPS[7]], writes=[bKM])
                    guS.release(pa)
                    if sg == 0:
                        S.dma("sp", ktctx[h], KTA[:, 0:TOK], reads=[bKT], writes=[db("L%dkt%d" % (lj, h))])
                        S.dma("sp", kmctx[:, h * 8:(h + 1) * 8], KMA[:, 0:8], reads=[bKM], writes=[db("L%dkm%d" % (lj, h))])
                    for r in range(5):
                        src = bass.AP(tvd.tensor, h * NDELTA + 512 - 128 * r, [[1, 128], [1, TT]])
                        S.dma("sp", BTP[:, r, :], src, reads=[db("tvd")], writes=[bBTP])
                    S.op("act", lambda q: q.copy(KMH, KMA), reads=[bKM], writes=[bKM])
                    S.op("dve", lambda q: q.tensor_tensor(KML, KMA, KMH, ALU.subtract), reads=[bKM], writes=[bKM])
                    for qt in range(16):
                        qs = slice(qt * 128, (qt + 1) * 128)
                        go = PS[6][:, qt * 16:(qt + 1) * 16]
                        S.op("pe", lambda q, qs=qs, go=go: q.matmul(go, QH[:, qs], KMH, start=True, stop=False), reads=[bQ, bKM], writes=[bPS[6]])
                        S.op("pe", lambda q, qs=qs, go=go: q.matmul(go, QH[:, qs], KML, start=False, stop=False), reads=[bQ, bKM], writes=[bPS[6]])
                        S.op("pe", lambda q, qs=qs, go=go: q.matmul(go, QL[:, qs], KMH, start=False, stop=True), reads=[bQ, bKM], writes=[bPS[6]])
                    g4 = lambda a: a.rearrange("p (g two n) -> p g two n", g=8, two=2)
                    pm4 = PMASK[:, sg * 8:sg * 8 + 8, :].unsqueeze(2).to_broadcast([128, 8, 2, 16])
                    no4 = NOTOWN[:, sg * 8:sg * 8 + 8, :].unsqueeze(2).to_broadcast([128, 8, 2, 16])
                    g3 = lambda a: a.rearrange("p (t n) -> p t n", t=16)
                    bc = lambda a: a.unsqueeze(2).to_broadcast([128, 16, 16])
                    S.op("dve", lambda q: q.tensor_tensor(g4(GMA), g4(PS[6][:, 0:256]), pm4, ALU.add), reads=[bPS[6], bK], writes=[bGM])
                    S.op("dve", lambda q: q.tensor_reduce(MX, g3(GMA), AX.X, ALU.max), reads=[bGM], writes=[bSB])
                    S.op("dve", lambda q: q.tensor_tensor(g3(GE), g3(GMA), bc(MX), ALU.is_ge), reads=[bGM, bSB], writes=[bSB])
                    S.op("dve", lambda q: q.scalar_tensor_tensor(G2, GE, -1e30, GMA, ALU.mult, ALU.add), reads=[bGM, bSB], writes=[bSB])
                    S.op("dve", lambda q: q.tensor_reduce(MX, g3(G2), AX.X, ALU.max), reads=[bSB], writes=[bSB])
                    S.op("dve", lambda q: q.tensor_tensor(g3(GE), g3(G2), bc(MX), ALU.is_ge), reads=[bSB], writes=[bSB])
                    S.op("dve", lambda q: q.scalar_tensor_tensor(G2, GE, -1e30, G2, ALU.mult, ALU.add), reads=[bSB], writes=[bSB])
                    S.op("dve", lambda q: q.tensor_reduce(MX, g3(G2), AX.X, ALU.max), reads=[bSB], writes=[bSB])
                    S.op("dve", lambda q: q.tensor_scalar_max(MX, MX, -1e29), reads=[bSB], writes=[bSB])
                    S.op("dve", lambda q: q.tensor_tensor(g3(GE), g3(GMA), bc(MX), ALU.is_ge), reads=[bGM, bSB], writes=[bSB])
                    S.op("dve", lambda q: q.tensor_scalar(GE, GE, 30000.0, -30000.0, ALU.mult, ALU.add), reads=[bSB], writes=[bSB])
                    S.op("dve", lambda q: q.tensor_tensor(g4(SELBA), g4(GE), no4, ALU.mult), reads=[bSB, bK], writes=[bSB])
                    for qt in range(16):
                        S.op("pe", lambda q, qt=qt: q.matmul(PS[7][0:16, (qt % 4) * 128:(qt % 4 + 1) * 128], SELBA[:, qt * 16:(qt + 1) * 16], IDB, start=True, stop=True),
                             reads=[bSB, bK], writes=[bPS[7]])
                        if qt % 4 == 3:
                            T = qt // 4
                            S.op("act", lambda q, T=T: q.copy(SELT[0:16, T * TT:(T + 1) * TT], PS[7][0:16, :]), reads=[bPS[7]], writes=[bSELT])
                    pending = [None, None]
                    for T in range(NT):
                        gT = sg * 4 + T
                        ts = slice(T * TT, (T + 1) * TT)
                        nch = 4 * gT + 4
                        psb = [0, 1, 6, 7]

                        def stage1(c, ts=ts, gT=gT, h=h):
                            n = c // 2
                            r = c - 4 * gT
                            near = r >= -1
                            ps = psb[c % NPT]
                            pt = PT[c % NPT]
                            bpt = bPT[c % NPT]
                            S.op("pe", lambda q: q.matmul(PS[ps], KTA[:, c * 128:(c + 1) * 128], QB[:, ts], start=True, stop=False),
                                 reads=[bKT, bQ], writes=[bPS[ps]])
                            S.op("pe", lambda q: q.matmul(PS[ps], SEL[0:16, n * 128:(n + 1) * 128], SELT[0:16, ts], start=False, stop=(not near)),
                                 reads=[bK, bSELT], writes=[bPS[ps]])
                            if near:
                                S.op("pe", lambda q: q.matmul(PS[ps], JB, BTP[:, r + 1, :], start=False, stop=True),
                                     reads=[bK, bBTP], writes=[bPS[ps]])
                                S.op("act", lambda q: q.activation(pt, PS[ps], AF.Exp), reads=[bPS[ps]], writes=[bpt])
                            else:
                                S.op("act", lambda q, h=h: q.activation(pt, PS[ps], AF.Exp, bias=RB31[:, h:h + 1]), reads=[bPS[ps], bK], writes=[bpt])

                        def stage2(c, hh=hh, nch=nch):
                            pt = PT[c % NPT]
                            bpt = bPT[c % NPT]
                            S.op("pe", lambda q: q.matmul(PS[2], VP[:, c, hh * 128:(hh + 1) * 128], pt, start=(c == 0), stop=(c == nch - 1)),
                                 reads=[bVP, bpt], writes=[bPS[2]])
                            if c == 0:
                                S.op("dve", lambda q: q.tensor_copy(SUMP, pt), reads=[bpt], writes=[bSUM])
                            else:
                                S.op("dve", lambda q: q.tensor_tensor(SUMP, SUMP, pt, ALU.add), reads=[bpt, bSUM], writes=[bSUM])

                        LOOK = 3
                        def tail_b():
                            S.op("pe", lambda q: q.matmul(PS[3], ONEB, SH, start=True, stop=False), reads=[bK, bREC], writes=[bPS[3]])
                            S.op("pe", lambda q: q.matmul(PS[3], ONEB, SL, start=False, stop=True), reads=[bK, bREC], writes=[bPS[3]])
                            S.op("dve", lambda q: q.reciprocal(REC, PS[3]), reads=[bPS[3]], writes=[bREC])
                            S.op("dve", lambda q: q.tensor_tensor(OT, PS[2], REC, ALU.mult), reads=[bPS[2], bREC], writes=[bOT])

                        def tail_d(T=T, sw=sw, h=h):
                            for i in range(KC):
                                po = 4 + (i % 2)
                                S.op("pe", lambda q, i=i, po=po: q.matmul(PS[po], WD[sw][:, i * 128:(i + 1) * 128], OT, start=True, stop=True),
                                     reads=[bWD[sw], bOT], writes=[bPS[po]])
                                add_into_x(i, T, po, h == 0)

                        cb = 1
                        cd = min(10, nch + LOOK - 1)
                        for c in range(nch + LOOK):
                            if c == cb and pending[0] is not None:
                                pending[0]()
                                pending[0] = None
                            if c == cd and pending[1] is not None:
                                pending[1]()
                                pending[1] = None
                            if c >= LOOK:
                                stage2(c - LOOK)
                            if c < nch:
                                stage1(c)
                        S.op("act", lambda q: q.copy(SH, SUMP), reads=[bSUM], writes=[bREC])
                        S.op("dve", lambda q: q.tensor_tensor(SL, SUMP, SH, ALU.subtract), reads=[bSUM, bREC], writes=[bREC])
                        pending[0] = tail_b
                        pending[1] = tail_d
                    pending[0]()
                    pending[1]()
                    wdS.release(wpos)
            barrier()

        def mlstm_mixer(layer, sg):
            barrier()
            A.off = scratch_base
            QT = A.bf16(2 * TOK).rearrange("p (c t) -> p c t", c=2)
            KT = A.bf16(2 * TOK).rearrange("p (c t) -> p c t", c=2)
            KM = A.bf16(16 * 256).rearrange("p (a f) -> p a f", a=16)
            VA = A.bf16(16 * 384).rearrange("p (a f) -> p a f", a=16)
            EBI = A.bf16(TOK)
            CF = A.f32(768).rearrange("p (c f) -> p c f", c=2)
            CB = A.bf16(768).rearrange("p (c f) -> p c f", c=2)
            WFR = A.bf16(KC * 128).rearrange("p (k f) -> p k f", k=KC)
            CGW = A.bf16(64).rearrange("p (k f) -> p k f", k=KC)
            CM = A.f32(80)
            CC = A.f32(384)
            IDB = A.bf16(128)
            O256 = A.bf16(128)
            GT = A.f32(128).rearrange("p (a g) -> p a g", a=16)
            LOGF = A.f32(64).rearrange("p (a g) -> p a g", a=16)
            BT_ = A.f32(64).rearrange("p (a g) -> p a g", a=16)
            EV = A.f32(64).rearrange("p (a g) -> p a g", a=16)
            NBG = A.f32(8)
            DEC = A.f32(32)
            TRIB = A.bf16(128)
            LHI = A.bf16(64)
            LLO = A.bf16(64)
            rbase = A.off
            PRE = A.f32(TOK + 4)
            ACC = A.f32(TOK)
            rend = A.off
            A.off = rbase
            WTS = [A.bf16(128), A.bf16(128)]
            DEN = A.f32(TT)
            HC = A.f32(2 * TT).rearrange("p (c t) -> p c t", c=2)
            HCB = A.bf16(2 * TT).rearrange("p (c t) -> p c t", c=2)
            HCQ = A.bf16(2 * TT).rearrange("p (c t) -> p c t", c=2)
            HN = A.bf16(2 * TT).rearrange("p (c t) -> p c t", c=2)
            SG = A.bf16(2 * TT).rearrange("p (c t) -> p c t", c=2)
            A.off = max(A.off, rend)
            assert A.off <= A.words, A.off
            bQT, bKT, bKM, bVA, bEBI, bCF, bCB, bWFR, bK = Buf(), Buf(), Buf(), Buf(), Buf(), Buf(), Buf(), Buf(), Buf()
            bG, bR1, bR2, bWT0, bDEN, bHC, bHCB, bHCQ, bHN, bSG, bDEC = Buf(), Buf(), Buf(), Buf(), Buf(), Buf(), Buf(), Buf(), Buf(), Buf(), Buf()
            TRI = CC[:, 0:128]
            CAUS = CC[:, 128:256]
            S.dma("pool", CGW, cgw.rearrange("p (k f) -> p k f", k=KC), writes=[bK])
            S.dma("sp", CM, cmisc, writes=[bK])
            S.dma("sp", CC, cconst, writes=[bK])
            S.op("act", lambda q: q.copy(IDB, CC[:, 256:384]), reads=[bK], writes=[bK])
            S.op("dve", lambda q: q.memset(O256, 1.0 / 256.0), writes=[bK])
            S.op("dve", lambda q: q.tensor_scalar_mul(NBG, CM[:, 72:80], -1.0), reads=[bK], writes=[bK])
            for tt in range(16):
                for k in range(KC):
                    S.op("pe", lambda q, tt=tt, k=k: q.matmul(PS[7][:, tt * 8:(tt + 1) * 8], Xb[:, k, tt * 128:(tt + 1) * 128], CGW[:, k, :], start=(k == 0), stop=(k == KC - 1)),
                         reads=[bK] + [bXb[k][tt // 4]], writes=[bPS[7]])
            S.op("dve", lambda q: q.tensor_tensor(GT, PS[7][:, 0:128].rearrange("p (a g) -> p a g", a=16), CM[:, 72:80].unsqueeze(1).to_broadcast([128, 16, 8]), ALU.add),
                 reads=[bPS[7], bK], writes=[bG])
            S.op("act", lambda q: q.activation(LOGF, GT[:, :, 4:8], AF.Exp, scale=-1.0), reads=[bG], writes=[bG])
            S.op("dve", lambda q: q.tensor_scalar_add(LOGF, LOGF, 1.0), reads=[bG], writes=[bG])
            S.op("act", lambda q: q.activation(LOGF, LOGF, AF.Ln), reads=[bG], writes=[bG])
            S.op("act", lambda q: q.copy(TRIB, TRI), reads=[bK], writes=[bK])
            S.op("act", lambda q: q.copy(LHI, LOGF.rearrange("p a g -> p (a g)")), reads=[bG], writes=[bG])
            S.op("dve", lambda q: q.tensor_tensor(LLO, LOGF.rearrange("p a g -> p (a g)"), LHI, ALU.subtract), reads=[bG], writes=[bG])
            S.op("pe", lambda q: q.matmul(PS[7][:, 128:192], TRIB, LHI, start=True, stop=False), reads=[bK, bG], writes=[bPS[7]])
            S.op("pe", lambda q: q.matmul(PS[7][:, 128:192], TRIB, LLO, start=False, stop=True), reads=[bK, bG], writes=[bPS[7]])
            S.op("dve", lambda q: q.tensor_tensor(BT_, GT[:, :, 0:4], PS[7][:, 128:192].rearrange("p (a g) -> p a g", a=16), ALU.add),
                 reads=[bPS[7], bG], writes=[bG])
            S.op("dve", lambda q: q.tensor_scalar_add(BT_, BT_, -math.log(16.0)), reads=[bG], writes=[bG])
            S.op("act", lambda q: q.activation(EV, BT_, AF.Exp), reads=[bG], writes=[bG])

            PW = TT + 4
            PREs = [PRE[:, 0:PW], PRE[:, PW:2 * PW]]
            ACCs = [ACC[:, 0:TT], ACC[:, TT:2 * TT]]
            bPRE = [Buf(), Buf()]
            bACC = [Buf(), Buf()]
            conv_tile = [0]

            def conv_silu(h, isk, cc, slot, OUTT, bOUT):
                ci = (h * 2 + isk) * 2 + cc
                cw = (isk * 8 + h * 2 + cc) * 4
                first = (isk == 0 and cc == 0)
                for t in range(NT):
                    ts = slice(t * TT, (t + 1) * TT)
                    i = conv_tile[0] % 2
                    conv_tile[0] += 1
                    pre, acc, bp, ba = PREs[i], ACCs[i], bPRE[i], bACC[i]
                    prev, bprev = PREs[1 - i], bPRE[1 - i]
                    w_alias = [bR1, bR2] if (first and t < 2) else []
                    r_alias = [bR1, bR2]
                    if t == 0:
                        if sg == 0:
                            S.op("dve", lambda q, pre=pre: q.memset(pre[:, 0:4], 0.0), reads=r_alias, writes=[bp] + w_alias)
                        else:
                            S.dma("sp", pre[:, 0:4], cctx[ci], reads=[db("cc%d" % ci)] + r_alias, writes=[bp] + w_alias)
                    else:
                        S.op("act", lambda q, pre=pre, prev=prev: q.copy(pre[:, 0:4], prev[:, TT:TT + 4]), reads=[bprev] + r_alias, writes=[bp] + w_alias)
                    p = t % 3
                    for k in range(KC):
                        S.op("pe", lambda q, k=k, ts=ts, p=p: q.matmul(PS[p], WGU[slot][:, k, cc * 128:(cc + 1) * 128], Xb[:, k, ts], start=(k == 0), stop=(k == KC - 1)),
                             reads=[bWGU[slot], bXb[k][t]], writes=[bPS[p]])
                    S.op("act", lambda q, p=p, pre=pre: q.copy(pre[:, 4:4 + TT], PS[p]), reads=[bPS[p]] + r_alias, writes=[bp])
                    if t == NT - 1 and sg == 0:
                        S.dma("sp", cctx[ci], pre[:, TT:TT + 4], reads=[bp], writes=[db("cc%d" % ci)])
                    S.op("dve", lambda q, pre=pre, acc=acc: q.tensor_scalar_mul(acc, pre[:, 1:1 + TT], CM[:, cw:cw + 1]), reads=[bp, bK] + r_alias, writes=[ba] + w_alias)
                    for j in range(1, 4):
                        S.op("dve", lambda q, j=j, pre=pre, acc=acc: q.scalar_tensor_tensor(acc, pre[:, 1 + j:1 + j + TT], CM[:, cw + j:cw + j + 1], acc, ALU.mult, ALU.add),
                             reads=[bp, ba, bK], writes=[ba])
                    S.op("act", lambda q, ts=ts, acc=acc: q.activation(OUTT[:, cc, ts], acc, AF.Silu), reads=[ba] + r_alias, writes=[bOUT])

            for h in range(4):
                sq, pq = guS.take(GU_MIX[layer] + 4 * h + 0)
                sk, pk = guS.take(GU_MIX[layer] + 4 * h + 1)
                sv, pv = guS.take(GU_MIX[layer] + 4 * h + 2)
                so, po_ = guS.take(GU_MIX[layer] + 4 * h + 3)
                w0, wp0 = wdS.take(WD_MIX[layer] + 2 * h)
                w1, wp1 = wdS.take(WD_MIX[layer] + 2 * h + 1)
                wsl = [w0, w1]
                S.dma("pool", WFR, cfrep[h].rearrange("p (k f) -> p k f", k=KC), writes=[bWFR])
                for t in range(NT):
                    ts = slice(t * TT, (t + 1) * TT)
                    p = t % 3
                    for k in range(KC):
                        S.op("pe", lambda q, k=k, ts=ts, p=p: q.matmul(PS[p], WFR[:, k, :], Xb[:, k, ts], start=(k == 0), stop=(k == KC - 1)),
                             reads=[bWFR, bXb[k][t]], writes=[bPS[p]])
                    S.op("act", lambda q, ts=ts, p=p, h=h: q.activation(PRE[:, ts], PS[p], AF.Exp, bias=NBG[:, 4 + h:5 + h], scale=-1.0),
                         reads=[bPS[p], bK], writes=[bR1])
                S.op("dve", lambda q: q.tensor_scalar_add(PRE[:, 0:TOK], PRE[:, 0:TOK], 1.0), reads=[bR1], writes=[bR1])
                S.op("act", lambda q: q.activation(PRE[:, 0:TOK], PRE[:, 0:TOK], AF.Ln), reads=[bR1], writes=[bR1])
                cur, bcur, nxt, bnxt = PRE[:, 0:TOK], bR1, ACC, bR2
                for i in range(7):
                    sh = 1 << i
                    c3 = cur.rearrange("p (c l) -> p c l", l=128)
                    n3 = nxt.rearrange("p (c l) -> p c l", l=128)
                    S.op("dve", lambda q, c3=c3, n3=n3, sh=sh: q.tensor_tensor(n3[:, :, sh:128], c3[:, :, sh:128], c3[:, :, 0:128 - sh], ALU.add),
                         reads=[bcur], writes=[bnxt])
                    S.op("act", lambda q, c3=c3, n3=n3, sh=sh: q.copy(n3[:, :, 0:sh], c3[:, :, 0:sh]), reads=[bcur], writes=[bnxt])
                    cur, bcur, nxt, bnxt = nxt, bnxt, cur, bcur
                S.op("act", lambda q, cur=cur: q.activation(EBI, cur, AF.Exp), reads=[bcur], writes=[bEBI])
                S.op("act", lambda q, cur=cur: q.activation(DEC[:, 0:16], cur.rearrange("p (c l) -> p c l", l=128)[:, :, 127], AF.Exp, scale=-1.0),
                     reads=[bcur], writes=[bDEC])
                for cc in range(2):
                    conv_silu(h, 0, cc, sq, QT, bQT)
                guS.release(pq)
                for cc in range(2):
                    conv_silu(h, 1, cc, sk, KT, bKT)
                guS.release(pk)
                for tt in range(16):
                    p = 3 + (tt % 2)
                    for cc in range(2):
                        S.op("pe", lambda q, tt=tt, cc=cc, p=p: q.matmul(PS[p][:, cc * 128:(cc + 1) * 128], KT[:, cc, tt * 128:(tt + 1) * 128], IDB, start=True, stop=True),
                             reads=[bKT, bK], writes=[bPS[p]])
                    S.op("act", lambda q, tt=tt, p=p: q.copy(KM[:, tt, :], PS[p][:, 0:256]), reads=[bPS[p]], writes=[bKM])
                for tt in range(16):
                    p = 5 + (tt % 2)
                    for k in range(KC):
                        S.op("pe", lambda q, tt=tt, k=k, p=p, sv=sv: q.matmul(PS[p][:, 0:256], Xb[:, k, tt * 128:(tt + 1) * 128], WGU[sv][:, k, :], start=(k == 0), stop=(k == KC - 1)),
                             reads=[bWGU[sv], bXb[k][tt // 4]], writes=[bPS[p]])
                    S.op("act", lambda q, tt=tt, p=p, h=h: q.activation(VA[:, tt, 0:256], PS[p][:, 0:256], AF.Identity, scale=EV[:, tt, h:h + 1]),
                         reads=[bPS[p], bG], writes=[bVA])
                    S.op("dve", lambda q, tt=tt, h=h: q.tensor_scalar(VA[:, tt, 256:384], ones, EV[:, tt, h:h + 1], 1024.0, ALU.mult, ALU.mult),
                         reads=[bones, bG], writes=[bVA])
                guS.release(pv)
                if sg == 0:
                    S.op("dve", lambda q: q.memset(CF, 0.0), writes=[bCF])
                else:
                    S.dma("sp", CF, sctx[h].rearrange("p (c f) -> p c f", c=2), reads=[db("sc%d" % h)], writes=[bCF])
                S.op("act", lambda q: q.copy(CB, CF), reads=[bCF], writes=[bCB])
                bWTS = [bWT0, Buf()]

                def st_stage(tt):
                    WT = WTS[tt % 2]
                    for kc in range(2):
                        S.op("pe", lambda q, kc=kc: q.matmul(PS[3][:, 0:128], KT[:, kc, tt * 128:(tt + 1) * 128], QT[:, kc, tt * 128:(tt + 1) * 128], start=(kc == 0), stop=(kc == 1)),
                             reads=[bKT, bQT], writes=[bPS[3]])
                    S.op("dve", lambda q: q.tensor_tensor(WT, PS[3][:, 0:128], CAUS, ALU.mult), reads=[bPS[3], bK], writes=[bWTS[tt % 2]])

                st_stage(0)
                for tt in range(16):
                    T4 = tt // 4
                    col0 = (tt % 4) * 128
                    WT = WTS[tt % 2]
                    bWT = bWTS[tt % 2]
                    for vc in range(3):
                        S.op("pe", lambda q, tt=tt, vc=vc, col0=col0, WT=WT: q.matmul(PS[vc][:, col0:col0 + 128], VA[:, tt, vc * 128:(vc + 1) * 128], WT, start=True, stop=False),
                             reads=[bVA, bWT], writes=[bPS[vc]])
                    if tt + 1 < 16:
                        st_stage(tt + 1)
                    for vc in range(3):
                        for kc in range(2):
                            S.op("pe", lambda q, tt=tt, vc=vc, kc=kc, col0=col0: q.matmul(PS[vc][:, col0:col0 + 128], CB[:, kc, vc * 128:(vc + 1) * 128], QT[:, kc, tt * 128:(tt + 1) * 128], start=False, stop=(kc == 1)),
                                 reads=[bCB, bQT], writes=[bPS[vc]])
                    S.op("dve", lambda q, tt=tt: q.tensor_scalar_mul(CF, CF, DEC[:, tt:tt + 1]), reads=[bCF, bDEC], writes=[bCF])
                    for kc in range(2):
                        S.op("pe", lambda q, tt=tt, kc=kc: q.matmul(PS[4 + kc][:, 0:384], KM[:, tt, kc * 128:(kc + 1) * 128], VA[:, tt, :], start=True, stop=True),
                             reads=[bKM, bVA], writes=[bPS[4 + kc]])
                    for kc in range(2):
                        S.op("dve", lambda q, kc=kc, tt=tt: q.scalar_tensor_tensor(CF[:, kc, :], PS[4 + kc][:, 0:384], DEC[:, tt:tt + 1], CF[:, kc, :], ALU.mult, ALU.add),
                             reads=[bPS[4 + kc], bCF, bDEC], writes=[bCF])
                    S.op("dve", lambda q: q.tensor_copy(CB, CF), reads=[bCF], writes=[bCB])
                    if tt % 4 != 3:
                        continue
                    ts = slice(T4 * TT, (T4 + 1) * TT)
                    S.op("act", lambda q: q.activation(DEN, PS[2], AF.Abs), reads=[bPS[2]], writes=[bDEN])
                    S.op("act", lambda q: q.copy(HC[:, 0, :], PS[0]), reads=[bPS[0]], writes=[bHC])
                    S.op("dve", lambda q: q.tensor_copy(HC[:, 1, :], PS[1]), reads=[bPS[1]], writes=[bHC])
                    S.op("dve", lambda q, ts=ts: q.tensor_tensor(DEN, DEN, EBI[:, ts], ALU.max), reads=[bDEN, bEBI], writes=[bDEN])
                    S.op("dve", lambda q: q.reciprocal(DEN, DEN), reads=[bDEN], writes=[bDEN])
                    for cc in range(2):
                        p = 6 + cc
                        for k in range(KC):
                            S.op("pe", lambda q, k=k, ts=ts, p=p, cc=cc, so=so: q.matmul(PS[p], WGU[so][:, k, cc * 128:(cc + 1) * 128], Xb[:, k, ts], start=(k == 0), stop=(k == KC - 1)),
                                 reads=[bWGU[so], bXb[k][T4]], writes=[bPS[p]])
                        S.op("act", lambda q, cc=cc, p=p: q.activation(SG[:, cc, :], PS[p], AF.Sigmoid), reads=[bPS[p]], writes=[bSG])
                    for vc in range(2):
                        S.op("dve", lambda q, vc=vc: q.tensor_tensor(HC[:, vc, :], HC[:, vc, :], DEN, ALU.mult), reads=[bHC, bDEN], writes=[bHC])
                        S.op("dve", lambda q, vc=vc: q.tensor_tensor(HC[:, vc, :], HC[:, vc, :], SG[:, vc, :], ALU.mult), reads=[bHC, bSG], writes=[bHC])
                    S.op("act", lambda q: q.copy(HCB, HC), reads=[bHC], writes=[bHCB])
                    S.op("act", lambda q: q.activation(HCQ, HC, AF.Square), reads=[bHC], writes=[bHCQ])
                    for vc in range(2):
                        S.op("pe", lambda q, vc=vc: q.matmul(PS[6], O256, HCB[:, vc, :], start=(vc == 0), stop=(vc == 1)), reads=[bK, bHCB], writes=[bPS[6]])
                    for vc in range(2):
                        S.op("pe", lambda q, vc=vc: q.matmul(PS[7], O256, HCQ[:, vc, :], start=(vc == 0), stop=(vc == 1)), reads=[bK, bHCQ], writes=[bPS[7]])
                    S.op("act", lambda q: q.copy(MEAN, PS[6]), reads=[bPS[6]], writes=[bST])
                    S.op("dve", lambda q: q.tensor_tensor(NMR, MEAN, MEAN, ALU.mult), reads=[bST], writes=[bST])
                    S.op("dve", lambda q: q.tensor_tensor(RSTD, PS[7], NMR, ALU.subtract), reads=[bPS[7], bST], writes=[bST])
                    S.op("dve", lambda q: q.tensor_scalar_add(RSTD, RSTD, LN_EPS), reads=[bST], writes=[bST])
                    S.op("act", lambda q: q.sqrt(RSTD, RSTD), reads=[bST], writes=[bST])
                    S.op("dve", lambda q: q.reciprocal(RSTD, RSTD), reads=[bST], writes=[bST])
                    S.op("dve", lambda q: q.scalar_tensor_tensor(NMR, MEAN, -1.0, RSTD, ALU.mult, ALU.mult), reads=[bST], writes=[bST])
                    for vc in range(2):
                        S.op("dve", lambda q, vc=vc: q.tensor_tensor(HC[:, vc, :], HC[:, vc, :], RSTD, ALU.mult), reads=[bHC, bST], writes=[bHC])
                        S.op("dve", lambda q, vc=vc: q.tensor_tensor(HC[:, vc, :], HC[:, vc, :], NMR, ALU.add), reads=[bHC, bST], writes=[bHC])
                        S.op("act", lambda q, vc=vc, h=h: q.activation(HN[:, vc, :], HC[:, vc, :], AF.Identity, scale=CM[:, 64 + 2 * h + vc:65 + 2 * h + vc]),
                             reads=[bHC, bK], writes=[bHN])
                    for i in range(KC):
                        po = 4 + (i % 2)
                        for vc in range(2):
                            S.op("pe", lambda q, i=i, vc=vc, po=po, wsl=wsl: q.matmul(PS[po], WD[wsl[vc]][:, i * 128:(i + 1) * 128], HN[:, vc, :], start=(vc == 0), stop=(vc == 1)),
                                 reads=[bWD[wsl[vc]], bHN], writes=[bPS[po]])
                        add_into_x(i, T4, po, h == 0)
                guS.release(po_)
                wdS.release(wp0)
                wdS.release(wp1)
                if sg == 0:
                    S.dma("sp", sctx[h].rearrange("p (c f) -> p c f", c=2), CF, reads=[bCF], writes=[db("sc%d" % h)])
            barrier()

        for sg in range(NSEG):
            for k in range(KC):
                S.dma("sp", X[:, k, :], xT[sg][:, k * TOK:(k + 1) * TOK], writes=bX[k])
            for k in range(KC):
                for t in range(NT):
                    S.op("act", lambda q, k=k, t=t: q.copy(Xb[:, k, t * TT:(t + 1) * TT], X[:, k, t * TT:(t + 1) * TT]),
                         reads=[bX[k][t]], writes=[bXb[k][t]])
            for (kind, layer, f) in phases:
                if kind == "ffn":
                    ffn(layer, f)
                else:
                    [moba_mixer, pool_mixer, mlstm_mixer][layer % 3](layer, sg)
                    layer_norm(layer, 1)
            for k in range(KC):
                S.dma("sp", yT[sg][:, k * TOK:(k + 1) * TOK], X[:, k, :], reads=bX[k], is_output=True)
        S.finish()
        with nc.Block() as block:
            S.emit(block)
    return nc


def _gu_layout(w):
    n = w.shape[1] // 256
    return np.ascontiguousarray(w.reshape(KC, 128, n, 256).transpose(2, 1, 0, 3)).reshape(n, 128, KC * 256)


def prep_inputs(inputs, phases):
    f32 = lambda a: np.asarray(a, dtype=np.float32)
    x = f32(inputs["x"])
    xs = []
    for b in range(BATCH):
        a = x[b].reshape(NSEG, TOK, KC, 128).transpose(0, 3, 2, 1)
        xs.append(np.ascontiguousarray(a).reshape(NSEG, 128, KC * TOK))
    wgu_in = f32(inputs["ffn_w_gu"])
    g = wgu_in[..., :DFF].reshape(DEPTH, 2, KC, 128, NJ, 128)
    u = wgu_in[..., DFF:].reshape(DEPTH, 2, KC, 128, NJ, 128)
    gu = np.stack([g, u], axis=5)
    gu = np.ascontiguousarray(gu.transpose(0, 1, 4, 3, 2, 5, 6)).reshape(DEPTH * 2 * NJ, 128, KC * 256)
    pieces = [gu]
    a_in, b_in, c_in = f32(inputs["a_w_in"]), f32(inputs["b_w_in"]), f32(inputs["c_w_in"])
    for layer in range(DEPTH):
        mk, j = layer % 3, layer // 3
        if mk == 0:
            w = a_in[j]
            q, k, v = w[:, 0:D], w[:, D:2 * D], w[:, 2 * D:3 * D]
            cols = []
            for hp in range(4):
                cols.append(v[:, hp * 256:(hp + 1) * 256])
                for h in (2 * hp, 2 * hp + 1):
                    cols.append(np.concatenate([q[:, h * 128:(h + 1) * 128], k[:, h * 128:(h + 1) * 128]], axis=1))
            pieces.append(_gu_layout(np.concatenate(cols, axis=1)))
        elif mk == 1:
            pieces.append(_gu_layout(b_in[j]))
        else:
            w = c_in[j]
            cols = []
            for h in range(4):
                for part in range(4):
                    cols.append(w[:, part * D + h * 256: part * D + (h + 1) * 256])
            pieces.append(_gu_layout(np.concatenate(cols, axis=1)))
    wgu = np.concatenate(pieces, axis=0)
    assert wgu.shape[0] == NP_GU, wgu.shape
    wds = [f32(inputs["ffn_w_down"]).reshape(DEPTH * 2 * NJ, 128, D)]
    for layer in range(DEPTH):
        mk, j = layer % 3, layer // 3
        wo = [inputs["a_w_out"], inputs["b_w_out"], inputs["c_w_out"]][mk]
        wds.append(f32(wo)[j].reshape(8, 128, D))
    wd = np.ascontiguousarray(np.concatenate(wds, axis=0))
    lg = f32(inputs["ln_g"]).reshape(DEPTH, 3, KC, 128)
    lb = f32(inputs["ln_b"]).reshape(DEPTH, 3, KC, 128)
    lnp = np.stack([lg, lb], axis=2)
    lnp = np.ascontiguousarray(lnp.transpose(4, 0, 1, 2, 3)).reshape(128, DEPTH * 3 * 2 * KC)
    wg = f32(inputs["b_w_group"])[0].reshape(4, 2, 128, 256).transpose(2, 0, 1, 3)
    pwg = np.ascontiguousarray(wg).reshape(128, 2048)
    invc = np.zeros((4, HALO), np.float32)
    for m in range(4):
        invc[m] = 1.0 / np.minimum(np.arange(HALO) + 1.0, float(2 ** (m + 1)))
    pmisc = np.zeros((128, 72), np.float32)
    pmisc[:, :64] = invc.reshape(1, 64)
    pmisc[:, 64:72] = f32(inputs["b_scale"])[0].reshape(8, 128).T
    cw_in = c_in[0]
    gw = cw_in[:, 4 * D:4 * D + 8]
    cgw = np.ascontiguousarray(gw.reshape(KC, 128, 8).transpose(1, 0, 2)).reshape(128, 64)
    cfrep = np.empty((4, 128, KC, 128), np.float32)
    for h in range(4):
        cfrep[h] = np.broadcast_to(gw[:, 4 + h].reshape(KC, 128).T[:, :, None], (128, KC, 128))
    cfrep = cfrep.reshape(4, 128, KC * 128)
    cmisc = np.zeros((128, 80), np.float32)
    cmisc[:, 0:64] = f32(inputs["c_conv_w"])[0].reshape(4, 16, 128).transpose(2, 1, 0).reshape(128, 64)
    cmisc[:, 64:72] = f32(inputs["c_norm_g"])[0].reshape(8, 128).T
    cmisc[:, 72:80] = np.broadcast_to(f32(inputs["c_b_gates"])[0].reshape(1, 8), (128, 8))
    ii = np.arange(128)
    tri = (ii[:, None] <= ii[None, :]).astype(np.float32)
    cconst = np.concatenate([tri, tri, np.eye(128, dtype=np.float32)], axis=1)
    rb = f32(inputs["rel_bias"])
    arbx = np.concatenate([rb, np.full((1, 8), -30000.0, np.float32)], axis=0)
    delta = np.arange(NDELTA) - 511
    nn = np.maximum(delta, 0)
    nf = np.maximum(nn, 1).astype(np.float32)
    large = 16 + (np.log(nf / np.float32(16)) / np.float32(math.log(128 / 16)) * np.float32(16)).astype(np.int32)
    large = np.minimum(large, 31)
    bucket = np.where(nn < 16, nn, large)
    aoh = np.zeros((33, NDELTA), np.float32)
    for i in range(NDELTA):
        if delta[i] >= 0:
            aoh[bucket[i], i] = 1.0
        else:
            aoh[32, i] = 1.0
    asel = np.zeros((16, 16, 128), np.float32)
    for n in range(16):
        asel[n, n, :] = 1.0
    asel = asel.reshape(16, 2048)
    gbi = np.arange(16)
    pm = np.where(gbi[None, :] < gbi[:, None], 0.0, -1e30).astype(np.float32)
    no = (1.0 - np.eye(16)).astype(np.float32)
    amask = np.ascontiguousarray(np.broadcast_to(np.concatenate([pm.reshape(1, 256), no.reshape(1, 256)], axis=1), (128, 512)))
    aconst = np.concatenate([np.eye(128, dtype=np.float32)[::-1], np.eye(128, dtype=np.float32)], axis=1)
    aconst = np.ascontiguousarray(aconst)
    arb31 = np.ascontiguousarray(np.broadcast_to(rb[31].reshape(1, 8), (128, 8)))
    gu_ids, wd_ids = used_pieces(phases)
    if len(gu_ids) != NP_GU:
        wgu = np.ascontiguousarray(wgu[gu_ids])
    if len(wd_ids) != NP_WD:
        wd = np.ascontiguousarray(wd[wd_ids])
    common = {"wgu": wgu, "wd": wd, "lnp": lnp, "pwg": pwg, "pmisc": pmisc,
              "asel": asel, "amask": amask, "aconst": aconst, "arb31": arb31, "arbx": arbx, "aoh": aoh,
              "cgw": cgw, "cfrep": cfrep, "cmisc": cmisc, "cconst": cconst}
    return [dict(common, xT=xs[c]) for c in range(NCORES)]


def run(inputs, phases=None):
    phases = ALL_PHASES if phases is None else phases
    nc = build_program(phases)
    in_maps = prep_inputs(inputs, phases)
    res = run_bass_kernel_spmd(nc, in_maps, core_ids=list(range(NCORES)))
    out = np.empty((BATCH, SEQ, D), dtype=np.float32)
    for b in range(NCORES):
        yT = np.asarray(res.results[b]["yT"]).reshape(NSEG, 128, KC, TOK)
        out[b] = yT.transpose(0, 3, 2, 1).reshape(SEQ, D)
    return out


def kernel(**inputs):
    return run(inputs)
```

```python
import math
from contextlib import ExitStack

import numpy as np
import concourse.bass as bass
import concourse.mybir as mybir
from concourse.bass_utils import run_bass_kernel_spmd

F32 = mybir.dt.float32
BF16 = mybir.dt.bfloat16
AF = mybir.ActivationFunctionType
ALU = mybir.AluOpType
AX = mybir.AxisListType

D = 1024
KC = 8
DFF = 2816
NJ = 22
DEPTH = 4
SEQ = 4096
BATCH = 4
TOK = 2048
TT = 512
NT = TOK // TT
ALPHA = (2 * DEPTH) ** 0.25
LN_EPS = 1e-5
JGROUPS = [8, 7, 7]
TOUT = 3
SEM_LIMIT = 30000


class Buf:
    __slots__ = ("w", "r", "name", "excl")

    def __init__(self, name="", excl=False):
        self.w = []
        self.r = []
        self.name = name
        self.excl = excl


class Eng:
    def __init__(self, name, same_sync):
        self.name = name
        self.prog = []
        self.sem = None
        self.count = 0
        self.waited = {}
        self.same_sync = same_sync
        self.ring = []
        self.ring_pos = 0


class Sched:
    def __init__(self, nc, stack):
        self.nc = nc
        self.stack = stack
        self.nsem = 0
        self.E = {
            "pe": Eng("pe", False),
            "act": Eng("act", True),
            "dve": Eng("dve", True),
            "pool": Eng("pool", True),
            "sp": Eng("sp", False),
        }
        for e in self.E.values():
            e.sem = self.new_sem(e.name)
        self.out_deps = []

    def new_sem(self, name):
        self.nsem += 1
        return self.stack.enter_context(self.nc.semaphore("s%d_%s" % (self.nsem, name)))

    def _ensure(self, e, dep):
        sem, val = dep
        if sem is e.sem and not e.same_sync:
            return
        key = id(sem)
        if e.waited.get(key, (None, 0))[1] >= val:
            return
        e.waited[key] = (sem, val)
        e.prog.append(("wait", sem, val))

    def _deps(self, e, reads, writes):
        for b in reads:
            for d in b.w:
                self._ensure(e, d)
            if b.excl:
                for d in b.r:
                    if d[0] is not e.sem:
                        self._ensure(e, d)
        for b in writes:
            for d in b.w:
                self._ensure(e, d)
            for d in b.r:
                self._ensure(e, d)

    def _commit(self, dep, reads, writes):
        for b in reads:
            for i, d in enumerate(b.r):
                if d[0] is dep[0]:
                    if d[1] < dep[1]:
                        b.r[i] = dep
                    break
            else:
                b.r.append(dep)
        for b in writes:
            b.w = [dep]
            b.r = []

    def op(self, eng, fn, reads=(), writes=()):
        e = self.E[eng]
        self._deps(e, reads, writes)
        if e.count >= SEM_LIMIT:
            e.sem = self.new_sem(e.name)
            e.count = 0
        e.count += 1
        dep = (e.sem, e.count)
        e.prog.append(("op", fn, e.sem, 1))
        self._commit(dep, reads, writes)
        return dep

    def dma(self, eng, out, in_, reads=(), writes=(), is_output=False, nring=8):
        e = self.E[eng]
        self._deps(e, reads, writes)
        if not e.ring:
            e.ring = [[self.new_sem(e.name + "d"), 0] for _ in range(nring)]
        slot = e.ring[e.ring_pos % len(e.ring)]
        e.ring_pos += 1
        if slot[1] > 0:
            self._ensure(e, (slot[0], slot[1]))
        slot[1] += 16
        dep = (slot[0], slot[1])
        e.prog.append(("op", lambda q, o=out, i=in_: q.dma_start(out=o, in_=i), slot[0], 16))
        self._commit(dep, reads, writes)
        if is_output:
            self.out_deps.append(dep)
        return dep

    def barrier(self, bufs):
        pass

    def finish(self):
        e = self.E["sp"]
        for d in self.out_deps:
            self._ensure(e, d)

    def emit(self, block):
        def run(prog):
            def f(q):
                for it in prog:
                    if it[0] == "wait":
                        q.wait_ge(it[1], it[2])
                    else:
                        it[1](q).then_inc(it[2], it[3])
            return f
        block.tensor(run(self.E["pe"].prog))
        block.scalar(run(self.E["act"].prog))
        block.vector(run(self.E["dve"].prog))
        block.gpsimd(run(self.E["pool"].prog))
        block.sync(run(self.E["sp"].prog))


class Arena:
    def __init__(self, nc, words):
        self.t = nc.alloc_sbuf_tensor("arena", [128, words], F32)
        self.words = words
        self.off = 0

    def f32(self, n):
        a = self.t[:, self.off:self.off + n]
        self.off += n
        assert self.off <= self.words, "SBUF arena overflow %d" % self.off
        return a

    def bf16(self, n):
        assert n % 2 == 0
        return self.f32(n // 2).bitcast(BF16)


NCORES = 4
NSEG = 2
HALO = 16
NDELTA = 1152


class Stream:
    def __init__(self, S, eng, dram, slots, bufs, order, view, idmap):
        self.S, self.eng, self.dram, self.slots, self.bufs = S, eng, dram, slots, bufs
        self.idmap = idmap
        self.order, self.view = order, view
        self.N = len(slots)
        self.issued = 0
        self.done = 0
        self.cursor = 0

    def prefetch(self):
        while self.issued < len(self.order) and self.issued < self.done + self.N:
            i = self.issued
            s = i % self.N
            self.S.dma(self.eng, self.slots[s], self.view(self.dram[self.idmap[self.order[i]]]), writes=[self.bufs[s]])
            self.issued += 1

    def take(self, piece):
        pos = self.cursor
        assert self.order[pos] == piece, (pos, self.order[pos], piece)
        self.cursor += 1
        self.prefetch()
        assert pos < self.issued
        return pos % self.N, pos

    def release(self, pos):
        assert pos == self.done, (pos, self.done)
        self.done = pos + 1
        self.prefetch()


def piece_tables():
    n = DEPTH * 2 * NJ
    gu_mix = {}
    for layer in range(DEPTH):
        gu_mix[layer] = n
        n += {0: 12, 1: 4, 2: 16}[layer % 3]
    m = DEPTH * 2 * NJ
    wd_mix = {}
    for layer in range(DEPTH):
        wd_mix[layer] = m
        m += 8
    return gu_mix, n, wd_mix, m


GU_MIX, NP_GU, WD_MIX, NP_WD = piece_tables()


def phase_pieces(ph):
    kind, layer, f = ph
    if kind == "ffn":
        ids = [(layer * 2 + f) * NJ + j for j in range(NJ)]
        return ids, list(ids)
    mk = layer % 3
    ngu = {0: 12, 1: 4, 2: 16}[mk]
    return [GU_MIX[layer] + i for i in range(ngu)], [WD_MIX[layer] + i for i in range(8)]


ALL_PHASES = []
for _l in range(DEPTH):
    ALL_PHASES += [("ffn", _l, 0), ("mix", _l, 0), ("ffn", _l, 1)]


def used_pieces(phases):
    gu, wdp = set(), set()
    for ph in phases:
        a, b = phase_pieces(ph)
        gu.update(a)
        wdp.update(b)
    return sorted(gu), sorted(wdp)


def build_program(phases):
    nc = bass.Bass("TRN2", target_bir_lowering=False)
    gu_ids, wd_ids = used_pieces(phases)
    gu_map = {p: i for i, p in enumerate(gu_ids)}
    wd_map = {p: i for i, p in enumerate(wd_ids)}
    xT = nc.dram_tensor("xT", [NSEG, 128, KC * TOK], F32, kind="ExternalInput").ap()
    wgu = nc.dram_tensor("wgu", [len(gu_ids), 128, KC * 256], F32, kind="ExternalInput").ap()
    wd = nc.dram_tensor("wd", [len(wd_ids), 128, D], F32, kind="ExternalInput").ap()
    lnp = nc.dram_tensor("lnp", [128, DEPTH * 3 * 2 * KC], F32, kind="ExternalInput").ap()
    pwg = nc.dram_tensor("pwg", [128, 2048], F32, kind="ExternalInput").ap()
    pmisc = nc.dram_tensor("pmisc", [128, 72], F32, kind="ExternalInput").ap()
    yT = nc.dram_tensor("yT", [NSEG, 128, KC * TOK], F32, kind="ExternalOutput").ap()
    uctx = nc.dram_tensor("uctx", [KC, 128, HALO], F32).ap()
    cgw = nc.dram_tensor("cgw", [128, 64], F32, kind="ExternalInput").ap()
    cfrep = nc.dram_tensor("cfrep", [4, 128, KC * 128], F32, kind="ExternalInput").ap()
    cmisc = nc.dram_tensor("cmisc", [128, 80], F32, kind="ExternalInput").ap()
    cconst = nc.dram_tensor("cconst", [128, 384], F32, kind="ExternalInput").ap()
    cctx = nc.dram_tensor("cctx", [16, 128, 4], F32).ap()
    asel = nc.dram_tensor("asel", [16, 16 * 128], F32, kind="ExternalInput").ap()
    amask = nc.dram_tensor("amask", [128, 512], F32, kind="ExternalInput").ap()
    aconst = nc.dram_tensor("aconst", [128, 256], F32, kind="ExternalInput").ap()
    arb31 = nc.dram_tensor("arb31", [128, 8], F32, kind="ExternalInput").ap()
    arbx = nc.dram_tensor("arbx", [33, 8], F32, kind="ExternalInput").ap()
    aoh = nc.dram_tensor("aoh", [33, NDELTA], F32, kind="ExternalInput").ap()
    tvd = nc.dram_tensor("tvd", [8, NDELTA], BF16).ap()
    ktctx_all = [nc.dram_tensor("ktctx%d" % j, [8, 128, TOK], BF16).ap() for j in range(2)]
    vctx_all = [nc.dram_tensor("vctx%d" % j, [4, 128, 16 * 256], BF16).ap() for j in range(2)]
    kmctx_all = [nc.dram_tensor("kmctx%d" % j, [128, 64], F32).ap() for j in range(2)]
    tv_state = {}
    DB = {}

    def db(name):
        if name not in DB:
            DB[name] = Buf(name)
        return DB[name]
    sctx = nc.dram_tensor("sctx", [4, 128, 768], F32).ap()

    gu_order, wd_order = [], []
    for sg in range(NSEG):
        for ph in phases:
            a, b = phase_pieces(ph)
            gu_order += a
            wd_order += b

    stack = ExitStack()
    with stack:
        S = Sched(nc, stack)
        A = Arena(nc, 53000)
        X = A.f32(KC * TOK).rearrange("p (k t) -> p k t", k=KC)
        Xb = A.bf16(KC * TOK).rearrange("p (k t) -> p k t", k=KC)
        bX = [[Buf("X%d_%d" % (k, t)) for t in range(NT)] for k in range(KC)]
        bXb = [[Buf("Xb%d_%d" % (k, t)) for t in range(NT)] for k in range(KC)]
        LNP = A.f32(DEPTH * 3 * 2 * KC)
        bLNP = Buf("lnp")
        ones = A.bf16(128)
        bones = Buf("ones")
        NGU = 4
        WGU = [A.bf16(KC * 256).rearrange("p (k f) -> p k f", k=KC) for _ in range(NGU)]
        bWGU = [Buf("wgu%d" % i) for i in range(NGU)]
        NWD = 8
        WD = [A.bf16(D) for _ in range(NWD)]
        bWD = [Buf("wd%d" % i) for i in range(NWD)]
        MEAN = A.f32(TT)
        RSTD = A.f32(TT)
        NMR = A.f32(TT)
        bST = Buf("stats")
        YT = [A.f32(TT) for _ in range(2)]
        bYT = [Buf("yt%d" % i) for i in range(2)]
        scratch_base = A.off
        ZB = A.bf16(KC * TT).rearrange("p (k t) -> p k t", k=KC)
        bZB = Buf("zb")
        ZQ = A.bf16(KC * TT).rearrange("p (k t) -> p k t", k=KC)
        bZQ = Buf("zq")
        HG = A.bf16(8 * TOK).rearrange("p (j t) -> p j t", j=8)
        bHG = [[Buf("h%d_%d" % (j, t)) for t in range(NT)] for j in range(8)]
        SIL = [A.f32(TT) for _ in range(2)]
        bSIL = [Buf("sil%d" % i) for i in range(2)]
        PS = [nc.alloc_psum_tensor("ps%d" % i, [128, 512], F32)[:, :] for i in range(8)]
        bPS = [Buf("ps%d" % i, excl=True) for i in range(8)]

        guS = Stream(S, "pool", wgu, WGU, bWGU, gu_order, lambda a: a.rearrange("p (k f) -> p k f", k=KC), gu_map)
        wdS = Stream(S, "pool", wd, WD, bWD, wd_order, lambda a: a, wd_map)

        def barrier():
            deps = []
            for e in S.E.values():
                if e.count > 0:
                    deps.append((e.sem, e.count))
                for slot in e.ring:
                    if slot[1] > 0:
                        deps.append((slot[0], slot[1]))
            for e in S.E.values():
                for d in deps:
                    S._ensure(e, d)

        S.dma("sp", LNP, lnp, writes=[bLNP])
        S.op("dve", lambda q: q.memset(ones, 1.0 / 1024.0), writes=[bones])

        def ln_col(layer, which, gb, k):
            c = ((layer * 3 + which) * 2 + gb) * KC + k
            return LNP[:, c:c + 1]

        def layer_norm(layer, which):
            for t in range(NT):
                ln_tile(layer, which, t)

        def ln_tile(layer, which, t):
            if True:
                ts = slice(t * TT, (t + 1) * TT)
                allx = [bX[k][t] for k in range(KC)]
                S.op("act", lambda q, ts=ts: q.copy(ZB, X[:, :, ts]), reads=allx, writes=[bZB])
                S.op("act", lambda q, ts=ts: q.activation(ZQ, X[:, :, ts], AF.Square), reads=allx, writes=[bZQ])
                for k in range(KC):
                    S.op("pe", lambda q, k=k: q.matmul(PS[6], ones, ZB[:, k, :], start=(k == 0), stop=(k == KC - 1)),
                         reads=[bones, bZB], writes=[bPS[6]])
                for k in range(KC):
                    S.op("pe", lambda q, k=k: q.matmul(PS[7], ones, ZQ[:, k, :], start=(k == 0), stop=(k == KC - 1)),
                         reads=[bones, bZQ], writes=[bPS[7]])
                S.op("act", lambda q: q.copy(MEAN, PS[6]), reads=[bPS[6]], writes=[bST])
                S.op("dve", lambda q: q.tensor_tensor(NMR, MEAN, MEAN, ALU.mult), reads=[bST], writes=[bST])
                S.op("dve", lambda q: q.tensor_tensor(RSTD, PS[7], NMR, ALU.subtract), reads=[bPS[7], bST], writes=[bST])
                S.op("dve", lambda q: q.tensor_scalar_add(RSTD, RSTD, LN_EPS), reads=[bST], writes=[bST])
                S.op("act", lambda q: q.sqrt(RSTD, RSTD), reads=[bST], writes=[bST])
                S.op("dve", lambda q: q.reciprocal(RSTD, RSTD), reads=[bST], writes=[bST])
                S.op("dve", lambda q: q.scalar_tensor_tensor(NMR, MEAN, -1.0, RSTD, ALU.mult, ALU.mult), reads=[bST], writes=[bST])
                for k in range(KC):
                    y = YT[k % 2]
                    by = bYT[k % 2]
                    S.op("dve", lambda q, k=k, ts=ts, y=y: q.tensor_tensor(y, X[:, k, ts], RSTD, ALU.mult),
                         reads=[bX[k][t], bST], writes=[by])
                    S.op("dve", lambda q, y=y: q.tensor_tensor(y, y, NMR, ALU.add), reads=[by, bST], writes=[by])
                    g = ln_col(layer, which, 0, k)
                    b = ln_col(layer, which, 1, k)
                    S.op("act", lambda q, k=k, ts=ts, y=y, g=g, b=b: q.activation(X[:, k, ts], y, AF.Identity, bias=b, scale=g),
                         reads=[by, bLNP], writes=[bX[k][t]])
                    S.op("act", lambda q, k=k, ts=ts, y=y, g=g, b=b: q.activation(Xb[:, k, ts], y, AF.Identity, bias=b, scale=g),
                         reads=[by, bLNP], writes=[bXb[k][t]])

        def add_into_x(i, t, ps_idx, first):
            ts = slice(t * TT, (t + 1) * TT)
            if first:
                S.op("dve", lambda q: q.scalar_tensor_tensor(X[:, i, ts], X[:, i, ts], ALPHA, PS[ps_idx], ALU.mult, ALU.add),
                     reads=[bPS[ps_idx], bX[i][t]], writes=[bX[i][t]])
            else:
                S.op("dve", lambda q: q.tensor_tensor(X[:, i, ts], X[:, i, ts], PS[ps_idx], ALU.add),
                     reads=[bPS[ps_idx], bX[i][t]], writes=[bX[i][t]])

        gu_cnt = [0]

        def gu_tile(s, jl, t):
            ts = slice(t * TT, (t + 1) * TT)
            par = gu_cnt[0] % 2
            gu_cnt[0] += 1
            pg = 2 * par
            pu = pg + 1
            for k in range(KC):
                S.op("pe", lambda q, k=k: q.matmul(PS[pg], WGU[s][:, k, 0:128], Xb[:, k, ts], start=(k == 0), stop=(k == KC - 1)),
                     reads=[bWGU[s], bXb[k][t]], writes=[bPS[pg]])
            for k in range(KC):
                S.op("pe", lambda q, k=k: q.matmul(PS[pu], WGU[s][:, k, 128:256], Xb[:, k, ts], start=(k == 0), stop=(k == KC - 1)),
                     reads=[bWGU[s], bXb[k][t]], writes=[bPS[pu]])
            sl = SIL[par]
            bsl = bSIL[par]
            S.op("act", lambda q: q.activation(sl, PS[pg], AF.Silu), reads=[bPS[pg]], writes=[bsl])
            S.op("dve", lambda q: q.scalar_tensor_tensor(HG[:, jl, ts], PS[pu], 0.5, sl, ALU.mult, ALU.mult),
                 reads=[bPS[pu], bsl], writes=[bHG[jl][t]])

        def ffn(layer, f):
            fi = layer * 2 + f
            which = 0 if f == 0 else 2
            jbase = 0
            ngrp = len(JGROUPS)
            for gi, gsz in enumerate(JGROUPS):
                jstart = 0
                if gi == 0:
                    held = [guS.take(fi * NJ + jbase + jl) for jl in range(TOUT)]
                    for t in range(NT):
                        for jl in range(TOUT):
                            gu_tile(held[jl][0], jl, t)
                    for (_, gpos) in held:
                        guS.release(gpos)
                    jstart = TOUT
                if True:
                    for jl in range(jstart, gsz):
                        s, gpos = guS.take(fi * NJ + jbase + jl)
                        for t in range(NT):
                            gu_tile(s, jl, t)
                        guS.release(gpos)
                slots = [wdS.take(fi * NJ + jbase + jl) for jl in range(gsz)]
                for t in range(NT):
                    ts = slice(t * TT, (t + 1) * TT)
                    for i in range(KC):
                        po = 4 + (i % 2)
                        for jl in range(gsz):
                            sw = slots[jl][0]
                            S.op("pe", lambda q, i=i, jl=jl, ts=ts, sw=sw, po=po, gsz=gsz: q.matmul(PS[po], WD[sw][:, i * 128:(i + 1) * 128], HG[:, jl, ts], start=(jl == 0), stop=(jl == gsz - 1)),
                                 reads=[bWD[sw], bHG[jl][t]], writes=[bPS[po]])
                        add_into_x(i, t, po, gi == 0)
                    if gi == ngrp - 1:
                        ln_tile(layer, which, t)
                for (_, wpos) in slots:
                    wdS.release(wpos)
                jbase += gsz

        def pool_mixer(layer, sg):
            barrier()
            A.off = scratch_base
            W = HALO + TOK
            UF = A.f32(W)
            T1 = A.f32(W)
            T2 = A.f32(W)
            TMP = A.f32(HALO)
            Pm = A.bf16(2 * TOK).rearrange("p (c t) -> p c t", c=2)
            Y1 = A.bf16(2 * TOK).rearrange("p (c t) -> p c t", c=2)
            WGS = A.bf16(2048)
            WGS4 = WGS.rearrange("p (m i f) -> p m i f", m=4, i=2)
            PMI = A.f32(72)
            bUF, bT1, bT2, bTMP, bWGS, bPMI = Buf(), Buf(), Buf(), Buf(), Buf(), Buf()
            bPm = [[Buf() for _ in range(NT)] for _ in range(2)]
            bY1 = [[Buf() for _ in range(NT)] for _ in range(2)]
            S.dma("pool", WGS, pwg, writes=[bWGS])
            S.dma("sp", PMI, pmisc, writes=[bPMI])
            for m in range(4):
                s, gpos = guS.take(GU_MIX[layer] + m)
                w = 2 ** (m + 1)
                for oc in range(2):
                    c = 2 * m + oc
                    if sg == 0:
                        S.op("dve", lambda q: q.memset(UF[:, 0:HALO], 0.0), writes=[bUF])
                    else:
                        S.dma("sp", UF[:, 0:HALO], uctx[c], reads=[db("u%d" % c)], writes=[bUF])
                    for t in range(NT):
                        ts = slice(t * TT, (t + 1) * TT)
                        p = t % 4
                        for k in range(KC):
                            S.op("pe", lambda q, k=k, ts=ts, s=s, p=p, oc=oc: q.matmul(PS[p], WGU[s][:, k, oc * 128:(oc + 1) * 128], Xb[:, k, ts], start=(k == 0), stop=(k == KC - 1)),
                                 reads=[bWGU[s], bXb[k][t]], writes=[bPS[p]])
                        S.op("act", lambda q, t=t, p=p: q.copy(UF[:, HALO + t * TT:HALO + (t + 1) * TT], PS[p]), reads=[bPS[p]], writes=[bUF])
                    if sg == 0:
                        S.dma("sp", uctx[c], UF[:, TOK:TOK + HALO], reads=[bUF], writes=[db("u%d" % c)])
                    cur, bcur = UF, bUF
                    tmps = [(T1, bT1), (T2, bT2)]
                    for i in range(m + 1):
                        sh = 1 << i
                        lo = (2 << i) - 1
                        nxt, bn = tmps[i % 2]
                        S.op("dve", lambda q, nxt=nxt, cur=cur, lo=lo, sh=sh: q.tensor_tensor(nxt[:, lo:W], cur[:, lo:W], cur[:, lo - sh:W - sh], ALU.add),
                             reads=[bcur], writes=[bn])
                        cur, bcur = nxt, bn
                    S.op("dve", lambda q, cur=cur, oc=oc, w=w: q.scalar_tensor_tensor(Pm[:, oc, :], cur[:, HALO:W], 1.0 / w, UF[:, HALO:W], ALU.mult, ALU.subtract),
                         reads=[bcur, bUF], writes=bPm[oc])
                    if sg == 0:
                        S.op("dve", lambda q, cur=cur, m=m: q.tensor_tensor(TMP, cur[:, HALO:2 * HALO], PMI[:, m * HALO:(m + 1) * HALO], ALU.mult),
                             reads=[bcur, bPMI], writes=[bTMP])
                        S.op("dve", lambda q, oc=oc: q.tensor_tensor(Pm[:, oc, 0:HALO], TMP, UF[:, HALO:2 * HALO], ALU.subtract),
                             reads=[bTMP, bUF], writes=[bPm[oc][0]])
                guS.release(gpos)
                for oc in range(2):
                    c = 2 * m + oc
                    for t in range(NT):
                        ts = slice(t * TT, (t + 1) * TT)
                        p = (2 * oc + t) % 4
                        for ic in range(2):
                            S.op("pe", lambda q, ic=ic, oc=oc, ts=ts, p=p, m=m: q.matmul(PS[p], WGS4[:, m, ic, oc * 128:(oc + 1) * 128], Pm[:, ic, ts], start=(ic == 0), stop=(ic == 1)),
                                 reads=[bWGS, bPm[ic][t]], writes=[bPS[p]])
                        S.op("act", lambda q, oc=oc, ts=ts, p=p, c=c: q.activation(Y1[:, oc, ts], PS[p], AF.Identity, scale=PMI[:, 64 + c:65 + c]),
                             reads=[bPS[p], bPMI], writes=[bY1[oc][t]])
                for oc in range(2):
                    c = 2 * m + oc
                    sw, wpos = wdS.take(WD_MIX[layer] + c)
                    for t in range(NT):
                        ts = slice(t * TT, (t + 1) * TT)
                        for i in range(KC):
                            po = 4 + (i % 2)
                            S.op("pe", lambda q, i=i, oc=oc, ts=ts, sw=sw, po=po: q.matmul(PS[po], WD[sw][:, i * 128:(i + 1) * 128], Y1[:, oc, ts], start=True, stop=True),
                                 reads=[bWD[sw], bY1[oc][t]], writes=[bPS[po]])
                            add_into_x(i, t, po, c == 0)
                    wdS.release(wpos)
            barrier()

        def moba_mixer(layer, sg):
            barrier()
            lj = layer // 3
            ktctx, vctx, kmctx = ktctx_all[lj], vctx_all[lj], kmctx_all[lj]
            A.off = scratch_base
            QH = A.bf16(TOK)
            QL = A.bf16(TOK)
            QB = A.bf16(TOK)
            KMH = A.bf16(16)
            KML = A.bf16(16)
            KTA = A.bf16(2 * TOK)
            VP = A.bf16(32 * 256).rearrange("p (a f) -> p a f", a=32)
            KMA = A.f32(16)
            SELT = A.bf16(TOK)
            BTP = A.bf16(5 * TT).rearrange("p (r t) -> p r t", r=5)
            SEL = A.bf16(16 * 128)
            MK = A.f32(512)
            CJ = A.f32(256)
            JB = A.bf16(128)
            IDB = A.bf16(128)
            ONEB = A.bf16(128)
            RB31 = A.f32(8)
            GMA = A.f32(256)
            G2 = A.f32(256)
            GE = A.f32(256)
            MX = A.f32(16)
            SELBA = A.bf16(256)
            base0 = A.off
            if sg == 0:
                RBX = A.f32(8)
                RBH = A.bf16(8)
                RBL = A.bf16(8)
                OH = A.f32(NDELTA)
                OHB = A.bf16(NDELTA)
                TVS = A.bf16(NDELTA)
                assert A.off <= A.words, A.off
            A.off = base0
            NPT = 4
            PT = [A.bf16(TT) for _ in range(NPT)]
            OT = A.bf16(TT)
            REC = A.f32(TT)
            SUMP = A.f32(TT)
            SH = A.bf16(TT)
            SL = A.bf16(TT)
            assert A.off <= A.words, A.off
            bQ, bKT, bVP, bKM, bSELT, bBTP, bOT, bREC, bK, bGM, bSB = Buf(), Buf(), Buf(), Buf(), Buf(), Buf(), Buf(), Buf(), Buf(), Buf(), Buf()
            bPT = [Buf() for _ in range(NPT)]
            bSUM = Buf()
            PMASK = MK[:, 0:256].rearrange("p (g n) -> p g n", g=16)
            NOTOWN = MK[:, 256:512].rearrange("p (g n) -> p g n", g=16)
            S.dma("pool", SEL[0:16, :], asel, writes=[bK])
            S.dma("sp", MK, amask, writes=[bK])
            S.dma("sp", CJ, aconst, writes=[bK])
            S.dma("sp", RB31, arb31, writes=[bK])
            S.op("act", lambda q: q.copy(JB, CJ[:, 0:128]), reads=[bK], writes=[bK])
            S.op("act", lambda q: q.copy(IDB, CJ[:, 128:256]), reads=[bK], writes=[bK])
            S.op("dve", lambda q: q.memset(ONEB, 1.0), writes=[bK])
            if sg == 0:
                S.dma("sp", RBX[0:33, :], arbx, writes=[bK])
                S.dma("sp", OH[0:33, :], aoh, writes=[bK])
                S.op("act", lambda q: q.copy(RBH[0:33, :], RBX[0:33, :]), reads=[bK], writes=[bK])
                S.op("dve", lambda q: q.tensor_tensor(RBL[0:33, :], RBX[0:33, :], RBH[0:33, :], ALU.subtract), reads=[bK], writes=[bK])
                S.op("act", lambda q: q.copy(OHB[0:33, :], OH[0:33, :]), reads=[bK], writes=[bK])
                for c3 in range(3):
                    S.op("pe", lambda q, c3=c3: q.matmul(PS[c3][0:8, 0:384], RBH[0:33, 0:8], OHB[0:33, c3 * 384:(c3 + 1) * 384], start=True, stop=False),
                         reads=[bK], writes=[bPS[c3]])
                    S.op("pe", lambda q, c3=c3: q.matmul(PS[c3][0:8, 0:384], RBL[0:33, 0:8], OHB[0:33, c3 * 384:(c3 + 1) * 384], start=False, stop=True),
                         reads=[bK], writes=[bPS[c3]])
                    S.op("act", lambda q, c3=c3: q.copy(TVS[0:8, c3 * 384:(c3 + 1) * 384], PS[c3][0:8, 0:384]), reads=[bPS[c3]], writes=[bK])
                S.dma("sp", tvd, TVS[0:8, :], reads=[bK], writes=[db("tvd")])
                barrier()
            for hp in range(4):
                sv, pv = guS.take(GU_MIX[layer] + 3 * hp)
                if sg == 1:
                    S.dma("sp", VP[:, 0:16, :], vctx[hp].rearrange("p (a f) -> p a f", a=16), reads=[db("L%dvc%d" % (lj, hp))], writes=[bVP])
                for tt in range(16):
                    p = 6 + (tt % 2)
                    for k in range(KC):
                        S.op("pe", lambda q, tt=tt, k=k, p=p, sv=sv: q.matmul(PS[p][:, 0:256], Xb[:, k, tt * 128:(tt + 1) * 128], WGU[sv][:, k, :], start=(k == 0), stop=(k == KC - 1)),
                             reads=[bWGU[sv], bXb[k][tt // 4]], writes=[bPS[p]])
                    S.op("act", lambda q, tt=tt, p=p: q.copy(VP[:, sg * 16 + tt, :], PS[p][:, 0:256]), reads=[bPS[p]], writes=[bVP])
                guS.release(pv)
                if sg == 0:
                    S.dma("sp", vctx[hp].rearrange("p (a f) -> p a f", a=16), VP[:, 0:16, :], reads=[bVP], writes=[db("L%dvc%d" % (lj, hp))])
                for hh in range(2):
                    h = 2 * hp + hh
                    sa, pa = guS.take(GU_MIX[layer] + 3 * hp + 1 + hh)
                    sw, wpos = wdS.take(WD_MIX[layer] + h)
                    S.op("dve", lambda q: q.memset(KMA, 0.0), writes=[bKM])
                    if sg == 1:
                        S.dma("sp", KTA[:, 0:TOK], ktctx[h], reads=[db("L%dkt%d" % (lj, h))], writes=[bKT])
                        S.dma("sp", KMA[:, 0:8], kmctx[:, h * 8:(h + 1) * 8], reads=[db("L%dkm%d" % (lj, h))], writes=[bKM])
                    for t in range(NT):
                        ts = slice(t * TT, (t + 1) * TT)
                        for k in range(KC):
                            S.op("pe", lambda q, k=k, ts=ts, sa=sa: q.matmul(PS[6], WGU[sa][:, k, 0:128], Xb[:, k, ts], start=(k == 0), stop=(k == KC - 1)),
                                 reads=[bWGU[sa], bXb[k][t]], writes=[bPS[6]])
                        S.op("act", lambda q, ts=ts: q.copy(QH[:, ts], PS[6]), reads=[bPS[6]], writes=[bQ])
                        S.op("dve", lambda q, ts=ts: q.tensor_tensor(QL[:, ts], PS[6], QH[:, ts], ALU.subtract), reads=[bPS[6], bQ], writes=[bQ])
                        S.op("act", lambda q, ts=ts: q.activation(QB[:, ts], PS[6], AF.Identity, scale=float(128 ** -0.5)), reads=[bPS[6]], writes=[bQ])
                        for k in range(KC):
                            S.op("pe", lambda q, k=k, ts=ts, sa=sa: q.matmul(PS[7], WGU[sa][:, k, 128:256], Xb[:, k, ts], start=(k == 0), stop=(k == KC - 1)),
                                 reads=[bWGU[sa], bXb[k][t]], writes=[bPS[7]])
                        S.op("act", lambda q, t=t: q.copy(KTA[:, sg * TOK + t * TT:sg * TOK + (t + 1) * TT], PS[7]), reads=[bPS[7]], writes=[bKT])
                        S.op("dve", lambda q, t=t: q.tensor_reduce(KMA[:, sg * 8 + 2 * t:sg * 8 + 2 * t + 2], PS[7].rearrange("p (b l) -> p b l", b=2), AX.X, ALU.add),
                             reads=[bPS[7]], writes=[bKM])
                    guS.release(pa)
                    if sg == 0:
                        S.dma("sp", ktctx[h], KTA[:, 0:TOK], reads=[bKT], writes=[db("L%dkt%d" % (lj, h))])
                        S.dma("sp", kmctx[:, h * 8:(h + 1) * 8], KMA[:, 0:8], reads=[bKM], writes=[db("L%dkm%d" % (lj, h))])
                    for r in range(5):
                        src = bass.AP(tvd.tensor, h * NDELTA + 512 - 128 * r, [[1, 128], [1, TT]])
                        S.dma("sp", BTP[:, r, :], src, reads=[db("tvd")], writes=[bBTP])
                    S.op("act", lambda q: q.copy(KMH, KMA), reads=[bKM], writes=[bKM])
                    S.op("dve", lambda q: q.tensor_tensor(KML, KMA, KMH, ALU.subtract), reads=[bKM], writes=[bKM])
                    for qt in range(16):
                        qs = slice(qt * 128, (qt + 1) * 128)
                        go = PS[6][:, qt * 16:(qt + 1) * 16]
                        S.op("pe", lambda q, qs=qs, go=go: q.matmul(go, QH[:, qs], KMH, start=True, stop=False), reads=[bQ, bKM], writes=[bPS[6]])
                        S.op("pe", lambda q, qs=qs, go=go: q.matmul(go, QH[:, qs], KML, start=False, stop=False), reads=[bQ, bKM], writes=[bPS[6]])
                        S.op("pe", lambda q, qs=qs, go=go: q.matmul(go, QL[:, qs], KMH, start=False, stop=True), reads=[bQ, bKM], writes=[bPS[6]])
                    g4 = lambda a: a.rearrange("p (g two n) -> p g two n", g=8, two=2)
                    pm4 = PMASK[:, sg * 8:sg * 8 + 8, :].unsqueeze(2).to_broadcast([128, 8, 2, 16])
                    no4 = NOTOWN[:, sg * 8:sg * 8 + 8, :].unsqueeze(2).to_broadcast([128, 8, 2, 16])
                    g3 = lambda a: a.rearrange("p (t n) -> p t n", t=16)
                    bc = lambda a: a.unsqueeze(2).to_broadcast([128, 16, 16])
                    S.op("dve", lambda q: q.tensor_tensor(g4(GMA), g4(PS[6][:, 0:256]), pm4, ALU.add), reads=[bPS[6], bK], writes=[bGM])
                    S.op("dve", lambda q: q.tensor_reduce(MX, g3(GMA), AX.X, ALU.max), reads=[bGM], writes=[bSB])
                    S.op("dve", lambda q: q.tensor_tensor(g3(GE), g3(GMA), bc(MX), ALU.is_ge), reads=[bGM, bSB], writes=[bSB])
                    S.op("dve", lambda q: q.scalar_tensor_tensor(G2, GE, -1e30, GMA, ALU.mult, ALU.add), reads=[bGM, bSB], writes=[bSB])
                    S.op("dve", lambda q: q.tensor_reduce(MX, g3(G2), AX.X, ALU.max), reads=[bSB], writes=[bSB])
                    S.op("dve", lambda q: q.tensor_tensor(g3(GE), g3(G2), bc(MX), ALU.is_ge), reads=[bSB], writes=[bSB])
                    S.op("dve", lambda q: q.scalar_tensor_tensor(G2, GE, -1e30, G2, ALU.mult, ALU.add), reads=[bSB], writes=[bSB])
                    S.op("dve", lambda q: q.tensor_reduce(MX, g3(G2), AX.X, ALU.max), reads=[bSB], writes=[bSB])
                    S.op("dve", lambda q: q.tensor_scalar_max(MX, MX, -1e29), reads=[bSB], writes=[bSB])
                    S.op("dve", lambda q: q.tensor_tensor(g3(GE), g3(GMA), bc(MX), ALU.is_ge), reads=[bGM, bSB], writes=[bSB])
                    S.op("dve", lambda q: q.tensor_scalar(GE, GE, 30000.0, -30000.0, ALU.mult, ALU.add), reads=[bSB], writes=[bSB])
                    S.op("dve", lambda q: q.tensor_tensor(g4(SELBA), g4(GE), no4, ALU.mult), reads=[bSB, bK], writes=[bSB])
                    for qt in range(16):
                        S.op("pe", lambda q, qt=qt: q.matmul(PS[7][0:16, (qt % 4) * 128:(qt % 4 + 1) * 128], SELBA[:, qt * 16:(qt + 1) * 16], IDB, start=True, stop=True),
                             reads=[bSB, bK], writes=[bPS[7]])
                        if qt % 4 == 3:
                            T = qt // 4
                            S.op("act", lambda q, T=T: q.copy(SELT[0:16, T * TT:(T + 1) * TT], PS[7][0:16, :]), reads=[bPS[7]], writes=[bSELT])
                    pending = [None, None]
                    for T in range(NT):
                        gT = sg * 4 + T
                        ts = slice(T * TT, (T + 1) * TT)
                        nch = 4 * gT + 4
                        psb = [0, 1, 6, 7]

                        def stage1(c, ts=ts, gT=gT, h=h):
                            n = c // 2
                            r = c - 4 * gT
                            near = r >= -1
                            ps = psb[c % NPT]
                            pt = PT[c % NPT]
                            bpt = bPT[c % NPT]
                            S.op("pe", lambda q: q.matmul(PS[ps], KTA[:, c * 128:(c + 1) * 128], QB[:, ts], start=True, stop=False),
                                 reads=[bKT, bQ], writes=[bPS[ps]])
                            if r < 2:
                                S.op("pe", lambda q: q.matmul(PS[ps], SEL[0:16, n * 128:(n + 1) * 128], SELT[0:16, ts], start=False, stop=(not near)),
                                     reads=[bK, bSELT], writes=[bPS[ps]])
                            if near:
                                S.op("pe", lambda q: q.matmul(PS[ps], JB, BTP[:, r + 1, :], start=False, stop=True),
                                     reads=[bK, bBTP], writes=[bPS[ps]])
                                S.op("act", lambda q: q.activation(pt, PS[ps], AF.Exp), reads=[bPS[ps]], writes=[bpt])
                            else:
                                S.op("act", lambda q, h=h: q.activation(pt, PS[ps], AF.Exp, bias=RB31[:, h:h + 1]), reads=[bPS[ps], bK], writes=[bpt])

                        def stage2(c, hh=hh, nch=nch):
                            pt = PT[c % NPT]
                            bpt = bPT[c % NPT]
                            S.op("pe", lambda q: q.matmul(PS[2], VP[:, c, hh * 128:(hh + 1) * 128], pt, start=(c == 0), stop=(c == nch - 1)),
                                 reads=[bVP, bpt], writes=[bPS[2]])
                            if c == 0:
                                S.op("dve", lambda q: q.tensor_copy(SUMP, pt), reads=[bpt], writes=[bSUM])
                            else:
                                S.op("dve", lambda q: q.tensor_tensor(SUMP, SUMP, pt, ALU.add), reads=[bpt, bSUM], writes=[bSUM])

                        LOOK = 3
                        def tail_b():
                            S.op("pe", lambda q: q.matmul(PS[3], ONEB, SH, start=True, stop=False), reads=[bK, bREC], writes=[bPS[3]])
                            S.op("pe", lambda q: q.matmul(PS[3], ONEB, SL, start=False, stop=True), reads=[bK, bREC], writes=[bPS[3]])
                            S.op("dve", lambda q: q.reciprocal(REC, PS[3]), reads=[bPS[3]], writes=[bREC])
                            S.op("dve", lambda q: q.tensor_tensor(OT, PS[2], REC, ALU.mult), reads=[bPS[2], bREC], writes=[bOT])

                        def tail_d(T=T, sw=sw, h=h):
                            for i in range(KC):
                                po = 4 + (i % 2)
                                S.op("pe", lambda q, i=i, po=po: q.matmul(PS[po], WD[sw][:, i * 128:(i + 1) * 128], OT, start=True, stop=True),
                                     reads=[bWD[sw], bOT], writes=[bPS[po]])
                                add_into_x(i, T, po, h == 0)

                        cb = 1
                        cd = min(10, nch + LOOK - 1)
                        for c in range(nch + LOOK):
                            if c < nch:
                                stage1(c)
                            if c == cb and pending[0] is not None:
                                pending[0]()
                                pending[0] = None
                            if c == cd and pending[1] is not None:
                                pending[1]()
                                pending[1] = None
                            if c >= LOOK:
                                stage2(c - LOOK)
                        S.op("act", lambda q: q.copy(SH, SUMP), reads=[bSUM], writes=[bREC])
                        S.op("dve", lambda q: q.tensor_tensor(SL, SUMP, SH, ALU.subtract), reads=[bSUM, bREC], writes=[bREC])
                        pending[0] = tail_b
                        pending[1] = tail_d
                    pending[0]()
                    pending[1]()
                    wdS.release(wpos)
            barrier()

        def mlstm_mixer(layer, sg):
            barrier()
            A.off = scratch_base
            QT = A.bf16(2 * TOK).rearrange("p (c t) -> p c t", c=2)
            KT = A.bf16(2 * TOK).rearrange("p (c t) -> p c t", c=2)
            KM = A.bf16(16 * 256).rearrange("p (a f) -> p a f", a=16)
            VA = A.bf16(16 * 384).rearrange("p (a f) -> p a f", a=16)
            EBI = A.bf16(TOK)
            CF = A.f32(768).rearrange("p (c f) -> p c f", c=2)
            CB = A.bf16(768).rearrange("p (c f) -> p c f", c=2)
            WFR = A.bf16(KC * 128).rearrange("p (k f) -> p k f", k=KC)
            CGW = A.bf16(64).rearrange("p (k f) -> p k f", k=KC)
            CM = A.f32(80)
            CC = A.f32(384)
            IDB = A.bf16(128)
            O256 = A.bf16(128)
            GT = A.f32(128).rearrange("p (a g) -> p a g", a=16)
            LOGF = A.f32(64).rearrange("p (a g) -> p a g", a=16)
            BT_ = A.f32(64).rearrange("p (a g) -> p a g", a=16)
            EV = A.f32(64).rearrange("p (a g) -> p a g", a=16)
            NBG = A.f32(8)
            DEC = A.f32(32)
            TRIB = A.bf16(128)
            LHI = A.bf16(64)
            LLO = A.bf16(64)
            rbase = A.off
            PRE = A.f32(TOK + 4)
            ACC = A.f32(TOK)
            rend = A.off
            A.off = rbase
            WTS = [A.bf16(128), A.bf16(128)]
            DEN = A.f32(TT)
            HC = A.f32(2 * TT).rearrange("p (c t) -> p c t", c=2)
            HCB = A.bf16(2 * TT).rearrange("p (c t) -> p c t", c=2)
            HCQ = A.bf16(2 * TT).rearrange("p (c t) -> p c t", c=2)
            HN = A.bf16(2 * TT).rearrange("p (c t) -> p c t", c=2)
            SG = A.bf16(2 * TT).rearrange("p (c t) -> p c t", c=2)
            A.off = max(A.off, rend)
            assert A.off <= A.words, A.off
            bQT, bKT, bKM, bVA, bEBI, bCF, bCB, bWFR, bK = Buf(), Buf(), Buf(), Buf(), Buf(), Buf(), Buf(), Buf(), Buf()
            bG, bR1, bR2, bWT0, bDEN, bHC, bHCB, bHCQ, bHN, bSG, bDEC = Buf(), Buf(), Buf(), Buf(), Buf(), Buf(), Buf(), Buf(), Buf(), Buf(), Buf()
            TRI = CC[:, 0:128]
            CAUS = CC[:, 128:256]
            S.dma("pool", CGW, cgw.rearrange("p (k f) -> p k f", k=KC), writes=[bK])
            S.dma("sp", CM, cmisc, writes=[bK])
            S.dma("sp", CC, cconst, writes=[bK])
            S.op("act", lambda q: q.copy(IDB, CC[:, 256:384]), reads=[bK], writes=[bK])
            S.op("dve", lambda q: q.memset(O256, 1.0 / 256.0), writes=[bK])
            S.op("dve", lambda q: q.tensor_scalar_mul(NBG, CM[:, 72:80], -1.0), reads=[bK], writes=[bK])
            for tt in range(16):
                for k in range(KC):
                    S.op("pe", lambda q, tt=tt, k=k: q.matmul(PS[7][:, tt * 8:(tt + 1) * 8], Xb[:, k, tt * 128:(tt + 1) * 128], CGW[:, k, :], start=(k == 0), stop=(k == KC - 1)),
                         reads=[bK] + [bXb[k][tt // 4]], writes=[bPS[7]])
            S.op("dve", lambda q: q.tensor_tensor(GT, PS[7][:, 0:128].rearrange("p (a g) -> p a g", a=16), CM[:, 72:80].unsqueeze(1).to_broadcast([128, 16, 8]), ALU.add),
                 reads=[bPS[7], bK], writes=[bG])
            S.op("act", lambda q: q.activation(LOGF, GT[:, :, 4:8], AF.Exp, scale=-1.0), reads=[bG], writes=[bG])
            S.op("dve", lambda q: q.tensor_scalar_add(LOGF, LOGF, 1.0), reads=[bG], writes=[bG])
            S.op("act", lambda q: q.activation(LOGF, LOGF, AF.Ln), reads=[bG], writes=[bG])
            S.op("act", lambda q: q.copy(TRIB, TRI), reads=[bK], writes=[bK])
            S.op("act", lambda q: q.copy(LHI, LOGF.rearrange("p a g -> p (a g)")), reads=[bG], writes=[bG])
            S.op("dve", lambda q: q.tensor_tensor(LLO, LOGF.rearrange("p a g -> p (a g)"), LHI, ALU.subtract), reads=[bG], writes=[bG])
            S.op("pe", lambda q: q.matmul(PS[7][:, 128:192], TRIB, LHI, start=True, stop=False), reads=[bK, bG], writes=[bPS[7]])
            S.op("pe", lambda q: q.matmul(PS[7][:, 128:192], TRIB, LLO, start=False, stop=True), reads=[bK, bG], writes=[bPS[7]])
            S.op("dve", lambda q: q.tensor_tensor(BT_, GT[:, :, 0:4], PS[7][:, 128:192].rearrange("p (a g) -> p a g", a=16), ALU.add),
                 reads=[bPS[7], bG], writes=[bG])
            S.op("dve", lambda q: q.tensor_scalar_add(BT_, BT_, -math.log(16.0)), reads=[bG], writes=[bG])
            S.op("act", lambda q: q.activation(EV, BT_, AF.Exp), reads=[bG], writes=[bG])

            def conv_silu(h, isk, cc, slot, OUTT, bOUT):
                ci = (h * 2 + isk) * 2 + cc
                cw = (isk * 8 + h * 2 + cc) * 4
                if sg == 0:
                    S.op("dve", lambda q: q.memset(PRE[:, 0:4], 0.0), writes=[bR1])
                else:
                    S.dma("sp", PRE[:, 0:4], cctx[ci], reads=[db("cc%d" % ci)], writes=[bR1])
                for t in range(NT):
                    ts = slice(t * TT, (t + 1) * TT)
                    p = t % 3
                    for k in range(KC):
                        S.op("pe", lambda q, k=k, ts=ts, p=p: q.matmul(PS[p], WGU[slot][:, k, cc * 128:(cc + 1) * 128], Xb[:, k, ts], start=(k == 0), stop=(k == KC - 1)),
                             reads=[bWGU[slot], bXb[k][t]], writes=[bPS[p]])
                    S.op("act", lambda q, t=t, p=p: q.copy(PRE[:, 4 + t * TT:4 + (t + 1) * TT], PS[p]), reads=[bPS[p]], writes=[bR1])
                if sg == 0:
                    S.dma("sp", cctx[ci], PRE[:, TOK:TOK + 4], reads=[bR1], writes=[db("cc%d" % ci)])
                S.op("dve", lambda q: q.tensor_scalar_mul(ACC, PRE[:, 1:1 + TOK], CM[:, cw:cw + 1]), reads=[bR1, bK], writes=[bR2])
                for j in range(1, 4):
                    S.op("dve", lambda q, j=j: q.scalar_tensor_tensor(ACC, PRE[:, 1 + j:1 + j + TOK], CM[:, cw + j:cw + j + 1], ACC, ALU.mult, ALU.add),
                         reads=[bR1, bR2, bK], writes=[bR2])
                S.op("act", lambda q: q.activation(OUTT[:, cc, :], ACC, AF.Silu), reads=[bR2], writes=[bOUT])

            for h in range(4):
                sq, pq = guS.take(GU_MIX[layer] + 4 * h + 0)
                sk, pk = guS.take(GU_MIX[layer] + 4 * h + 1)
                sv, pv = guS.take(GU_MIX[layer] + 4 * h + 2)
                so, po_ = guS.take(GU_MIX[layer] + 4 * h + 3)
                w0, wp0 = wdS.take(WD_MIX[layer] + 2 * h)
                w1, wp1 = wdS.take(WD_MIX[layer] + 2 * h + 1)
                wsl = [w0, w1]
                S.dma("pool", WFR, cfrep[h].rearrange("p (k f) -> p k f", k=KC), writes=[bWFR])
                for t in range(NT):
                    ts = slice(t * TT, (t + 1) * TT)
                    p = t % 3
                    for k in range(KC):
                        S.op("pe", lambda q, k=k, ts=ts, p=p: q.matmul(PS[p], WFR[:, k, :], Xb[:, k, ts], start=(k == 0), stop=(k == KC - 1)),
                             reads=[bWFR, bXb[k][t]], writes=[bPS[p]])
                    S.op("act", lambda q, ts=ts, p=p, h=h: q.activation(PRE[:, ts], PS[p], AF.Exp, bias=NBG[:, 4 + h:5 + h], scale=-1.0),
                         reads=[bPS[p], bK], writes=[bR1])
                S.op("dve", lambda q: q.tensor_scalar_add(PRE[:, 0:TOK], PRE[:, 0:TOK], 1.0), reads=[bR1], writes=[bR1])
                S.op("act", lambda q: q.activation(PRE[:, 0:TOK], PRE[:, 0:TOK], AF.Ln), reads=[bR1], writes=[bR1])
                cur, bcur, nxt, bnxt = PRE[:, 0:TOK], bR1, ACC, bR2
                for i in range(7):
                    sh = 1 << i
                    c3 = cur.rearrange("p (c l) -> p c l", l=128)
                    n3 = nxt.rearrange("p (c l) -> p c l", l=128)
                    S.op("dve", lambda q, c3=c3, n3=n3, sh=sh: q.tensor_tensor(n3[:, :, sh:128], c3[:, :, sh:128], c3[:, :, 0:128 - sh], ALU.add),
                         reads=[bcur], writes=[bnxt])
                    S.op("act", lambda q, c3=c3, n3=n3, sh=sh: q.copy(n3[:, :, 0:sh], c3[:, :, 0:sh]), reads=[bcur], writes=[bnxt])
                    cur, bcur, nxt, bnxt = nxt, bnxt, cur, bcur
                S.op("act", lambda q, cur=cur: q.activation(EBI, cur, AF.Exp), reads=[bcur], writes=[bEBI])
                S.op("act", lambda q, cur=cur: q.activation(DEC[:, 0:16], cur.rearrange("p (c l) -> p c l", l=128)[:, :, 127], AF.Exp, scale=-1.0),
                     reads=[bcur], writes=[bDEC])
                for cc in range(2):
                    conv_silu(h, 0, cc, sq, QT, bQT)
                guS.release(pq)
                for cc in range(2):
                    conv_silu(h, 1, cc, sk, KT, bKT)
                guS.release(pk)
                for tt in range(16):
                    p = 3 + (tt % 2)
                    for cc in range(2):
                        S.op("pe", lambda q, tt=tt, cc=cc, p=p: q.matmul(PS[p][:, cc * 128:(cc + 1) * 128], KT[:, cc, tt * 128:(tt + 1) * 128], IDB, start=True, stop=True),
                             reads=[bKT, bK], writes=[bPS[p]])
                    S.op("act", lambda q, tt=tt, p=p: q.copy(KM[:, tt, :], PS[p][:, 0:256]), reads=[bPS[p]], writes=[bKM])
                for tt in range(16):
                    p = 5 + (tt % 2)
                    for k in range(KC):
                        S.op("pe", lambda q, tt=tt, k=k, p=p, sv=sv: q.matmul(PS[p][:, 0:256], Xb[:, k, tt * 128:(tt + 1) * 128], WGU[sv][:, k, :], start=(k == 0), stop=(k == KC - 1)),
                             reads=[bWGU[sv], bXb[k][tt // 4]], writes=[bPS[p]])
                    S.op("act", lambda q, tt=tt, p=p, h=h: q.activation(VA[:, tt, 0:256], PS[p][:, 0:256], AF.Identity, scale=EV[:, tt, h:h + 1]),
                         reads=[bPS[p], bG], writes=[bVA])
                    S.op("dve", lambda q, tt=tt, h=h: q.tensor_scalar(VA[:, tt, 256:384], ones, EV[:, tt, h:h + 1], 1024.0, ALU.mult, ALU.mult),
                         reads=[bones, bG], writes=[bVA])
                guS.release(pv)
                if sg == 0:
                    S.op("dve", lambda q: q.memset(CF, 0.0), writes=[bCF])
                else:
                    S.dma("sp", CF, sctx[h].rearrange("p (c f) -> p c f", c=2), reads=[db("sc%d" % h)], writes=[bCF])
                S.op("act", lambda q: q.copy(CB, CF), reads=[bCF], writes=[bCB])
                bWTS = [bWT0, Buf()]

                def st_stage(tt):
                    WT = WTS[tt % 2]
                    for kc in range(2):
                        S.op("pe", lambda q, kc=kc: q.matmul(PS[3][:, 0:128], KT[:, kc, tt * 128:(tt + 1) * 128], QT[:, kc, tt * 128:(tt + 1) * 128], start=(kc == 0), stop=(kc == 1)),
                             reads=[bKT, bQT], writes=[bPS[3]])
                    S.op("dve", lambda q: q.tensor_tensor(WT, PS[3][:, 0:128], CAUS, ALU.mult), reads=[bPS[3], bK], writes=[bWTS[tt % 2]])

                st_stage(0)
                for tt in range(16):
                    T4 = tt // 4
                    col0 = (tt % 4) * 128
                    WT = WTS[tt % 2]
                    bWT = bWTS[tt % 2]
                    for vc in range(3):
                        S.op("pe", lambda q, tt=tt, vc=vc, col0=col0, WT=WT: q.matmul(PS[vc][:, col0:col0 + 128], VA[:, tt, vc * 128:(vc + 1) * 128], WT, start=True, stop=False),
                             reads=[bVA, bWT], writes=[bPS[vc]])
                    if tt + 1 < 16:
                        st_stage(tt + 1)
                    for vc in range(3):
                        for kc in range(2):
                            S.op("pe", lambda q, tt=tt, vc=vc, kc=kc, col0=col0: q.matmul(PS[vc][:, col0:col0 + 128], CB[:, kc, vc * 128:(vc + 1) * 128], QT[:, kc, tt * 128:(tt + 1) * 128], start=False, stop=(kc == 1)),
                                 reads=[bCB, bQT], writes=[bPS[vc]])
                    S.op("dve", lambda q, tt=tt: q.tensor_scalar_mul(CF, CF, DEC[:, tt:tt + 1]), reads=[bCF, bDEC], writes=[bCF])
                    for kc in range(2):
                        S.op("pe", lambda q, tt=tt, kc=kc: q.matmul(PS[4 + kc][:, 0:384], KM[:, tt, kc * 128:(kc + 1) * 128], VA[:, tt, :], start=True, stop=True),
                             reads=[bKM, bVA], writes=[bPS[4 + kc]])
                    for kc in range(2):
                        S.op("dve", lambda q, kc=kc, tt=tt: q.scalar_tensor_tensor(CF[:, kc, :], PS[4 + kc][:, 0:384], DEC[:, tt:tt + 1], CF[:, kc, :], ALU.mult, ALU.add),
                             reads=[bPS[4 + kc], bCF, bDEC], writes=[bCF])
                    S.op("dve", lambda q: q.tensor_copy(CB, CF), reads=[bCF], writes=[bCB])
                    if tt % 4 != 3:
                        continue
                    ts = slice(T4 * TT, (T4 + 1) * TT)
                    S.op("act", lambda q: q.activation(DEN, PS[2], AF.Abs), reads=[bPS[2]], writes=[bDEN])
                    S.op("act", lambda q: q.copy(HC[:, 0, :], PS[0]), reads=[bPS[0]], writes=[bHC])
                    S.op("dve", lambda q: q.tensor_copy(HC[:, 1, :], PS[1]), reads=[bPS[1]], writes=[bHC])
                    S.op("dve", lambda q, ts=ts: q.tensor_tensor(DEN, DEN, EBI[:, ts], ALU.max), reads=[bDEN, bEBI], writes=[bDEN])
                    S.op("dve", lambda q: q.reciprocal(DEN, DEN), reads=[bDEN], writes=[bDEN])
                    for cc in range(2):
                        p = 6 + cc
                        for k in range(KC):
                            S.op("pe", lambda q, k=k, ts=ts, p=p, cc=cc, so=so: q.matmul(PS[p], WGU[so][:, k, cc * 128:(cc + 1) * 128], Xb[:, k, ts], start=(k == 0), stop=(k == KC - 1)),
                                 reads=[bWGU[so], bXb[k][T4]], writes=[bPS[p]])
                        S.op("act", lambda q, cc=cc, p=p: q.activation(SG[:, cc, :], PS[p], AF.Sigmoid), reads=[bPS[p]], writes=[bSG])
                    for vc in range(2):
                        S.op("dve", lambda q, vc=vc: q.tensor_tensor(HC[:, vc, :], HC[:, vc, :], DEN, ALU.mult), reads=[bHC, bDEN], writes=[bHC])
                        S.op("dve", lambda q, vc=vc: q.tensor_tensor(HC[:, vc, :], HC[:, vc, :], SG[:, vc, :], ALU.mult), reads=[bHC, bSG], writes=[bHC])
                    S.op("act", lambda q: q.copy(HCB, HC), reads=[bHC], writes=[bHCB])
                    S.op("act", lambda q: q.activation(HCQ, HC, AF.Square), reads=[bHC], writes=[bHCQ])
                    for vc in range(2):
                        S.op("pe", lambda q, vc=vc: q.matmul(PS[6], O256, HCB[:, vc, :], start=(vc == 0), stop=(vc == 1)), reads=[bK, bHCB], writes=[bPS[6]])
                    for vc in range(2):
                        S.op("pe", lambda q, vc=vc: q.matmul(PS[7], O256, HCQ[:, vc, :], start=(vc == 0), stop=(vc == 1)), reads=[bK, bHCQ], writes=[bPS[7]])
                    S.op("act", lambda q: q.copy(MEAN, PS[6]), reads=[bPS[6]], writes=[bST])
                    S.op("dve", lambda q: q.tensor_tensor(NMR, MEAN, MEAN, ALU.mult), reads=[bST], writes=[bST])
                    S.op("dve", lambda q: q.tensor_tensor(RSTD, PS[7], NMR, ALU.subtract), reads=[bPS[7], bST], writes=[bST])
                    S.op("dve", lambda q: q.tensor_scalar_add(RSTD, RSTD, LN_EPS), reads=[bST], writes=[bST])
                    S.op("act", lambda q: q.sqrt(RSTD, RSTD), reads=[bST], writes=[bST])
                    S.op("dve", lambda q: q.reciprocal(RSTD, RSTD), reads=[bST], writes=[bST])
                    S.op("dve", lambda q: q.scalar_tensor_tensor(NMR, MEAN, -1.0, RSTD, ALU.mult, ALU.mult), reads=[bST], writes=[bST])
                    for vc in range(2):
                        S.op("dve", lambda q, vc=vc: q.tensor_tensor(HC[:, vc, :], HC[:, vc, :], RSTD, ALU.mult), reads=[bHC, bST], writes=[bHC])
                        S.op("dve", lambda q, vc=vc: q.tensor_tensor(HC[:, vc, :], HC[:, vc, :], NMR, ALU.add), reads=[bHC, bST], writes=[bHC])
                        S.op("act", lambda q, vc=vc, h=h: q.activation(HN[:, vc, :], HC[:, vc, :], AF.Identity, scale=CM[:, 64 + 2 * h + vc:65 + 2 * h + vc]),
                             reads=[bHC, bK], writes=[bHN])
                    for i in range(KC):
                        po = 4 + (i % 2)
                        for vc in range(2):
                            S.op("pe", lambda q, i=i, vc=vc, po=po, wsl=wsl: q.matmul(PS[po], WD[wsl[vc]][:, i * 128:(i + 1) * 128], HN[:, vc, :], start=(vc == 0), stop=(vc == 1)),
                                 reads=[bWD[wsl[vc]], bHN], writes=[bPS[po]])
                        add_into_x(i, T4, po, h == 0)
                guS.release(po_)
                wdS.release(wp0)
                wdS.release(wp1)
                if sg == 0:
                    S.dma("sp", sctx[h].rearrange("p (c f) -> p c f", c=2), CF, reads=[bCF], writes=[db("sc%d" % h)])
            barrier()

        for sg in range(NSEG):
            for k in range(KC):
                S.dma("sp", X[:, k, :], xT[sg][:, k * TOK:(k + 1) * TOK], writes=bX[k])
            for k in range(KC):
                for t in range(NT):
                    S.op("act", lambda q, k=k, t=t: q.copy(Xb[:, k, t * TT:(t + 1) * TT], X[:, k, t * TT:(t + 1) * TT]),
                         reads=[bX[k][t]], writes=[bXb[k][t]])
            for (kind, layer, f) in phases:
                if kind == "ffn":
                    ffn(layer, f)
                else:
                    [moba_mixer, pool_mixer, mlstm_mixer][layer % 3](layer, sg)
                    layer_norm(layer, 1)
            for k in range(KC):
                S.dma("sp", yT[sg][:, k * TOK:(k + 1) * TOK], X[:, k, :], reads=bX[k], is_output=True)
        S.finish()
        with nc.Block() as block:
            S.emit(block)
    return nc


def _gu_layout(w):
    n = w.shape[1] // 256
    return np.ascontiguousarray(w.reshape(KC, 128, n, 256).transpose(2, 1, 0, 3)).reshape(n, 128, KC * 256)


def prep_inputs(inputs, phases):
    f32 = lambda a: np.asarray(a, dtype=np.float32)
    x = f32(inputs["x"])
    xs = []
    for b in range(BATCH):
        a = x[b].reshape(NSEG, TOK, KC, 128).transpose(0, 3, 2, 1)
        xs.append(np.ascontiguousarray(a).reshape(NSEG, 128, KC * TOK))
    wgu_in = f32(inputs["ffn_w_gu"])
    g = wgu_in[..., :DFF].reshape(DEPTH, 2, KC, 128, NJ, 128)
    u = wgu_in[..., DFF:].reshape(DEPTH, 2, KC, 128, NJ, 128)
    gu = np.stack([g, u], axis=5)
    gu = np.ascontiguousarray(gu.transpose(0, 1, 4, 3, 2, 5, 6)).reshape(DEPTH * 2 * NJ, 128, KC * 256)
    pieces = [gu]
    a_in, b_in, c_in = f32(inputs["a_w_in"]), f32(inputs["b_w_in"]), f32(inputs["c_w_in"])
    for layer in range(DEPTH):
        mk, j = layer % 3, layer // 3
        if mk == 0:
            w = a_in[j]
            q, k, v = w[:, 0:D], w[:, D:2 * D], w[:, 2 * D:3 * D]
            cols = []
            for hp in range(4):
                cols.append(v[:, hp * 256:(hp + 1) * 256])
                for h in (2 * hp, 2 * hp + 1):
                    cols.append(np.concatenate([q[:, h * 128:(h + 1) * 128], k[:, h * 128:(h + 1) * 128]], axis=1))
            pieces.append(_gu_layout(np.concatenate(cols, axis=1)))
        elif mk == 1:
            pieces.append(_gu_layout(b_in[j]))
        else:
            w = c_in[j]
            cols = []
            for h in range(4):
                for part in range(4):
                    cols.append(w[:, part * D + h * 256: part * D + (h + 1) * 256])
            pieces.append(_gu_layout(np.concatenate(cols, axis=1)))
    wgu = np.concatenate(pieces, axis=0)
    assert wgu.shape[0] == NP_GU, wgu.shape
    wds = [f32(inputs["ffn_w_down"]).reshape(DEPTH * 2 * NJ, 128, D)]
    for layer in range(DEPTH):
        mk, j = layer % 3, layer // 3
        wo = [inputs["a_w_out"], inputs["b_w_out"], inputs["c_w_out"]][mk]
        wds.append(f32(wo)[j].reshape(8, 128, D))
    wd = np.ascontiguousarray(np.concatenate(wds, axis=0))
    lg = f32(inputs["ln_g"]).reshape(DEPTH, 3, KC, 128)
    lb = f32(inputs["ln_b"]).reshape(DEPTH, 3, KC, 128)
    lnp = np.stack([lg, lb], axis=2)
    lnp = np.ascontiguousarray(lnp.transpose(4, 0, 1, 2, 3)).reshape(128, DEPTH * 3 * 2 * KC)
    wg = f32(inputs["b_w_group"])[0].reshape(4, 2, 128, 256).transpose(2, 0, 1, 3)
    pwg = np.ascontiguousarray(wg).reshape(128, 2048)
    invc = np.zeros((4, HALO), np.float32)
    for m in range(4):
        invc[m] = 1.0 / np.minimum(np.arange(HALO) + 1.0, float(2 ** (m + 1)))
    pmisc = np.zeros((128, 72), np.float32)
    pmisc[:, :64] = invc.reshape(1, 64)
    pmisc[:, 64:72] = f32(inputs["b_scale"])[0].reshape(8, 128).T
    cw_in = c_in[0]
    gw = cw_in[:, 4 * D:4 * D + 8]
    cgw = np.ascontiguousarray(gw.reshape(KC, 128, 8).transpose(1, 0, 2)).reshape(128, 64)
    cfrep = np.empty((4, 128, KC, 128), np.float32)
    for h in range(4):
        cfrep[h] = np.broadcast_to(gw[:, 4 + h].reshape(KC, 128).T[:, :, None], (128, KC, 128))
    cfrep = cfrep.reshape(4, 128, KC * 128)
    cmisc = np.zeros((128, 80), np.float32)
    cmisc[:, 0:64] = f32(inputs["c_conv_w"])[0].reshape(4, 16, 128).transpose(2, 1, 0).reshape(128, 64)
    cmisc[:, 64:72] = f32(inputs["c_norm_g"])[0].reshape(8, 128).T
    cmisc[:, 72:80] = np.broadcast_to(f32(inputs["c_b_gates"])[0].reshape(1, 8), (128, 8))
    ii = np.arange(128)
    tri = (ii[:, None] <= ii[None, :]).astype(np.float32)
    cconst = np.concatenate([tri, tri, np.eye(128, dtype=np.float32)], axis=1)
    rb = f32(inputs["rel_bias"])
    arbx = np.concatenate([rb, np.full((1, 8), -30000.0, np.float32)], axis=0)
    delta = np.arange(NDELTA) - 511
    nn = np.maximum(delta, 0)
    nf = np.maximum(nn, 1).astype(np.float32)
    large = 16 + (np.log(nf / np.float32(16)) / np.float32(math.log(128 / 16)) * np.float32(16)).astype(np.int32)
    large = np.minimum(large, 31)
    bucket = np.where(nn < 16, nn, large)
    aoh = np.zeros((33, NDELTA), np.float32)
    for i in range(NDELTA):
        if delta[i] >= 0:
            aoh[bucket[i], i] = 1.0
        else:
            aoh[32, i] = 1.0
    asel = np.zeros((16, 16, 128), np.float32)
    for n in range(16):
        asel[n, n, :] = 1.0
    asel = asel.reshape(16, 2048)
    gbi = np.arange(16)
    pm = np.where(gbi[None, :] < gbi[:, None], 0.0, -1e30).astype(np.float32)
    no = (1.0 - np.eye(16)).astype(np.float32)
    amask = np.ascontiguousarray(np.broadcast_to(np.concatenate([pm.reshape(1, 256), no.reshape(1, 256)], axis=1), (128, 512)))
    aconst = np.concatenate([np.eye(128, dtype=np.float32)[::-1], np.eye(128, dtype=np.float32)], axis=1)
    aconst = np.ascontiguousarray(aconst)
    arb31 = np.ascontiguousarray(np.broadcast_to(rb[31].reshape(1, 8), (128, 8)))
    gu_ids, wd_ids = used_pieces(phases)
    if len(gu_ids) != NP_GU:
        wgu = np.ascontiguousarray(wgu[gu_ids])
    if len(wd_ids) != NP_WD:
        wd = np.ascontiguousarray(wd[wd_ids])
    common = {"wgu": wgu, "wd": wd, "lnp": lnp, "pwg": pwg, "pmisc": pmisc,
              "asel": asel, "amask": amask, "aconst": aconst, "arb31": arb31, "arbx": arbx, "aoh": aoh,
              "cgw": cgw, "cfrep": cfrep, "cmisc": cmisc, "cconst": cconst}
    return [dict(common, xT=xs[c]) for c in range(NCORES)]


def run(inputs, phases=None):
    phases = ALL_PHASES if phases is None else phases
    nc = build_program(phases)
    in_maps = prep_inputs(inputs, phases)
    res = run_bass_kernel_spmd(nc, in_maps, core_ids=list(range(NCORES)))
    out = np.empty((BATCH, SEQ, D), dtype=np.float32)
    for b in range(NCORES):
        yT = np.asarray(res.results[b]["yT"]).reshape(NSEG, 128, KC, TOK)
        out[b] = yT.transpose(0, 3, 2, 1).reshape(SEQ, D)
    return out


def kernel(**inputs):
    return run(inputs)
```

```python
import math
from contextlib import ExitStack

import numpy as np
import concourse.bass as bass
import concourse.mybir as mybir
from concourse.bass_utils import run_bass_kernel_spmd

F32 = mybir.dt.float32
BF16 = mybir.dt.bfloat16
AF = mybir.ActivationFunctionType
ALU = mybir.AluOpType
AX = mybir.AxisListType

D = 1024
KC = 8
DFF = 2816
NJ = 22
DEPTH = 4
SEQ = 4096
BATCH = 4
TOK = 2048
TT = 512
NT = TOK // TT
ALPHA = (2 * DEPTH) ** 0.25
LN_EPS = 1e-5
JGROUPS = [8, 7, 7]
TOUT = 3
SEM_LIMIT = 30000


class Buf:
    __slots__ = ("w", "r", "name", "excl")

    def __init__(self, name="", excl=False):
        self.w = []
        self.r = []
        self.name = name
        self.excl = excl


class Eng:
    def __init__(self, name, same_sync):
        self.name = name
        self.prog = []
        self.sem = None
        self.count = 0
        self.waited = {}
        self.same_sync = same_sync
        self.ring = []
        self.ring_pos = 0


class Sched:
    def __init__(self, nc, stack):
        self.nc = nc
        self.stack = stack
        self.nsem = 0
        self.E = {
            "pe": Eng("pe", False),
            "act": Eng("act", True),
            "dve": Eng("dve", True),
            "pool": Eng("pool", True),
            "sp": Eng("sp", False),
        }
        for e in self.E.values():
            e.sem = self.new_sem(e.name)
        self.out_deps = []

    def new_sem(self, name):
        self.nsem += 1
        return self.stack.enter_context(self.nc.semaphore("s%d_%s" % (self.nsem, name)))

    def _ensure(self, e, dep):
        sem, val = dep
        if sem is e.sem and not e.same_sync:
            return
        key = id(sem)
        if e.waited.get(key, (None, 0))[1] >= val:
            return
        e.waited[key] = (sem, val)
        e.prog.append(("wait", sem, val))

    def _deps(self, e, reads, writes):
        for b in reads:
            for d in b.w:
                self._ensure(e, d)
            if b.excl:
                for d in b.r:
                    if d[0] is not e.sem:
                        self._ensure(e, d)
        for b in writes:
            for d in b.w:
                self._ensure(e, d)
            for d in b.r:
                self._ensure(e, d)

    def _commit(self, dep, reads, writes):
        for b in reads:
            for i, d in enumerate(b.r):
                if d[0] is dep[0]:
                    if d[1] < dep[1]:
                        b.r[i] = dep
                    break
            else:
                b.r.append(dep)
        for b in writes:
            b.w = [dep]
            b.r = []

    def op(self, eng, fn, reads=(), writes=()):
        e = self.E[eng]
        self._deps(e, reads, writes)
        if e.count >= SEM_LIMIT:
            e.sem = self.new_sem(e.name)
            e.count = 0
        e.count += 1
        dep = (e.sem, e.count)
        e.prog.append(("op", fn, e.sem, 1))
        self._commit(dep, reads, writes)
        return dep

    def dma(self, eng, out, in_, reads=(), writes=(), is_output=False, nring=8):
        e = self.E[eng]
        self._deps(e, reads, writes)
        if not e.ring:
            e.ring = [[self.new_sem(e.name + "d"), 0] for _ in range(nring)]
        slot = e.ring[e.ring_pos % len(e.ring)]
        e.ring_pos += 1
        if slot[1] > 0:
            self._ensure(e, (slot[0], slot[1]))
        slot[1] += 16
        dep = (slot[0], slot[1])
        e.prog.append(("op", lambda q, o=out, i=in_: q.dma_start(out=o, in_=i), slot[0], 16))
        self._commit(dep, reads, writes)
        if is_output:
            self.out_deps.append(dep)
        return dep

    def alias_after(self, dst, src):
        for d in dst:
            for s in src:
                for dep in list(s.w) + list(s.r):
                    for i, x in enumerate(d.r):
                        if x[0] is dep[0]:
                            if x[1] < dep[1]:
                                d.r[i] = dep
                            break
                    else:
                        d.r.append(dep)

    def finish(self):
        e = self.E["sp"]
        for d in self.out_deps:
            self._ensure(e, d)

    def emit(self, block):
        def run(prog):
            def f(q):
                for it in prog:
                    if it[0] == "wait":
                        q.wait_ge(it[1], it[2])
                    else:
                        it[1](q).then_inc(it[2], it[3])
            return f
        block.tensor(run(self.E["pe"].prog))
        block.scalar(run(self.E["act"].prog))
        block.vector(run(self.E["dve"].prog))
        block.gpsimd(run(self.E["pool"].prog))
        block.sync(run(self.E["sp"].prog))


class Arena:
    def __init__(self, nc, words):
        self.t = nc.alloc_sbuf_tensor("arena", [128, words], F32)
        self.words = words
        self.off = 0

    def f32(self, n):
        a = self.t[:, self.off:self.off + n]
        self.off += n
        assert self.off <= self.words, "SBUF arena overflow %d" % self.off
        return a

    def bf16(self, n):
        assert n % 2 == 0
        return self.f32(n // 2).bitcast(BF16)


NCORES = 4
NSEG = 2
HALO = 16
NDELTA = 1152


class Stream:
    def __init__(self, S, eng, dram, slots, bufs, order, view, idmap):
        self.S, self.eng, self.dram, self.slots, self.bufs = S, eng, dram, slots, bufs
        self.idmap = idmap
        self.order, self.view = order, view
        self.N = len(slots)
        self.issued = 0
        self.done = 0
        self.cursor = 0

    def prefetch(self):
        while self.issued < len(self.order) and self.issued < self.done + self.N:
            i = self.issued
            s = i % self.N
            self.S.dma(self.eng, self.slots[s], self.view(self.dram[self.idmap[self.order[i]]]), writes=[self.bufs[s]])
            self.issued += 1

    def take(self, piece):
        pos = self.cursor
        assert self.order[pos] == piece, (pos, self.order[pos], piece)
        self.cursor += 1
        self.prefetch()
        assert pos < self.issued
        return pos % self.N, pos

    def release(self, pos):
        assert pos == self.done, (pos, self.done)
        self.done = pos + 1
        self.prefetch()


def piece_tables():
    n = DEPTH * 2 * NJ
    gu_mix = {}
    for layer in range(DEPTH):
        gu_mix[layer] = n
        n += {0: 12, 1: 4, 2: 16}[layer % 3]
    m = DEPTH * 2 * NJ
    wd_mix = {}
    for layer in range(DEPTH):
        wd_mix[layer] = m
        m += 8
    return gu_mix, n, wd_mix, m


GU_MIX, NP_GU, WD_MIX, NP_WD = piece_tables()


def phase_pieces(ph):
    kind, layer, f = ph
    if kind == "ffn":
        ids = [(layer * 2 + f) * NJ + j for j in range(NJ)]
        return ids, list(ids)
    mk = layer % 3
    ngu = {0: 12, 1: 4, 2: 16}[mk]
    return [GU_MIX[layer] + i for i in range(ngu)], [WD_MIX[layer] + i for i in range(8)]


ALL_PHASES = []
for _l in range(DEPTH):
    ALL_PHASES += [("ffn", _l, 0), ("mix", _l, 0), ("ffn", _l, 1)]


def used_pieces(phases):
    gu, wdp = set(), set()
    for ph in phases:
        a, b = phase_pieces(ph)
        gu.update(a)
        wdp.update(b)
    return sorted(gu), sorted(wdp)


def build_program(phases):
    nc = bass.Bass("TRN2", target_bir_lowering=False)
    gu_ids, wd_ids = used_pieces(phases)
    gu_map = {p: i for i, p in enumerate(gu_ids)}
    wd_map = {p: i for i, p in enumerate(wd_ids)}
    xT = nc.dram_tensor("xT", [NSEG, 128, KC * TOK], F32, kind="ExternalInput").ap()
    wgu = nc.dram_tensor("wgu", [len(gu_ids), 128, KC * 256], F32, kind="ExternalInput").ap()
    wd = nc.dram_tensor("wd", [len(wd_ids), 128, D], F32, kind="ExternalInput").ap()
    lnp = nc.dram_tensor("lnp", [128, DEPTH * 3 * 2 * KC], F32, kind="ExternalInput").ap()
    pwg = nc.dram_tensor("pwg", [128, 2048], F32, kind="ExternalInput").ap()
    pmisc = nc.dram_tensor("pmisc", [128, 72], F32, kind="ExternalInput").ap()
    yT = nc.dram_tensor("yT", [NSEG, 128, KC * TOK], F32, kind="ExternalOutput").ap()
    uctx = nc.dram_tensor("uctx", [KC, 128, HALO], F32).ap()
    cgw = nc.dram_tensor("cgw", [128, 64], F32, kind="ExternalInput").ap()
    cfrep = nc.dram_tensor("cfrep", [4, 128, KC * 128], F32, kind="ExternalInput").ap()
    cmisc = nc.dram_tensor("cmisc", [128, 80], F32, kind="ExternalInput").ap()
    cconst = nc.dram_tensor("cconst", [128, 384], F32, kind="ExternalInput").ap()
    cctx = nc.dram_tensor("cctx", [16, 128, 4], F32).ap()
    asel = nc.dram_tensor("asel", [16, 16 * 128], F32, kind="ExternalInput").ap()
    amask = nc.dram_tensor("amask", [128, 512], F32, kind="ExternalInput").ap()
    aconst = nc.dram_tensor("aconst", [128, 256], F32, kind="ExternalInput").ap()
    arb31 = nc.dram_tensor("arb31", [128, 8], F32, kind="ExternalInput").ap()
    arbx = nc.dram_tensor("arbx", [33, 8], F32, kind="ExternalInput").ap()
    aoh = nc.dram_tensor("aoh", [33, NDELTA], F32, kind="ExternalInput").ap()
    tvd = nc.dram_tensor("tvd", [8, NDELTA], BF16).ap()
    ktctx_all = [nc.dram_tensor("ktctx%d" % j, [8, 128, TOK], BF16).ap() for j in range(2)]
    vctx_all = [nc.dram_tensor("vctx%d" % j, [4, 128, 16 * 256], BF16).ap() for j in range(2)]
    kmctx_all = [nc.dram_tensor("kmctx%d" % j, [128, 64], F32).ap() for j in range(2)]
    tv_state = {}
    DB = {}

    def db(name):
        if name not in DB:
            DB[name] = Buf(name)
        return DB[name]
    sctx = nc.dram_tensor("sctx", [4, 128, 768], F32).ap()

    gu_order, wd_order = [], []
    for sg in range(NSEG):
        for ph in phases:
            a, b = phase_pieces(ph)
            gu_order += a
            wd_order += b

    stack = ExitStack()
    with stack:
        S = Sched(nc, stack)
        A = Arena(nc, 53000)
        X = A.f32(KC * TOK).rearrange("p (k t) -> p k t", k=KC)
        Xb = A.bf16(KC * TOK).rearrange("p (k t) -> p k t", k=KC)
        bX = [[Buf("X%d_%d" % (k, t)) for t in range(NT)] for k in range(KC)]
        bXb = [[Buf("Xb%d_%d" % (k, t)) for t in range(NT)] for k in range(KC)]
        LNP = A.f32(DEPTH * 3 * 2 * KC)
        bLNP = Buf("lnp")
        ones = A.bf16(128)
        bones = Buf("ones")
        NGU = 4
        WGU = [A.bf16(KC * 256).rearrange("p (k f) -> p k f", k=KC) for _ in range(NGU)]
        bWGU = [Buf("wgu%d" % i) for i in range(NGU)]
        NWD = 8
        WD = [A.bf16(D) for _ in range(NWD)]
        bWD = [Buf("wd%d" % i) for i in range(NWD)]
        MEAN = A.f32(TT)
        RSTD = A.f32(TT)
        NMR = A.f32(TT)
        bST = Buf("stats")
        YT = [A.f32(TT) for _ in range(2)]
        bYT = [Buf("yt%d" % i) for i in range(2)]
        scratch_base = A.off
        ZB = A.bf16(KC * TT).rearrange("p (k t) -> p k t", k=KC)
        bZB = Buf("zb")
        ZQ = A.bf16(KC * TT).rearrange("p (k t) -> p k t", k=KC)
        bZQ = Buf("zq")
        HG = A.bf16(8 * TOK).rearrange("p (j t) -> p j t", j=8)
        bHG = [[Buf("h%d_%d" % (j, t)) for t in range(NT)] for j in range(8)]
        SIL = [A.f32(TT) for _ in range(2)]
        bSIL = [Buf("sil%d" % i) for i in range(2)]
        PS = [nc.alloc_psum_tensor("ps%d" % i, [128, 512], F32)[:, :] for i in range(8)]
        bPS = [Buf("ps%d" % i, excl=True) for i in range(8)]

        guS = Stream(S, "pool", wgu, WGU, bWGU, gu_order, lambda a: a.rearrange("p (k f) -> p k f", k=KC), gu_map)
        wdS = Stream(S, "pool", wd, WD, bWD, wd_order, lambda a: a, wd_map)

        def barrier():
            deps = []
            for e in S.E.values():
                if e.count > 0:
                    deps.append((e.sem, e.count))
                for slot in e.ring:
                    if slot[1] > 0:
                        deps.append((slot[0], slot[1]))
            for e in S.E.values():
                for d in deps:
                    S._ensure(e, d)

        S.dma("sp", LNP, lnp, writes=[bLNP])
        S.op("dve", lambda q: q.memset(ones, 1.0 / 1024.0), writes=[bones])

        def ln_col(layer, which, gb, k):
            c = ((layer * 3 + which) * 2 + gb) * KC + k
            return LNP[:, c:c + 1]

        def layer_norm(layer, which):
            for t in range(NT):
                ln_tile(layer, which, t)

        def ln_tile(layer, which, t):
            if True:
                ts = slice(t * TT, (t + 1) * TT)
                allx = [bX[k][t] for k in range(KC)]
                S.op("act", lambda q, ts=ts: q.copy(ZB, X[:, :, ts]), reads=allx, writes=[bZB])
                S.op("act", lambda q, ts=ts: q.activation(ZQ, X[:, :, ts], AF.Square), reads=allx, writes=[bZQ])
                for k in range(KC):
                    S.op("pe", lambda q, k=k: q.matmul(PS[6], ones, ZB[:, k, :], start=(k == 0), stop=(k == KC - 1)),
                         reads=[bones, bZB], writes=[bPS[6]])
                for k in range(KC):
                    S.op("pe", lambda q, k=k: q.matmul(PS[7], ones, ZQ[:, k, :], start=(k == 0), stop=(k == KC - 1)),
                         reads=[bones, bZQ], writes=[bPS[7]])
                S.op("act", lambda q: q.copy(MEAN, PS[6]), reads=[bPS[6]], writes=[bST])
                S.op("dve", lambda q: q.tensor_tensor(NMR, MEAN, MEAN, ALU.mult), reads=[bST], writes=[bST])
                S.op("dve", lambda q: q.tensor_tensor(RSTD, PS[7], NMR, ALU.subtract), reads=[bPS[7], bST], writes=[bST])
                S.op("dve", lambda q: q.tensor_scalar_add(RSTD, RSTD, LN_EPS), reads=[bST], writes=[bST])
                S.op("act", lambda q: q.sqrt(RSTD, RSTD), reads=[bST], writes=[bST])
                S.op("dve", lambda q: q.reciprocal(RSTD, RSTD), reads=[bST], writes=[bST])
                S.op("dve", lambda q: q.scalar_tensor_tensor(NMR, MEAN, -1.0, RSTD, ALU.mult, ALU.mult), reads=[bST], writes=[bST])
                for k in range(KC):
                    y = YT[k % 2]
                    by = bYT[k % 2]
                    S.op("dve", lambda q, k=k, ts=ts, y=y: q.tensor_tensor(y, X[:, k, ts], RSTD, ALU.mult),
                         reads=[bX[k][t], bST], writes=[by])
                    S.op("dve", lambda q, y=y: q.tensor_tensor(y, y, NMR, ALU.add), reads=[by, bST], writes=[by])
                    g = ln_col(layer, which, 0, k)
                    b = ln_col(layer, which, 1, k)
                    S.op("act", lambda q, k=k, ts=ts, y=y, g=g, b=b: q.activation(X[:, k, ts], y, AF.Identity, bias=b, scale=g),
                         reads=[by, bLNP], writes=[bX[k][t]])
                    S.op("act", lambda q, k=k, ts=ts, y=y, g=g, b=b: q.activation(Xb[:, k, ts], y, AF.Identity, bias=b, scale=g),
                         reads=[by, bLNP], writes=[bXb[k][t]])

        def add_into_x(i, t, ps_idx, first):
            ts = slice(t * TT, (t + 1) * TT)
            if first:
                S.op("dve", lambda q: q.scalar_tensor_tensor(X[:, i, ts], X[:, i, ts], ALPHA, PS[ps_idx], ALU.mult, ALU.add),
                     reads=[bPS[ps_idx], bX[i][t]], writes=[bX[i][t]])
            else:
                S.op("dve", lambda q: q.tensor_tensor(X[:, i, ts], X[:, i, ts], PS[ps_idx], ALU.add),
                     reads=[bPS[ps_idx], bX[i][t]], writes=[bX[i][t]])

        gu_cnt = [0]

        def gu_tile(s, jl, t):
            ts = slice(t * TT, (t + 1) * TT)
            par = gu_cnt[0] % 2
            gu_cnt[0] += 1
            pg = 2 * par
            pu = pg + 1
            for k in range(KC):
                S.op("pe", lambda q, k=k: q.matmul(PS[pg], WGU[s][:, k, 0:128], Xb[:, k, ts], start=(k == 0), stop=(k == KC - 1)),
                     reads=[bWGU[s], bXb[k][t]], writes=[bPS[pg]])
            for k in range(KC):
                S.op("pe", lambda q, k=k: q.matmul(PS[pu], WGU[s][:, k, 128:256], Xb[:, k, ts], start=(k == 0), stop=(k == KC - 1)),
                     reads=[bWGU[s], bXb[k][t]], writes=[bPS[pu]])
            sl = SIL[par]
            bsl = bSIL[par]
            S.op("act", lambda q: q.activation(sl, PS[pg], AF.Silu), reads=[bPS[pg]], writes=[bsl])
            S.op("dve", lambda q: q.scalar_tensor_tensor(HG[:, jl, ts], PS[pu], 0.5, sl, ALU.mult, ALU.mult),
                 reads=[bPS[pu], bsl], writes=[bHG[jl][t]])

        def ffn(layer, f):
            fi = layer * 2 + f
            which = 0 if f == 0 else 2
            jbase = 0
            ngrp = len(JGROUPS)
            for gi, gsz in enumerate(JGROUPS):
                jstart = 0
                if gi == 0:
                    held = [guS.take(fi * NJ + jbase + jl) for jl in range(TOUT)]
                    for t in range(NT):
                        for jl in range(TOUT):
                            gu_tile(held[jl][0], jl, t)
                    for (_, gpos) in held:
                        guS.release(gpos)
                    jstart = TOUT
                if True:
                    for jl in range(jstart, gsz):
                        s, gpos = guS.take(fi * NJ + jbase + jl)
                        for t in range(NT):
                            gu_tile(s, jl, t)
                        guS.release(gpos)
                slots = [wdS.take(fi * NJ + jbase + jl) for jl in range(gsz)]
                for t in range(NT):
                    ts = slice(t * TT, (t + 1) * TT)
                    for i in range(KC):
                        po = 4 + (i % 2)
                        for jl in range(gsz):
                            sw = slots[jl][0]
                            S.op("pe", lambda q, i=i, jl=jl, ts=ts, sw=sw, po=po, gsz=gsz: q.matmul(PS[po], WD[sw][:, i * 128:(i + 1) * 128], HG[:, jl, ts], start=(jl == 0), stop=(jl == gsz - 1)),
                                 reads=[bWD[sw], bHG[jl][t]], writes=[bPS[po]])
                        add_into_x(i, t, po, gi == 0)
                    if gi == ngrp - 1:
                        ln_tile(layer, which, t)
                for (_, wpos) in slots:
                    wdS.release(wpos)
                jbase += gsz

        def pool_mixer(layer, sg):
            barrier()
            A.off = scratch_base
            W = HALO + TOK
            UF = A.f32(W)
            T1 = A.f32(W)
            T2 = A.f32(W)
            TMP = A.f32(HALO)
            Pm = A.bf16(2 * TOK).rearrange("p (c t) -> p c t", c=2)
            Y1 = A.bf16(2 * TOK).rearrange("p (c t) -> p c t", c=2)
            WGS = A.bf16(2048)
            WGS4 = WGS.rearrange("p (m i f) -> p m i f", m=4, i=2)
            PMI = A.f32(72)
            bUF, bT1, bT2, bTMP, bWGS, bPMI = Buf(), Buf(), Buf(), Buf(), Buf(), Buf()
            bPm = [[Buf() for _ in range(NT)] for _ in range(2)]
            bY1 = [[Buf() for _ in range(NT)] for _ in range(2)]
            S.dma("pool", WGS, pwg, writes=[bWGS])
            S.dma("sp", PMI, pmisc, writes=[bPMI])
            for m in range(4):
                s, gpos = guS.take(GU_MIX[layer] + m)
                w = 2 ** (m + 1)
                for oc in range(2):
                    c = 2 * m + oc
                    if sg == 0:
                        S.op("dve", lambda q: q.memset(UF[:, 0:HALO], 0.0), writes=[bUF])
                    else:
                        S.dma("sp", UF[:, 0:HALO], uctx[c], reads=[db("u%d" % c)], writes=[bUF])
                    for t in range(NT):
                        ts = slice(t * TT, (t + 1) * TT)
                        p = t % 4
                        for k in range(KC):
                            S.op("pe", lambda q, k=k, ts=ts, s=s, p=p, oc=oc: q.matmul(PS[p], WGU[s][:, k, oc * 128:(oc + 1) * 128], Xb[:, k, ts], start=(k == 0), stop=(k == KC - 1)),
                                 reads=[bWGU[s], bXb[k][t]], writes=[bPS[p]])
                        S.op("act", lambda q, t=t, p=p: q.copy(UF[:, HALO + t * TT:HALO + (t + 1) * TT], PS[p]), reads=[bPS[p]], writes=[bUF])
                    if sg == 0:
                        S.dma("sp", uctx[c], UF[:, TOK:TOK + HALO], reads=[bUF], writes=[db("u%d" % c)])
                    cur, bcur = UF, bUF
                    tmps = [(T1, bT1), (T2, bT2)]
                    for i in range(m + 1):
                        sh = 1 << i
                        lo = (2 << i) - 1
                        nxt, bn = tmps[i % 2]
                        S.op("dve", lambda q, nxt=nxt, cur=cur, lo=lo, sh=sh: q.tensor_tensor(nxt[:, lo:W], cur[:, lo:W], cur[:, lo - sh:W - sh], ALU.add),
                             reads=[bcur], writes=[bn])
                        cur, bcur = nxt, bn
                    S.op("dve", lambda q, cur=cur, oc=oc, w=w: q.scalar_tensor_tensor(Pm[:, oc, :], cur[:, HALO:W], 1.0 / w, UF[:, HALO:W], ALU.mult, ALU.subtract),
                         reads=[bcur, bUF], writes=bPm[oc])
                    if sg == 0:
                        S.op("dve", lambda q, cur=cur, m=m: q.tensor_tensor(TMP, cur[:, HALO:2 * HALO], PMI[:, m * HALO:(m + 1) * HALO], ALU.mult),
                             reads=[bcur, bPMI], writes=[bTMP])
                        S.op("dve", lambda q, oc=oc: q.tensor_tensor(Pm[:, oc, 0:HALO], TMP, UF[:, HALO:2 * HALO], ALU.subtract),
                             reads=[bTMP, bUF], writes=[bPm[oc][0]])
                guS.release(gpos)
                for oc in range(2):
                    c = 2 * m + oc
                    for t in range(NT):
                        ts = slice(t * TT, (t + 1) * TT)
                        p = (2 * oc + t) % 4
                        for ic in range(2):
                            S.op("pe", lambda q, ic=ic, oc=oc, ts=ts, p=p, m=m: q.matmul(PS[p], WGS4[:, m, ic, oc * 128:(oc + 1) * 128], Pm[:, ic, ts], start=(ic == 0), stop=(ic == 1)),
                                 reads=[bWGS, bPm[ic][t]], writes=[bPS[p]])
                        S.op("act", lambda q, oc=oc, ts=ts, p=p, c=c: q.activation(Y1[:, oc, ts], PS[p], AF.Identity, scale=PMI[:, 64 + c:65 + c]),
                             reads=[bPS[p], bPMI], writes=[bY1[oc][t]])
                for oc in range(2):
                    c = 2 * m + oc
                    sw, wpos = wdS.take(WD_MIX[layer] + c)
                    for t in range(NT):
                        ts = slice(t * TT, (t + 1) * TT)
                        for i in range(KC):
                            po = 4 + (i % 2)
                            S.op("pe", lambda q, i=i, oc=oc, ts=ts, sw=sw, po=po: q.matmul(PS[po], WD[sw][:, i * 128:(i + 1) * 128], Y1[:, oc, ts], start=True, stop=True),
                                 reads=[bWD[sw], bY1[oc][t]], writes=[bPS[po]])
                            add_into_x(i, t, po, c == 0)
                    wdS.release(wpos)
            barrier()

        def moba_mixer(layer, sg):
            barrier()
            lj = layer // 3
            ktctx, vctx, kmctx = ktctx_all[lj], vctx_all[lj], kmctx_all[lj]
            A.off = scratch_base
            QH = A.bf16(TOK)
            QL = A.bf16(TOK)
            QB = A.bf16(TOK)
            KMH = A.bf16(16)
            KML = A.bf16(16)
            KTA = A.bf16(2 * TOK)
            VP = A.bf16(32 * 256).rearrange("p (a f) -> p a f", a=32)
            KMA = A.f32(16)
            SELT = A.bf16(TOK)
            BTP = A.bf16(5 * TT).rearrange("p (r t) -> p r t", r=5)
            SEL = A.bf16(16 * 128)
            MK = A.f32(512)
            CJ = A.f32(256)
            JB = A.bf16(128)
            IDB = A.bf16(128)
            ONEB = A.bf16(128)
            RB31 = A.f32(8)
            GMA = A.f32(256)
            G2 = A.f32(256)
            GE = A.f32(256)
            MX = A.f32(16)
            SELBA = A.bf16(256)
            base0 = A.off
            if sg == 0:
                RBX = A.f32(8)
                RBH = A.bf16(8)
                RBL = A.bf16(8)
                OH = A.f32(NDELTA)
                OHB = A.bf16(NDELTA)
                TVS = A.bf16(NDELTA)
                assert A.off <= A.words, A.off
            A.off = base0
            NPT = 4
            PT = [A.bf16(TT) for _ in range(NPT)]
            OT = A.bf16(TT)
            REC = A.f32(TT)
            SUMP = A.f32(TT)
            SH = A.bf16(TT)
            SL = A.bf16(TT)
            assert A.off <= A.words, A.off
            bQ, bKT, bVP, bKM, bSELT, bBTP, bOT, bREC, bK, bGM, bSB = Buf(), Buf(), Buf(), Buf(), Buf(), Buf(), Buf(), Buf(), Buf(), Buf(), Buf()
            bPT = [Buf() for _ in range(NPT)]
            bSUM = Buf()
            PMASK = MK[:, 0:256].rearrange("p (g n) -> p g n", g=16)
            NOTOWN = MK[:, 256:512].rearrange("p (g n) -> p g n", g=16)
            S.dma("pool", SEL[0:16, :], asel, writes=[bK])
            S.dma("sp", MK, amask, writes=[bK])
            S.dma("sp", CJ, aconst, writes=[bK])
            S.dma("sp", RB31, arb31, writes=[bK])
            S.op("act", lambda q: q.copy(JB, CJ[:, 0:128]), reads=[bK], writes=[bK])
            S.op("act", lambda q: q.copy(IDB, CJ[:, 128:256]), reads=[bK], writes=[bK])
            S.op("dve", lambda q: q.memset(ONEB, 1.0), writes=[bK])
            if sg == 0:
                S.dma("sp", RBX[0:33, :], arbx, writes=[bK])
                S.dma("sp", OH[0:33, :], aoh, writes=[bK])
                S.op("act", lambda q: q.copy(RBH[0:33, :], RBX[0:33, :]), reads=[bK], writes=[bK])
                S.op("dve", lambda q: q.tensor_tensor(RBL[0:33, :], RBX[0:33, :], RBH[0:33, :], ALU.subtract), reads=[bK], writes=[bK])
                S.op("act", lambda q: q.copy(OHB[0:33, :], OH[0:33, :]), reads=[bK], writes=[bK])
                for c3 in range(3):
                    S.op("pe", lambda q, c3=c3: q.matmul(PS[c3][0:8, 0:384], RBH[0:33, 0:8], OHB[0:33, c3 * 384:(c3 + 1) * 384], start=True, stop=False),
                         reads=[bK], writes=[bPS[c3]])
                    S.op("pe", lambda q, c3=c3: q.matmul(PS[c3][0:8, 0:384], RBL[0:33, 0:8], OHB[0:33, c3 * 384:(c3 + 1) * 384], start=False, stop=True),
                         reads=[bK], writes=[bPS[c3]])
                    S.op("act", lambda q, c3=c3: q.copy(TVS[0:8, c3 * 384:(c3 + 1) * 384], PS[c3][0:8, 0:384]), reads=[bPS[c3]], writes=[bK])
                S.dma("sp", tvd, TVS[0:8, :], reads=[bK], writes=[db("tvd")])
                barrier()
            for hp in range(4):
                sv, pv = guS.take(GU_MIX[layer] + 3 * hp)
                if sg == 1:
                    S.dma("sp", VP[:, 0:16, :], vctx[hp].rearrange("p (a f) -> p a f", a=16), reads=[db("L%dvc%d" % (lj, hp))], writes=[bVP])
                for tt in range(16):
                    p = 6 + (tt % 2)
                    for k in range(KC):
                        S.op("pe", lambda q, tt=tt, k=k, p=p, sv=sv: q.matmul(PS[p][:, 0:256], Xb[:, k, tt * 128:(tt + 1) * 128], WGU[sv][:, k, :], start=(k == 0), stop=(k == KC - 1)),
                             reads=[bWGU[sv], bXb[k][tt // 4]], writes=[bPS[p]])
                    S.op("act", lambda q, tt=tt, p=p: q.copy(VP[:, sg * 16 + tt, :], PS[p][:, 0:256]), reads=[bPS[p]], writes=[bVP])
                guS.release(pv)
                if sg == 0:
                    S.dma("sp", vctx[hp].rearrange("p (a f) -> p a f", a=16), VP[:, 0:16, :], reads=[bVP], writes=[db("L%dvc%d" % (lj, hp))])
                for hh in range(2):
                    h = 2 * hp + hh
                    sa, pa = guS.take(GU_MIX[layer] + 3 * hp + 1 + hh)
                    sw, wpos = wdS.take(WD_MIX[layer] + h)
                    S.op("dve", lambda q: q.memset(KMA, 0.0), writes=[bKM])
                    if sg == 1:
                        S.dma("sp", KTA[:, 0:TOK], ktctx[h], reads=[db("L%dkt%d" % (lj, h))], writes=[bKT])
                        S.dma("sp", KMA[:, 0:8], kmctx[:, h * 8:(h + 1) * 8], reads=[db("L%dkm%d" % (lj, h))], writes=[bKM])
                    for t in range(NT):
                        ts = slice(t * TT, (t + 1) * TT)
                        for k in range(KC):
                            S.op("pe", lambda q, k=k, ts=ts, sa=sa: q.matmul(PS[6], WGU[sa][:, k, 0:128], Xb[:, k, ts], start=(k == 0), stop=(k == KC - 1)),
                                 reads=[bWGU[sa], bXb[k][t]], writes=[bPS[6]])
                        S.op("act", lambda q, ts=ts: q.copy(QH[:, ts], PS[6]), reads=[bPS[6]], writes=[bQ])
                        S.op("dve", lambda q, ts=ts: q.tensor_tensor(QL[:, ts], PS[6], QH[:, ts], ALU.subtract), reads=[bPS[6], bQ], writes=[bQ])
                        S.op("act", lambda q, ts=ts: q.activation(QB[:, ts], PS[6], AF.Identity, scale=float(128 ** -0.5)), reads=[bPS[6]], writes=[bQ])
                        for k in range(KC):
                            S.op("pe", lambda q, k=k, ts=ts, sa=sa: q.matmul(PS[7], WGU[sa][:, k, 128:256], Xb[:, k, ts], start=(k == 0), stop=(k == KC - 1)),
                                 reads=[bWGU[sa], bXb[k][t]], writes=[bPS[7]])
                        S.op("act", lambda q, t=t: q.copy(KTA[:, sg * TOK + t * TT:sg * TOK + (t + 1) * TT], PS[7]), reads=[bPS[7]], writes=[bKT])
                        S.op("dve", lambda q, t=t: q.tensor_reduce(KMA[:, sg * 8 + 2 * t:sg * 8 + 2 * t + 2], PS[7].rearrange("p (b l) -> p b l", b=2), AX.X, ALU.add),
                             reads=[bPS[7]], writes=[bKM])
                    guS.release(pa)
                    if sg == 0:
                        S.dma("sp", ktctx[h], KTA[:, 0:TOK], reads=[bKT], writes=[db("L%dkt%d" % (lj, h))])
                        S.dma("sp", kmctx[:, h * 8:(h + 1) * 8], KMA[:, 0:8], reads=[bKM], writes=[db("L%dkm%d" % (lj, h))])
                    for r in range(5):
                        src = bass.AP(tvd.tensor, h * NDELTA + 512 - 128 * r, [[1, 128], [1, TT]])
                        S.dma("sp", BTP[:, r, :], src, reads=[db("tvd")], writes=[bBTP])
                    S.op("act", lambda q: q.copy(KMH, KMA), reads=[bKM], writes=[bKM])
                    S.op("dve", lambda q: q.tensor_tensor(KML, KMA, KMH, ALU.subtract), reads=[bKM], writes=[bKM])
                    for qt in range(16):
                        qs = slice(qt * 128, (qt + 1) * 128)
                        go = PS[6][:, qt * 16:(qt + 1) * 16]
                        S.op("pe", lambda q, qs=qs, go=go: q.matmul(go, QH[:, qs], KMH, start=True, stop=False), reads=[bQ, bKM], writes=[bPS[6]])
                        S.op("pe", lambda q, qs=qs, go=go: q.matmul(go, QH[:, qs], KML, start=False, stop=False), reads=[bQ, bKM], writes=[bPS[6]])
                        S.op("pe", lambda q, qs=qs, go=go: q.matmul(go, QL[:, qs], KMH, start=False, stop=True), reads=[bQ, bKM], writes=[bPS[6]])
                    g4 = lambda a: a.rearrange("p (g two n) -> p g two n", g=8, two=2)
                    pm4 = PMASK[:, sg * 8:sg * 8 + 8, :].unsqueeze(2).to_broadcast([128, 8, 2, 16])
                    no4 = NOTOWN[:, sg * 8:sg * 8 + 8, :].unsqueeze(2).to_broadcast([128, 8, 2, 16])
                    g3 = lambda a: a.rearrange("p (t n) -> p t n", t=16)
                    bc = lambda a: a.unsqueeze(2).to_broadcast([128, 16, 16])
                    S.op("dve", lambda q: q.tensor_tensor(g4(GMA), g4(PS[6][:, 0:256]), pm4, ALU.add), reads=[bPS[6], bK], writes=[bGM])
                    S.op("dve", lambda q: q.tensor_reduce(MX, g3(GMA), AX.X, ALU.max), reads=[bGM], writes=[bSB])
                    S.op("dve", lambda q: q.tensor_tensor(g3(GE), g3(GMA), bc(MX), ALU.is_ge), reads=[bGM, bSB], writes=[bSB])
                    S.op("dve", lambda q: q.scalar_tensor_tensor(G2, GE, -1e30, GMA, ALU.mult, ALU.add), reads=[bGM, bSB], writes=[bSB])
                    S.op("dve", lambda q: q.tensor_reduce(MX, g3(G2), AX.X, ALU.max), reads=[bSB], writes=[bSB])
                    S.op("dve", lambda q: q.tensor_tensor(g3(GE), g3(G2), bc(MX), ALU.is_ge), reads=[bSB], writes=[bSB])
                    S.op("dve", lambda q: q.scalar_tensor_tensor(G2, GE, -1e30, G2, ALU.mult, ALU.add), reads=[bSB], writes=[bSB])
                    S.op("dve", lambda q: q.tensor_reduce(MX, g3(G2), AX.X, ALU.max), reads=[bSB], writes=[bSB])
                    S.op("dve", lambda q: q.tensor_scalar_max(MX, MX, -1e29), reads=[bSB], writes=[bSB])
                    S.op("dve", lambda q: q.tensor_tensor(g3(GE), g3(GMA), bc(MX), ALU.is_ge), reads=[bGM, bSB], writes=[bSB])
                    S.op("dve", lambda q: q.tensor_scalar(GE, GE, 30000.0, -30000.0, ALU.mult, ALU.add), reads=[bSB], writes=[bSB])
                    S.op("dve", lambda q: q.tensor_tensor(g4(SELBA), g4(GE), no4, ALU.mult), reads=[bSB, bK], writes=[bSB])
                    for qt in range(16):
                        S.op("pe", lambda q, qt=qt: q.matmul(PS[7][0:16, (qt % 4) * 128:(qt % 4 + 1) * 128], SELBA[:, qt * 16:(qt + 1) * 16], IDB, start=True, stop=True),
                             reads=[bSB, bK], writes=[bPS[7]])
                        if qt % 4 == 3:
                            T = qt // 4
                            S.op("act", lambda q, T=T: q.copy(SELT[0:16, T * TT:(T + 1) * TT], PS[7][0:16, :]), reads=[bPS[7]], writes=[bSELT])
                    pending = [None, None]
                    for T in range(NT):
                        gT = sg * 4 + T
                        ts = slice(T * TT, (T + 1) * TT)
                        nch = 4 * gT + 4
                        psb = [0, 1, 6, 7]

                        def stage1(c, ts=ts, gT=gT, h=h):
                            n = c // 2
                            r = c - 4 * gT
                            near = r >= -1
                            ps = psb[c % NPT]
                            pt = PT[c % NPT]
                            bpt = bPT[c % NPT]
                            S.op("pe", lambda q: q.matmul(PS[ps], KTA[:, c * 128:(c + 1) * 128], QB[:, ts], start=True, stop=False),
                                 reads=[bKT, bQ], writes=[bPS[ps]])
                            if r < 2:
                                S.op("pe", lambda q: q.matmul(PS[ps], SEL[0:16, n * 128:(n + 1) * 128], SELT[0:16, ts], start=False, stop=(not near)),
                                     reads=[bK, bSELT], writes=[bPS[ps]])
                            if near:
                                S.op("pe", lambda q: q.matmul(PS[ps], JB, BTP[:, r + 1, :], start=False, stop=True),
                                     reads=[bK, bBTP], writes=[bPS[ps]])
                                S.op("act", lambda q: q.activation(pt, PS[ps], AF.Exp), reads=[bPS[ps]], writes=[bpt])
                            else:
                                S.op("act", lambda q, h=h: q.activation(pt, PS[ps], AF.Exp, bias=RB31[:, h:h + 1]), reads=[bPS[ps], bK], writes=[bpt])

                        def stage2(c, hh=hh, nch=nch):
                            pt = PT[c % NPT]
                            bpt = bPT[c % NPT]
                            S.op("pe", lambda q: q.matmul(PS[2], VP[:, c, hh * 128:(hh + 1) * 128], pt, start=(c == 0), stop=(c == nch - 1)),
                                 reads=[bVP, bpt], writes=[bPS[2]])
                            if c == 0:
                                S.op("dve", lambda q: q.tensor_copy(SUMP, pt), reads=[bpt], writes=[bSUM])
                            else:
                                S.op("dve", lambda q: q.tensor_tensor(SUMP, SUMP, pt, ALU.add), reads=[bpt, bSUM], writes=[bSUM])

                        LOOK = 3
                        def tail_b():
                            S.op("pe", lambda q: q.matmul(PS[3], ONEB, SH, start=True, stop=False), reads=[bK, bREC], writes=[bPS[3]])
                            S.op("pe", lambda q: q.matmul(PS[3], ONEB, SL, start=False, stop=True), reads=[bK, bREC], writes=[bPS[3]])
                            S.op("dve", lambda q: q.reciprocal(REC, PS[3]), reads=[bPS[3]], writes=[bREC])
                            S.op("dve", lambda q: q.tensor_tensor(OT, PS[2], REC, ALU.mult), reads=[bPS[2], bREC], writes=[bOT])

                        def tail_d(T=T, sw=sw, h=h):
                            for i in range(KC):
                                po = 4 + (i % 2)
                                S.op("pe", lambda q, i=i, po=po: q.matmul(PS[po], WD[sw][:, i * 128:(i + 1) * 128], OT, start=True, stop=True),
                                     reads=[bWD[sw], bOT], writes=[bPS[po]])
                                add_into_x(i, T, po, h == 0)

                        cb = 1
                        cd = min(10, nch + LOOK - 1)
                        for c in range(nch + LOOK):
                            if c < nch:
                                stage1(c)
                            if c == cb and pending[0] is not None:
                                pending[0]()
                                pending[0] = None
                            if c == cd and pending[1] is not None:
                                pending[1]()
                                pending[1] = None
                            if c >= LOOK:
                                stage2(c - LOOK)
                        S.op("act", lambda q: q.copy(SH, SUMP), reads=[bSUM], writes=[bREC])
                        S.op("dve", lambda q: q.tensor_tensor(SL, SUMP, SH, ALU.subtract), reads=[bSUM, bREC], writes=[bREC])
                        pending[0] = tail_b
                        pending[1] = tail_d
                    pending[0]()
                    pending[1]()
                    wdS.release(wpos)
            barrier()

        def mlstm_mixer(layer, sg):
            barrier()
            A.off = scratch_base
            QT = A.bf16(2 * TOK).rearrange("p (c t) -> p c t", c=2)
            KT = A.bf16(2 * TOK).rearrange("p (c t) -> p c t", c=2)
            KM = A.bf16(16 * 256).rearrange("p (a f) -> p a f", a=16)
            VA = A.bf16(16 * 384).rearrange("p (a f) -> p a f", a=16)
            EBI = A.bf16(TOK)
            CF = A.f32(768).rearrange("p (c f) -> p c f", c=2)
            CB = A.bf16(768).rearrange("p (c f) -> p c f", c=2)
            WFR = A.bf16(KC * 128).rearrange("p (k f) -> p k f", k=KC)
            CGW = A.bf16(64).rearrange("p (k f) -> p k f", k=KC)
            CM = A.f32(80)
            CC = A.f32(384)
            IDB = A.bf16(128)
            O256 = A.bf16(128)
            GT = A.f32(128).rearrange("p (a g) -> p a g", a=16)
            LOGF = A.f32(64).rearrange("p (a g) -> p a g", a=16)
            BT_ = A.f32(64).rearrange("p (a g) -> p a g", a=16)
            EV = A.f32(64).rearrange("p (a g) -> p a g", a=16)
            NBG = A.f32(8)
            DEC = A.f32(32)
            TRIB = A.bf16(128)
            LHI = A.bf16(64)
            LLO = A.bf16(64)
            rbase = A.off
            PRE = A.f32(TOK + 4)
            ACC = A.f32(TOK)
            rend = A.off
            A.off = rbase
            WTS = [A.bf16(128), A.bf16(128)]
            DEN = A.f32(TT)
            HC = A.f32(2 * TT).rearrange("p (c t) -> p c t", c=2)
            HCB = A.bf16(2 * TT).rearrange("p (c t) -> p c t", c=2)
            HCQ = A.bf16(2 * TT).rearrange("p (c t) -> p c t", c=2)
            HN = A.bf16(2 * TT).rearrange("p (c t) -> p c t", c=2)
            SG = A.bf16(2 * TT).rearrange("p (c t) -> p c t", c=2)
            A.off = max(A.off, rend)
            assert A.off <= A.words, A.off
            bQT, bKT, bKM, bVA, bEBI, bCF, bCB, bWFR, bK = Buf(), Buf(), Buf(), Buf(), Buf(), Buf(), Buf(), Buf(), Buf()
            bG, bR1, bR2, bWT0, bDEN, bHC, bHCB, bHCQ, bHN, bSG, bDEC = Buf(), Buf(), Buf(), Buf(), Buf(), Buf(), Buf(), Buf(), Buf(), Buf(), Buf()
            bWT1 = Buf()
            TRI = CC[:, 0:128]
            CAUS = CC[:, 128:256]
            S.dma("pool", CGW, cgw.rearrange("p (k f) -> p k f", k=KC), writes=[bK])
            S.dma("sp", CM, cmisc, writes=[bK])
            S.dma("sp", CC, cconst, writes=[bK])
            S.op("act", lambda q: q.copy(IDB, CC[:, 256:384]), reads=[bK], writes=[bK])
            S.op("dve", lambda q: q.memset(O256, 1.0 / 256.0), writes=[bK])
            S.op("dve", lambda q: q.tensor_scalar_mul(NBG, CM[:, 72:80], -1.0), reads=[bK], writes=[bK])
            for tt in range(16):
                for k in range(KC):
                    S.op("pe", lambda q, tt=tt, k=k: q.matmul(PS[7][:, tt * 8:(tt + 1) * 8], Xb[:, k, tt * 128:(tt + 1) * 128], CGW[:, k, :], start=(k == 0), stop=(k == KC - 1)),
                         reads=[bK] + [bXb[k][tt // 4]], writes=[bPS[7]])
            S.op("dve", lambda q: q.tensor_tensor(GT, PS[7][:, 0:128].rearrange("p (a g) -> p a g", a=16), CM[:, 72:80].unsqueeze(1).to_broadcast([128, 16, 8]), ALU.add),
                 reads=[bPS[7], bK], writes=[bG])
            S.op("act", lambda q: q.activation(LOGF, GT[:, :, 4:8], AF.Exp, scale=-1.0), reads=[bG], writes=[bG])
            S.op("dve", lambda q: q.tensor_scalar_add(LOGF, LOGF, 1.0), reads=[bG], writes=[bG])
            S.op("act", lambda q: q.activation(LOGF, LOGF, AF.Ln), reads=[bG], writes=[bG])
            S.op("act", lambda q: q.copy(TRIB, TRI), reads=[bK], writes=[bK])
            S.op("act", lambda q: q.copy(LHI, LOGF.rearrange("p a g -> p (a g)")), reads=[bG], writes=[bG])
            S.op("dve", lambda q: q.tensor_tensor(LLO, LOGF.rearrange("p a g -> p (a g)"), LHI, ALU.subtract), reads=[bG], writes=[bG])
            S.op("pe", lambda q: q.matmul(PS[7][:, 128:192], TRIB, LHI, start=True, stop=False), reads=[bK, bG], writes=[bPS[7]])
            S.op("pe", lambda q: q.matmul(PS[7][:, 128:192], TRIB, LLO, start=False, stop=True), reads=[bK, bG], writes=[bPS[7]])
            S.op("dve", lambda q: q.tensor_tensor(BT_, GT[:, :, 0:4], PS[7][:, 128:192].rearrange("p (a g) -> p a g", a=16), ALU.add),
                 reads=[bPS[7], bG], writes=[bG])
            S.op("dve", lambda q: q.tensor_scalar_add(BT_, BT_, -math.log(16.0)), reads=[bG], writes=[bG])
            S.op("act", lambda q: q.activation(EV, BT_, AF.Exp), reads=[bG], writes=[bG])

            def conv_silu(h, isk, cc, slot, OUTT, bOUT):
                ci = (h * 2 + isk) * 2 + cc
                cw = (isk * 8 + h * 2 + cc) * 4
                if sg == 0:
                    S.op("dve", lambda q: q.memset(PRE[:, 0:4], 0.0), writes=[bR1])
                else:
                    S.dma("sp", PRE[:, 0:4], cctx[ci], reads=[db("cc%d" % ci)], writes=[bR1])
                for t in range(NT):
                    ts = slice(t * TT, (t + 1) * TT)
                    p = t % 3
                    for k in range(KC):
                        S.op("pe", lambda q, k=k, ts=ts, p=p: q.matmul(PS[p], WGU[slot][:, k, cc * 128:(cc + 1) * 128], Xb[:, k, ts], start=(k == 0), stop=(k == KC - 1)),
                             reads=[bWGU[slot], bXb[k][t]], writes=[bPS[p]])
                    S.op("act", lambda q, t=t, p=p: q.copy(PRE[:, 4 + t * TT:4 + (t + 1) * TT], PS[p]), reads=[bPS[p]], writes=[bR1])
                if sg == 0:
                    S.dma("sp", cctx[ci], PRE[:, TOK:TOK + 4], reads=[bR1], writes=[db("cc%d" % ci)])
                S.op("dve", lambda q: q.tensor_scalar_mul(ACC, PRE[:, 1:1 + TOK], CM[:, cw:cw + 1]), reads=[bR1, bK], writes=[bR2])
                for j in range(1, 4):
                    S.op("dve", lambda q, j=j: q.scalar_tensor_tensor(ACC, PRE[:, 1 + j:1 + j + TOK], CM[:, cw + j:cw + j + 1], ACC, ALU.mult, ALU.add),
                         reads=[bR1, bR2, bK], writes=[bR2])
                S.op("act", lambda q: q.activation(OUTT[:, cc, :], ACC, AF.Silu), reads=[bR2], writes=[bOUT])

            for h in range(4):
                sq, pq = guS.take(GU_MIX[layer] + 4 * h + 0)
                sk, pk = guS.take(GU_MIX[layer] + 4 * h + 1)
                sv, pv = guS.take(GU_MIX[layer] + 4 * h + 2)
                so, po_ = guS.take(GU_MIX[layer] + 4 * h + 3)
                w0, wp0 = wdS.take(WD_MIX[layer] + 2 * h)
                w1, wp1 = wdS.take(WD_MIX[layer] + 2 * h + 1)
                wsl = [w0, w1]
                S.alias_after([bR1, bR2], [bWT0, bWT1, bDEN, bHC, bHCB, bHCQ, bHN, bSG])
                S.dma("pool", WFR, cfrep[h].rearrange("p (k f) -> p k f", k=KC), writes=[bWFR])
                for t in range(NT):
                    ts = slice(t * TT, (t + 1) * TT)
                    p = t % 3
                    for k in range(KC):
                        S.op("pe", lambda q, k=k, ts=ts, p=p: q.matmul(PS[p], WFR[:, k, :], Xb[:, k, ts], start=(k == 0), stop=(k == KC - 1)),
                             reads=[bWFR, bXb[k][t]], writes=[bPS[p]])
                    S.op("act", lambda q, ts=ts, p=p, h=h: q.activation(PRE[:, ts], PS[p], AF.Exp, bias=NBG[:, 4 + h:5 + h], scale=-1.0),
                         reads=[bPS[p], bK], writes=[bR1])
                S.op("dve", lambda q: q.tensor_scalar_add(PRE[:, 0:TOK], PRE[:, 0:TOK], 1.0), reads=[bR1], writes=[bR1])
                S.op("act", lambda q: q.activation(PRE[:, 0:TOK], PRE[:, 0:TOK], AF.Ln), reads=[bR1], writes=[bR1])
                cur, bcur, nxt, bnxt = PRE[:, 0:TOK], bR1, ACC, bR2
                for i in range(7):
                    sh = 1 << i
                    c3 = cur.rearrange("p (c l) -> p c l", l=128)
                    n3 = nxt.rearrange("p (c l) -> p c l", l=128)
                    S.op("dve", lambda q, c3=c3, n3=n3, sh=sh: q.tensor_tensor(n3[:, :, sh:128], c3[:, :, sh:128], c3[:, :, 0:128 - sh], ALU.add),
                         reads=[bcur], writes=[bnxt])
                    S.op("act", lambda q, c3=c3, n3=n3, sh=sh: q.copy(n3[:, :, 0:sh], c3[:, :, 0:sh]), reads=[bcur], writes=[bnxt])
                    cur, bcur, nxt, bnxt = nxt, bnxt, cur, bcur
                S.op("act", lambda q, cur=cur: q.activation(EBI, cur, AF.Exp), reads=[bcur], writes=[bEBI])
                S.op("act", lambda q, cur=cur: q.activation(DEC[:, 0:16], cur.rearrange("p (c l) -> p c l", l=128)[:, :, 127], AF.Exp, scale=-1.0),
                     reads=[bcur], writes=[bDEC])
                for cc in range(2):
                    conv_silu(h, 0, cc, sq, QT, bQT)
                guS.release(pq)
                for cc in range(2):
                    conv_silu(h, 1, cc, sk, KT, bKT)
                guS.release(pk)
                for tt in range(16):
                    p = 3 + (tt % 2)
                    for cc in range(2):
                        S.op("pe", lambda q, tt=tt, cc=cc, p=p: q.matmul(PS[p][:, cc * 128:(cc + 1) * 128], KT[:, cc, tt * 128:(tt + 1) * 128], IDB, start=True, stop=True),
                             reads=[bKT, bK], writes=[bPS[p]])
                    S.op("act", lambda q, tt=tt, p=p: q.copy(KM[:, tt, :], PS[p][:, 0:256]), reads=[bPS[p]], writes=[bKM])
                for tt in range(16):
                    p = 5 + (tt % 2)
                    for k in range(KC):
                        S.op("pe", lambda q, tt=tt, k=k, p=p, sv=sv: q.matmul(PS[p][:, 0:256], Xb[:, k, tt * 128:(tt + 1) * 128], WGU[sv][:, k, :], start=(k == 0), stop=(k == KC - 1)),
                             reads=[bWGU[sv], bXb[k][tt // 4]], writes=[bPS[p]])
                    S.op("act", lambda q, tt=tt, p=p, h=h: q.activation(VA[:, tt, 0:256], PS[p][:, 0:256], AF.Identity, scale=EV[:, tt, h:h + 1]),
                         reads=[bPS[p], bG], writes=[bVA])
                    S.op("dve", lambda q, tt=tt, h=h: q.tensor_scalar(VA[:, tt, 256:384], ones, EV[:, tt, h:h + 1], 1024.0, ALU.mult, ALU.mult),
                         reads=[bones, bG], writes=[bVA])
                guS.release(pv)
                if sg == 0:
                    S.op("dve", lambda q: q.memset(CF, 0.0), writes=[bCF])
                else:
                    S.dma("sp", CF, sctx[h].rearrange("p (c f) -> p c f", c=2), reads=[db("sc%d" % h)], writes=[bCF])
                S.op("act", lambda q: q.copy(CB, CF), reads=[bCF], writes=[bCB])
                bWTS = [bWT0, bWT1]
                stage_bufs = [bWT0, bWT1, bDEN, bHC, bHCB, bHCQ, bHN, bSG]
                S.alias_after(stage_bufs, [bR1, bR2])

                def st_stage(tt):
                    WT = WTS[tt % 2]
                    for kc in range(2):
                        S.op("pe", lambda q, kc=kc: q.matmul(PS[3][:, 0:128], KT[:, kc, tt * 128:(tt + 1) * 128], QT[:, kc, tt * 128:(tt + 1) * 128], start=(kc == 0), stop=(kc == 1)),
                             reads=[bKT, bQT], writes=[bPS[3]])
                    S.op("dve", lambda q: q.tensor_tensor(WT, PS[3][:, 0:128], CAUS, ALU.mult), reads=[bPS[3], bK], writes=[bWTS[tt % 2]])

                st_stage(0)
                for tt in range(16):
                    T4 = tt // 4
                    col0 = (tt % 4) * 128
                    WT = WTS[tt % 2]
                    bWT = bWTS[tt % 2]
                    for vc in range(3):
                        S.op("pe", lambda q, tt=tt, vc=vc, col0=col0, WT=WT: q.matmul(PS[vc][:, col0:col0 + 128], VA[:, tt, vc * 128:(vc + 1) * 128], WT, start=True, stop=False),
                             reads=[bVA, bWT], writes=[bPS[vc]])
                    if tt + 1 < 16:
                        st_stage(tt + 1)
                    for vc in range(3):
                        for kc in range(2):
                            S.op("pe", lambda q, tt=tt, vc=vc, kc=kc, col0=col0: q.matmul(PS[vc][:, col0:col0 + 128], CB[:, kc, vc * 128:(vc + 1) * 128], QT[:, kc, tt * 128:(tt + 1) * 128], start=False, stop=(kc == 1)),
                                 reads=[bCB, bQT], writes=[bPS[vc]])
                    S.op("dve", lambda q, tt=tt: q.tensor_scalar_mul(CF, CF, DEC[:, tt:tt + 1]), reads=[bCF, bDEC], writes=[bCF])
                    for kc in range(2):
                        S.op("pe", lambda q, tt=tt, kc=kc: q.matmul(PS[4 + kc][:, 0:384], KM[:, tt, kc * 128:(kc + 1) * 128], VA[:, tt, :], start=True, stop=True),
                             reads=[bKM, bVA], writes=[bPS[4 + kc]])
                    for kc in range(2):
                        S.op("dve", lambda q, kc=kc, tt=tt: q.scalar_tensor_tensor(CF[:, kc, :], PS[4 + kc][:, 0:384], DEC[:, tt:tt + 1], CF[:, kc, :], ALU.mult, ALU.add),
                             reads=[bPS[4 + kc], bCF, bDEC], writes=[bCF])
                    S.op("dve", lambda q: q.tensor_copy(CB, CF), reads=[bCF], writes=[bCB])
                    if tt % 4 != 3:
                        continue
                    ts = slice(T4 * TT, (T4 + 1) * TT)
                    S.op("act", lambda q: q.activation(DEN, PS[2], AF.Abs), reads=[bPS[2]], writes=[bDEN])
                    S.op("act", lambda q: q.copy(HC[:, 0, :], PS[0]), reads=[bPS[0]], writes=[bHC])
                    S.op("dve", lambda q: q.tensor_copy(HC[:, 1, :], PS[1]), reads=[bPS[1]], writes=[bHC])
                    S.op("dve", lambda q, ts=ts: q.tensor_tensor(DEN, DEN, EBI[:, ts], ALU.max), reads=[bDEN, bEBI], writes=[bDEN])
                    S.op("dve", lambda q: q.reciprocal(DEN, DEN), reads=[bDEN], writes=[bDEN])
                    for cc in range(2):
                        p = 6 + cc
                        for k in range(KC):
                            S.op("pe", lambda q, k=k, ts=ts, p=p, cc=cc, so=so: q.matmul(PS[p], WGU[so][:, k, cc * 128:(cc + 1) * 128], Xb[:, k, ts], start=(k == 0), stop=(k == KC - 1)),
                                 reads=[bWGU[so], bXb[k][T4]], writes=[bPS[p]])
                        S.op("act", lambda q, cc=cc, p=p: q.activation(SG[:, cc, :], PS[p], AF.Sigmoid), reads=[bPS[p]], writes=[bSG])
                    for vc in range(2):
                        S.op("dve", lambda q, vc=vc: q.tensor_tensor(HC[:, vc, :], HC[:, vc, :], DEN, ALU.mult), reads=[bHC, bDEN], writes=[bHC])
                        S.op("dve", lambda q, vc=vc: q.tensor_tensor(HC[:, vc, :], HC[:, vc, :], SG[:, vc, :], ALU.mult), reads=[bHC, bSG], writes=[bHC])
                    S.op("act", lambda q: q.copy(HCB, HC), reads=[bHC], writes=[bHCB])
                    S.op("act", lambda q: q.activation(HCQ, HC, AF.Square), reads=[bHC], writes=[bHCQ])
                    for vc in range(2):
                        S.op("pe", lambda q, vc=vc: q.matmul(PS[6], O256, HCB[:, vc, :], start=(vc == 0), stop=(vc == 1)), reads=[bK, bHCB], writes=[bPS[6]])
                    for vc in range(2):
                        S.op("pe", lambda q, vc=vc: q.matmul(PS[7], O256, HCQ[:, vc, :], start=(vc == 0), stop=(vc == 1)), reads=[bK, bHCQ], writes=[bPS[7]])
                    S.op("act", lambda q: q.copy(MEAN, PS[6]), reads=[bPS[6]], writes=[bST])
                    S.op("dve", lambda q: q.tensor_tensor(NMR, MEAN, MEAN, ALU.mult), reads=[bST], writes=[bST])
                    S.op("dve", lambda q: q.tensor_tensor(RSTD, PS[7], NMR, ALU.subtract), reads=[bPS[7], bST], writes=[bST])
                    S.op("dve", lambda q: q.tensor_scalar_add(RSTD, RSTD, LN_EPS), reads=[bST], writes=[bST])
                    S.op("act", lambda q: q.sqrt(RSTD, RSTD), reads=[bST], writes=[bST])
                    S.op("dve", lambda q: q.reciprocal(RSTD, RSTD), reads=[bST], writes=[bST])
                    S.op("dve", lambda q: q.scalar_tensor_tensor(NMR, MEAN, -1.0, RSTD, ALU.mult, ALU.mult), reads=[bST], writes=[bST])
                    for vc in range(2):
                        S.op("dve", lambda q, vc=vc: q.tensor_tensor(HC[:, vc, :], HC[:, vc, :], RSTD, ALU.mult), reads=[bHC, bST], writes=[bHC])
                        S.op("dve", lambda q, vc=vc: q.tensor_tensor(HC[:, vc, :], HC[:, vc, :], NMR, ALU.add), reads=[bHC, bST], writes=[bHC])
                        S.op("act", lambda q, vc=vc, h=h: q.activation(HN[:, vc, :], HC[:, vc, :], AF.Identity, scale=CM[:, 64 + 2 * h + vc:65 + 2 * h + vc]),
                             reads=[bHC, bK], writes=[bHN])
                    for i in range(KC):
                        po = 4 + (i % 2)
                        for vc in range(2):
                            S.op("pe", lambda q, i=i, vc=vc, po=po, wsl=wsl: q.matmul(PS[po], WD[wsl[vc]][:, i * 128:(i + 1) * 128], HN[:, vc, :], start=(vc == 0), stop=(vc == 1)),
                                 reads=[bWD[wsl[vc]], bHN], writes=[bPS[po]])
                        add_into_x(i, T4, po, h == 0)
                guS.release(po_)
                wdS.release(wp0)
                wdS.release(wp1)
                if sg == 0:
                    S.dma("sp", sctx[h].rearrange("p (c f) -> p c f", c=2), CF, reads=[bCF], writes=[db("sc%d" % h)])
            barrier()

        for sg in range(NSEG):
            for k in range(KC):
                S.dma("sp", X[:, k, :], xT[sg][:, k * TOK:(k + 1) * TOK], writes=bX[k])
            for k in range(KC):
                for t in range(NT):
                    S.op("act", lambda q, k=k, t=t: q.copy(Xb[:, k, t * TT:(t + 1) * TT], X[:, k, t * TT:(t + 1) * TT]),
                         reads=[bX[k][t]], writes=[bXb[k][t]])
            for (kind, layer, f) in phases:
                if kind == "ffn":
                    ffn(layer, f)
                else:
                    [moba_mixer, pool_mixer, mlstm_mixer][layer % 3](layer, sg)
                    layer_norm(layer, 1)
            for k in range(KC):
                S.dma("sp", yT[sg][:, k * TOK:(k + 1) * TOK], X[:, k, :], reads=bX[k], is_output=True)
        S.finish()
        with nc.Block() as block:
            S.emit(block)
    return nc


def _gu_layout(w):
    n = w.shape[1] // 256
    return np.ascontiguousarray(w.reshape(KC, 128, n, 256).transpose(2, 1, 0, 3)).reshape(n, 128, KC * 256)


def prep_inputs(inputs, phases):
    f32 = lambda a: np.asarray(a, dtype=np.float32)
    x = f32(inputs["x"])
    xs = []
    for b in range(BATCH):
        a = x[b].reshape(NSEG, TOK, KC, 128).transpose(0, 3, 2, 1)
        xs.append(np.ascontiguousarray(a).reshape(NSEG, 128, KC * TOK))
    wgu_in = f32(inputs["ffn_w_gu"])
    g = wgu_in[..., :DFF].reshape(DEPTH, 2, KC, 128, NJ, 128)
    u = wgu_in[..., DFF:].reshape(DEPTH, 2, KC, 128, NJ, 128)
    gu = np.stack([g, u], axis=5)
    gu = np.ascontiguousarray(gu.transpose(0, 1, 4, 3, 2, 5, 6)).reshape(DEPTH * 2 * NJ, 128, KC * 256)
    pieces = [gu]
    a_in, b_in, c_in = f32(inputs["a_w_in"]), f32(inputs["b_w_in"]), f32(inputs["c_w_in"])
    for layer in range(DEPTH):
        mk, j = layer % 3, layer // 3
        if mk == 0:
            w = a_in[j]
            q, k, v = w[:, 0:D], w[:, D:2 * D], w[:, 2 * D:3 * D]
            cols = []
            for hp in range(4):
                cols.append(v[:, hp * 256:(hp + 1) * 256])
                for h in (2 * hp, 2 * hp + 1):
                    cols.append(np.concatenate([q[:, h * 128:(h + 1) * 128], k[:, h * 128:(h + 1) * 128]], axis=1))
            pieces.append(_gu_layout(np.concatenate(cols, axis=1)))
        elif mk == 1:
            pieces.append(_gu_layout(b_in[j]))
        else:
            w = c_in[j]
            cols = []
            for h in range(4):
                for part in range(4):
                    cols.append(w[:, part * D + h * 256: part * D + (h + 1) * 256])
            pieces.append(_gu_layout(np.concatenate(cols, axis=1)))
    wgu = np.concatenate(pieces, axis=0)
    assert wgu.shape[0] == NP_GU, wgu.shape
    wds = [f32(inputs["ffn_w_down"]).reshape(DEPTH * 2 * NJ, 128, D)]
    for layer in range(DEPTH):
        mk, j = layer % 3, layer // 3
        wo = [inputs["a_w_out"], inputs["b_w_out"], inputs["c_w_out"]][mk]
        wds.append(f32(wo)[j].reshape(8, 128, D))
    wd = np.ascontiguousarray(np.concatenate(wds, axis=0))
    lg = f32(inputs["ln_g"]).reshape(DEPTH, 3, KC, 128)
    lb = f32(inputs["ln_b"]).reshape(DEPTH, 3, KC, 128)
    lnp = np.stack([lg, lb], axis=2)
    lnp = np.ascontiguousarray(lnp.transpose(4, 0, 1, 2, 3)).reshape(128, DEPTH * 3 * 2 * KC)
    wg = f32(inputs["b_w_group"])[0].reshape(4, 2, 128, 256).transpose(2, 0, 1, 3)
    pwg = np.ascontiguousarray(wg).reshape(128, 2048)
    invc = np.zeros((4, HALO), np.float32)
    for m in range(4):
        invc[m] = 1.0 / np.minimum(np.arange(HALO) + 1.0, float(2 ** (m + 1)))
    pmisc = np.zeros((128, 72), np.float32)
    pmisc[:, :64] = invc.reshape(1, 64)
    pmisc[:, 64:72] = f32(inputs["b_scale"])[0].reshape(8, 128).T
    cw_in = c_in[0]
    gw = cw_in[:, 4 * D:4 * D + 8]
    cgw = np.ascontiguousarray(gw.reshape(KC, 128, 8).transpose(1, 0, 2)).reshape(128, 64)
    cfrep = np.empty((4, 128, KC, 128), np.float32)
    for h in range(4):
        cfrep[h] = np.broadcast_to(gw[:, 4 + h].reshape(KC, 128).T[:, :, None], (128, KC, 128))
    cfrep = cfrep.reshape(4, 128, KC * 128)
    cmisc = np.zeros((128, 80), np.float32)
    cmisc[:, 0:64] = f32(inputs["c_conv_w"])[0].reshape(4, 16, 128).transpose(2, 1, 0).reshape(128, 64)
    cmisc[:, 64:72] = f32(inputs["c_norm_g"])[0].reshape(8, 128).T
    cmisc[:, 72:80] = np.broadcast_to(f32(inputs["c_b_gates"])[0].reshape(1, 8), (128, 8))
    ii = np.arange(128)
    tri = (ii[:, None] <= ii[None, :]).astype(np.float32)
    cconst = np.concatenate([tri, tri, np.eye(128, dtype=np.float32)], axis=1)
    rb = f32(inputs["rel_bias"])
    arbx = np.concatenate([rb, np.full((1, 8), -30000.0, np.float32)], axis=0)
    delta = np.arange(NDELTA) - 511
    nn = np.maximum(delta, 0)
    nf = np.maximum(nn, 1).astype(np.float32)
    large = 16 + (np.log(nf / np.float32(16)) / np.float32(math.log(128 / 16)) * np.float32(16)).astype(np.int32)
    large = np.minimum(large, 31)
    bucket = np.where(nn < 16, nn, large)
    aoh = np.zeros((33, NDELTA), np.float32)
    for i in range(NDELTA):
        if delta[i] >= 0:
            aoh[bucket[i], i] = 1.0
        else:
            aoh[32, i] = 1.0
    asel = np.zeros((16, 16, 128), np.float32)
    for n in range(16):
        asel[n, n, :] = 1.0
    asel = asel.reshape(16, 2048)
    gbi = np.arange(16)
    pm = np.where(gbi[None, :] < gbi[:, None], 0.0, -1e30).astype(np.float32)
    no = (1.0 - np.eye(16)).astype(np.float32)
    amask = np.ascontiguousarray(np.broadcast_to(np.concatenate([pm.reshape(1, 256), no.reshape(1, 256)], axis=1), (128, 512)))
    aconst = np.concatenate([np.eye(128, dtype=np.float32)[::-1], np.eye(128, dtype=np.float32)], axis=1)
    aconst = np.ascontiguousarray(aconst)
    arb31 = np.ascontiguousarray(np.broadcast_to(rb[31].reshape(1, 8), (128, 8)))
    gu_ids, wd_ids = used_pieces(phases)
    if len(gu_ids) != NP_GU:
        wgu = np.ascontiguousarray(wgu[gu_ids])
    if len(wd_ids) != NP_WD:
        wd = np.ascontiguousarray(wd[wd_ids])
    common = {"wgu": wgu, "wd": wd, "lnp": lnp, "pwg": pwg, "pmisc": pmisc,
              "asel": asel, "amask": amask, "aconst": aconst, "arb31": arb31, "arbx": arbx, "aoh": aoh,
              "cgw": cgw, "cfrep": cfrep, "cmisc": cmisc, "cconst": cconst}
    return [dict(common, xT=xs[c]) for c in range(NCORES)]


def run(inputs, phases=None):
    phases = ALL_PHASES if phases is None else phases
    nc = build_program(phases)
    in_maps = prep_inputs(inputs, phases)
    res = run_bass_kernel_spmd(nc, in_maps, core_ids=list(range(NCORES)))
    out = np.empty((BATCH, SEQ, D), dtype=np.float32)
    for b in range(NCORES):
        yT = np.asarray(res.results[b]["yT"]).reshape(NSEG, 128, KC, TOK)
        out[b] = yT.transpose(0, 3, 2, 1).reshape(SEQ, D)
    return out


def kernel(**inputs):
    return run(inputs)
```

```python
import math
from contextlib import ExitStack

import numpy as np
import concourse.bass as bass
import concourse.mybir as mybir
from concourse.bass_utils import run_bass_kernel_spmd

F32 = mybir.dt.float32
BF16 = mybir.dt.bfloat16
AF = mybir.ActivationFunctionType
ALU = mybir.AluOpType
AX = mybir.AxisListType

D = 1024
KC = 8
DFF = 2816
NJ = 22
DEPTH = 4
SEQ = 4096
BATCH = 4
TOK = 2048
TT = 512
NT = TOK // TT
ALPHA = (2 * DEPTH) ** 0.25
LN_EPS = 1e-5
JGROUPS = [8, 7, 7]
TOUT = 3
SEM_LIMIT = 30000


class Buf:
    __slots__ = ("w", "r", "name", "excl")

    def __init__(self, name="", excl=False):
        self.w = []
        self.r = []
        self.name = name
        self.excl = excl


class Eng:
    def __init__(self, name, same_sync):
        self.name = name
        self.prog = []
        self.sem = None
        self.count = 0
        self.waited = {}
        self.same_sync = same_sync
        self.ring = []
        self.ring_pos = 0


class Sched:
    def __init__(self, nc, stack):
        self.nc = nc
        self.stack = stack
        self.nsem = 0
        self.E = {
            "pe": Eng("pe", False),
            "act": Eng("act", True),
            "dve": Eng("dve", True),
            "pool": Eng("pool", True),
            "sp": Eng("sp", False),
        }
        for e in self.E.values():
            e.sem = self.new_sem(e.name)
        self.out_deps = []

    def new_sem(self, name):
        self.nsem += 1
        return self.stack.enter_context(self.nc.semaphore("s%d_%s" % (self.nsem, name)))

    def _ensure(self, e, dep):
        sem, val = dep
        if sem is e.sem and not e.same_sync:
            return
        key = id(sem)
        if e.waited.get(key, (None, 0))[1] >= val:
            return
        e.waited[key] = (sem, val)
        e.prog.append(("wait", sem, val))

    def _deps(self, e, reads, writes):
        for b in reads:
            for d in b.w:
                self._ensure(e, d)
            if b.excl:
                for d in b.r:
                    if d[0] is not e.sem:
                        self._ensure(e, d)
        for b in writes:
            for d in b.w:
                self._ensure(e, d)
            for d in b.r:
                self._ensure(e, d)

    def _commit(self, dep, reads, writes):
        for b in reads:
            for i, d in enumerate(b.r):
                if d[0] is dep[0]:
                    if d[1] < dep[1]:
                        b.r[i] = dep
                    break
            else:
                b.r.append(dep)
        for b in writes:
            b.w = [dep]
            b.r = []

    def op(self, eng, fn, reads=(), writes=()):
        e = self.E[eng]
        self._deps(e, reads, writes)
        if e.count >= SEM_LIMIT:
            e.sem = self.new_sem(e.name)
            e.count = 0
        e.count += 1
        dep = (e.sem, e.count)
        e.prog.append(("op", fn, e.sem, 1))
        self._commit(dep, reads, writes)
        return dep

    def dma(self, eng, out, in_, reads=(), writes=(), is_output=False, nring=8):
        e = self.E[eng]
        self._deps(e, reads, writes)
        if not e.ring:
            e.ring = [[self.new_sem(e.name + "d"), 0] for _ in range(nring)]
        slot = e.ring[e.ring_pos % len(e.ring)]
        e.ring_pos += 1
        if slot[1] > 0:
            self._ensure(e, (slot[0], slot[1]))
        slot[1] += 16
        dep = (slot[0], slot[1])
        e.prog.append(("op", lambda q, o=out, i=in_: q.dma_start(out=o, in_=i), slot[0], 16))
        self._commit(dep, reads, writes)
        if is_output:
            self.out_deps.append(dep)
        return dep

    def alias_after(self, dst, src):
        for d in dst:
            for s in src:
                for dep in list(s.w) + list(s.r):
                    for i, x in enumerate(d.r):
                        if x[0] is dep[0]:
                            if x[1] < dep[1]:
                                d.r[i] = dep
                            break
                    else:
                        d.r.append(dep)

    def finish(self):
        e = self.E["sp"]
        for d in self.out_deps:
            self._ensure(e, d)

    def emit(self, block):
        def run(prog):
            def f(q):
                for it in prog:
                    if it[0] == "wait":
                        q.wait_ge(it[1], it[2])
                    else:
                        it[1](q).then_inc(it[2], it[3])
            return f
        block.tensor(run(self.E["pe"].prog))
        block.scalar(run(self.E["act"].prog))
        block.vector(run(self.E["dve"].prog))
        block.gpsimd(run(self.E["pool"].prog))
        block.sync(run(self.E["sp"].prog))


class Arena:
    def __init__(self, nc, words):
        self.t = nc.alloc_sbuf_tensor("arena", [128, words], F32)
        self.words = words
        self.off = 0

    def f32(self, n):
        a = self.t[:, self.off:self.off + n]
        self.off += n
        assert self.off <= self.words, "SBUF arena overflow %d" % self.off
        return a

    def bf16(self, n):
        assert n % 2 == 0
        return self.f32(n // 2).bitcast(BF16)


NCORES = 4
NSEG = 2
HALO = 16
NDELTA = 1152


class Stream:
    def __init__(self, S, eng, dram, slots, bufs, order, view, idmap):
        self.S, self.eng, self.dram, self.slots, self.bufs = S, eng, dram, slots, bufs
        self.idmap = idmap
        self.order, self.view = order, view
        self.N = len(slots)
        self.issued = 0
        self.done = 0
        self.cursor = 0

    def prefetch(self):
        while self.issued < len(self.order) and self.issued < self.done + self.N:
            i = self.issued
            s = i % self.N
            self.S.dma(self.eng, self.slots[s], self.view(self.dram[self.idmap[self.order[i]]]), writes=[self.bufs[s]])
            self.issued += 1

    def take(self, piece):
        pos = self.cursor
        assert self.order[pos] == piece, (pos, self.order[pos], piece)
        self.cursor += 1
        self.prefetch()
        assert pos < self.issued
        return pos % self.N, pos

    def release(self, pos):
        assert pos == self.done, (pos, self.done)
        self.done = pos + 1
        self.prefetch()


def piece_tables():
    n = DEPTH * 2 * NJ
    gu_mix = {}
    for layer in range(DEPTH):
        gu_mix[layer] = n
        n += {0: 12, 1: 4, 2: 16}[layer % 3]
    m = DEPTH * 2 * NJ
    wd_mix = {}
    for layer in range(DEPTH):
        wd_mix[layer] = m
        m += 8
    return gu_mix, n, wd_mix, m


GU_MIX, NP_GU, WD_MIX, NP_WD = piece_tables()


def phase_pieces(ph):
    kind, layer, f = ph
    if kind == "ffn":
        ids = [(layer * 2 + f) * NJ + j for j in range(NJ)]
        return ids, list(ids)
    mk = layer % 3
    ngu = {0: 12, 1: 4, 2: 16}[mk]
    return [GU_MIX[layer] + i for i in range(ngu)], [WD_MIX[layer] + i for i in range(8)]


ALL_PHASES = []
for _l in range(DEPTH):
    ALL_PHASES += [("ffn", _l, 0), ("mix", _l, 0), ("ffn", _l, 1)]


def used_pieces(phases):
    gu, wdp = set(), set()
    for ph in phases:
        a, b = phase_pieces(ph)
        gu.update(a)
        wdp.update(b)
    return sorted(gu), sorted(wdp)


def build_program(phases):
    nc = bass.Bass("TRN2", target_bir_lowering=False)
    gu_ids, wd_ids = used_pieces(phases)
    gu_map = {p: i for i, p in enumerate(gu_ids)}
    wd_map = {p: i for i, p in enumerate(wd_ids)}
    xT = nc.dram_tensor("xT", [NSEG, 128, KC * TOK], F32, kind="ExternalInput").ap()
    wgu = nc.dram_tensor("wgu", [len(gu_ids), 128, KC * 256], F32, kind="ExternalInput").ap()
    wd = nc.dram_tensor("wd", [len(wd_ids), 128, D], F32, kind="ExternalInput").ap()
    lnp = nc.dram_tensor("lnp", [128, DEPTH * 3 * 2 * KC], F32, kind="ExternalInput").ap()
    pwg = nc.dram_tensor("pwg", [128, 2048], F32, kind="ExternalInput").ap()
    pmisc = nc.dram_tensor("pmisc", [128, 72], F32, kind="ExternalInput").ap()
    yT = nc.dram_tensor("yT", [NSEG, 128, KC * TOK], F32, kind="ExternalOutput").ap()
    uctx = nc.dram_tensor("uctx", [KC, 128, HALO], F32).ap()
    cgw = nc.dram_tensor("cgw", [128, 64], F32, kind="ExternalInput").ap()
    cfrep = nc.dram_tensor("cfrep", [4, 128, KC * 128], F32, kind="ExternalInput").ap()
    cmisc = nc.dram_tensor("cmisc", [128, 80], F32, kind="ExternalInput").ap()
    cconst = nc.dram_tensor("cconst", [128, 384], F32, kind="ExternalInput").ap()
    cctx = nc.dram_tensor("cctx", [16, 128, 4], F32).ap()
    asel = nc.dram_tensor("asel", [16, 16 * 128], F32, kind="ExternalInput").ap()
    amask = nc.dram_tensor("amask", [128, 512], F32, kind="ExternalInput").ap()
    aconst = nc.dram_tensor("aconst", [128, 256], F32, kind="ExternalInput").ap()
    arb31 = nc.dram_tensor("arb31", [128, 8], F32, kind="ExternalInput").ap()
    arbx = nc.dram_tensor("arbx", [33, 8], F32, kind="ExternalInput").ap()
    aoh = nc.dram_tensor("aoh", [33, NDELTA], F32, kind="ExternalInput").ap()
    tvd = nc.dram_tensor("tvd", [8, NDELTA], BF16).ap()
    ktctx_all = [nc.dram_tensor("ktctx%d" % j, [8, 128, TOK], BF16).ap() for j in range(2)]
    vctx_all = [nc.dram_tensor("vctx%d" % j, [4, 128, 16 * 256], BF16).ap() for j in range(2)]
    kmctx_all = [nc.dram_tensor("kmctx%d" % j, [128, 64], F32).ap() for j in range(2)]
    tv_state = {}
    DB = {}

    def db(name):
        if name not in DB:
            DB[name] = Buf(name)
        return DB[name]
    sctx = nc.dram_tensor("sctx", [4, 128, 768], F32).ap()

    gu_order, wd_order = [], []
    for sg in range(NSEG):
        for ph in phases:
            a, b = phase_pieces(ph)
            gu_order += a
            wd_order += b

    stack = ExitStack()
    with stack:
        S = Sched(nc, stack)
        A = Arena(nc, 53000)
        X = A.f32(KC * TOK).rearrange("p (k t) -> p k t", k=KC)
        Xb = A.bf16(KC * TOK).rearrange("p (k t) -> p k t", k=KC)
        bX = [[Buf("X%d_%d" % (k, t)) for t in range(NT)] for k in range(KC)]
        bXb = [[Buf("Xb%d_%d" % (k, t)) for t in range(NT)] for k in range(KC)]
        LNP = A.f32(DEPTH * 3 * 2 * KC)
        bLNP = Buf("lnp")
        ones = A.bf16(128)
        bones = Buf("ones")
        NGU = 4
        WGU = [A.bf16(KC * 256).rearrange("p (k f) -> p k f", k=KC) for _ in range(NGU)]
        bWGU = [Buf("wgu%d" % i) for i in range(NGU)]
        NWD = 8
        WD = [A.bf16(D) for _ in range(NWD)]
        bWD = [Buf("wd%d" % i) for i in range(NWD)]
        MEAN = A.f32(TT)
        RSTD = A.f32(TT)
        NMR = A.f32(TT)
        bST = Buf("stats")
        YT = [A.f32(TT) for _ in range(2)]
        bYT = [Buf("yt%d" % i) for i in range(2)]
        scratch_base = A.off
        ZB = A.bf16(KC * TT).rearrange("p (k t) -> p k t", k=KC)
        bZB = Buf("zb")
        ZQ = A.bf16(KC * TT).rearrange("p (k t) -> p k t", k=KC)
        bZQ = Buf("zq")
        HG = A.bf16(8 * TOK).rearrange("p (j t) -> p j t", j=8)
        bHG = [[Buf("h%d_%d" % (j, t)) for t in range(NT)] for j in range(8)]
        SIL = [A.f32(TT) for _ in range(2)]
        bSIL = [Buf("sil%d" % i) for i in range(2)]
        PS = [nc.alloc_psum_tensor("ps%d" % i, [128, 512], F32)[:, :] for i in range(8)]
        bPS = [Buf("ps%d" % i, excl=True) for i in range(8)]

        guS = Stream(S, "pool", wgu, WGU, bWGU, gu_order, lambda a: a.rearrange("p (k f) -> p k f", k=KC), gu_map)
        wdS = Stream(S, "pool", wd, WD, bWD, wd_order, lambda a: a, wd_map)

        def barrier():
            deps = []
            for e in S.E.values():
                if e.count > 0:
                    deps.append((e.sem, e.count))
                for slot in e.ring:
                    if slot[1] > 0:
                        deps.append((slot[0], slot[1]))
            for e in S.E.values():
                for d in deps:
                    S._ensure(e, d)

        S.dma("sp", LNP, lnp, writes=[bLNP])
        S.op("dve", lambda q: q.memset(ones, 1.0 / 1024.0), writes=[bones])

        def ln_col(layer, which, gb, k):
            c = ((layer * 3 + which) * 2 + gb) * KC + k
            return LNP[:, c:c + 1]

        def layer_norm(layer, which):
            for t in range(NT):
                ln_tile(layer, which, t)

        def ln_tile(layer, which, t):
            if True:
                ts = slice(t * TT, (t + 1) * TT)
                allx = [bX[k][t] for k in range(KC)]
                S.op("act", lambda q, ts=ts: q.copy(ZB, X[:, :, ts]), reads=allx, writes=[bZB])
                S.op("act", lambda q, ts=ts: q.activation(ZQ, X[:, :, ts], AF.Square), reads=allx, writes=[bZQ])
                for k in range(KC):
                    S.op("pe", lambda q, k=k: q.matmul(PS[6], ones, ZB[:, k, :], start=(k == 0), stop=(k == KC - 1)),
                         reads=[bones, bZB], writes=[bPS[6]])
                for k in range(KC):
                    S.op("pe", lambda q, k=k: q.matmul(PS[7], ones, ZQ[:, k, :], start=(k == 0), stop=(k == KC - 1)),
                         reads=[bones, bZQ], writes=[bPS[7]])
                S.op("act", lambda q: q.copy(MEAN, PS[6]), reads=[bPS[6]], writes=[bST])
                S.op("dve", lambda q: q.tensor_tensor(NMR, MEAN, MEAN, ALU.mult), reads=[bST], writes=[bST])
                S.op("dve", lambda q: q.tensor_tensor(RSTD, PS[7], NMR, ALU.subtract), reads=[bPS[7], bST], writes=[bST])
                S.op("dve", lambda q: q.tensor_scalar_add(RSTD, RSTD, LN_EPS), reads=[bST], writes=[bST])
                S.op("act", lambda q: q.sqrt(RSTD, RSTD), reads=[bST], writes=[bST])
                S.op("dve", lambda q: q.reciprocal(RSTD, RSTD), reads=[bST], writes=[bST])
                S.op("dve", lambda q: q.scalar_tensor_tensor(NMR, MEAN, -1.0, RSTD, ALU.mult, ALU.mult), reads=[bST], writes=[bST])
                for k in range(KC):
                    y = YT[k % 2]
                    by = bYT[k % 2]
                    S.op("dve", lambda q, k=k, ts=ts, y=y: q.tensor_tensor(y, X[:, k, ts], RSTD, ALU.mult),
                         reads=[bX[k][t], bST], writes=[by])
                    S.op("dve", lambda q, y=y: q.tensor_tensor(y, y, NMR, ALU.add), reads=[by, bST], writes=[by])
                    g = ln_col(layer, which, 0, k)
                    b = ln_col(layer, which, 1, k)
                    S.op("act", lambda q, k=k, ts=ts, y=y, g=g, b=b: q.activation(X[:, k, ts], y, AF.Identity, bias=b, scale=g),
                         reads=[by, bLNP], writes=[bX[k][t]])
                    S.op("act", lambda q, k=k, ts=ts, y=y, g=g, b=b: q.activation(Xb[:, k, ts], y, AF.Identity, bias=b, scale=g),
                         reads=[by, bLNP], writes=[bXb[k][t]])

        def add_into_x(i, t, ps_idx, first):
            ts = slice(t * TT, (t + 1) * TT)
            if first:
                S.op("dve", lambda q: q.scalar_tensor_tensor(X[:, i, ts], X[:, i, ts], ALPHA, PS[ps_idx], ALU.mult, ALU.add),
                     reads=[bPS[ps_idx], bX[i][t]], writes=[bX[i][t]])
            else:
                S.op("dve", lambda q: q.tensor_tensor(X[:, i, ts], X[:, i, ts], PS[ps_idx], ALU.add),
                     reads=[bPS[ps_idx], bX[i][t]], writes=[bX[i][t]])

        gu_cnt = [0]

        def gu_tile(s, jl, t):
            ts = slice(t * TT, (t + 1) * TT)
            par = gu_cnt[0] % 2
            gu_cnt[0] += 1
            pg = 2 * par
            pu = pg + 1
            for k in range(KC):
                S.op("pe", lambda q, k=k: q.matmul(PS[pg], WGU[s][:, k, 0:128], Xb[:, k, ts], start=(k == 0), stop=(k == KC - 1)),
                     reads=[bWGU[s], bXb[k][t]], writes=[bPS[pg]])
            for k in range(KC):
                S.op("pe", lambda q, k=k: q.matmul(PS[pu], WGU[s][:, k, 128:256], Xb[:, k, ts], start=(k == 0), stop=(k == KC - 1)),
                     reads=[bWGU[s], bXb[k][t]], writes=[bPS[pu]])
            sl = SIL[par]
            bsl = bSIL[par]
            S.op("act", lambda q: q.activation(sl, PS[pg], AF.Silu), reads=[bPS[pg]], writes=[bsl])
            S.op("dve", lambda q: q.scalar_tensor_tensor(HG[:, jl, ts], PS[pu], 0.5, sl, ALU.mult, ALU.mult),
                 reads=[bPS[pu], bsl], writes=[bHG[jl][t]])

        def ffn(layer, f):
            fi = layer * 2 + f
            which = 0 if f == 0 else 2
            jbase = 0
            ngrp = len(JGROUPS)
            for gi, gsz in enumerate(JGROUPS):
                jstart = 0
                if gi == 0:
                    held = [guS.take(fi * NJ + jbase + jl) for jl in range(TOUT)]
                    for t in range(NT):
                        for jl in range(TOUT):
                            gu_tile(held[jl][0], jl, t)
                    for (_, gpos) in held:
                        guS.release(gpos)
                    jstart = TOUT
                if True:
                    for jl in range(jstart, gsz):
                        s, gpos = guS.take(fi * NJ + jbase + jl)
                        for t in range(NT):
                            gu_tile(s, jl, t)
                        guS.release(gpos)
                slots = [wdS.take(fi * NJ + jbase + jl) for jl in range(gsz)]
                for t in range(NT):
                    ts = slice(t * TT, (t + 1) * TT)
                    for i in range(KC):
                        po = 4 + (i % 2)
                        for jl in range(gsz):
                            sw = slots[jl][0]
                            S.op("pe", lambda q, i=i, jl=jl, ts=ts, sw=sw, po=po, gsz=gsz: q.matmul(PS[po], WD[sw][:, i * 128:(i + 1) * 128], HG[:, jl, ts], start=(jl == 0), stop=(jl == gsz - 1)),
                                 reads=[bWD[sw], bHG[jl][t]], writes=[bPS[po]])
                        add_into_x(i, t, po, gi == 0)
                    if gi == ngrp - 1:
                        ln_tile(layer, which, t)
                for (_, wpos) in slots:
                    wdS.release(wpos)
                jbase += gsz

        def pool_mixer(layer, sg):
            barrier()
            A.off = scratch_base
            W = HALO + TOK
            UF = A.f32(W)
            T1 = A.f32(W)
            T2 = A.f32(W)
            TMP = A.f32(HALO)
            Pm = A.bf16(2 * TOK).rearrange("p (c t) -> p c t", c=2)
            Y1 = A.bf16(2 * TOK).rearrange("p (c t) -> p c t", c=2)
            WGS = A.bf16(2048)
            WGS4 = WGS.rearrange("p (m i f) -> p m i f", m=4, i=2)
            PMI = A.f32(72)
            bUF, bT1, bT2, bTMP, bWGS, bPMI = Buf(), Buf(), Buf(), Buf(), Buf(), Buf()
            bPm = [[Buf() for _ in range(NT)] for _ in range(2)]
            bY1 = [[Buf() for _ in range(NT)] for _ in range(2)]
            S.dma("pool", WGS, pwg, writes=[bWGS])
            S.dma("sp", PMI, pmisc, writes=[bPMI])
            for m in range(4):
                s, gpos = guS.take(GU_MIX[layer] + m)
                w = 2 ** (m + 1)
                for oc in range(2):
                    c = 2 * m + oc
                    if sg == 0:
                        S.op("dve", lambda q: q.memset(UF[:, 0:HALO], 0.0), writes=[bUF])
                    else:
                        S.dma("sp", UF[:, 0:HALO], uctx[c], reads=[db("u%d" % c)], writes=[bUF])
                    for t in range(NT):
                        ts = slice(t * TT, (t + 1) * TT)
                        p = t % 4
                        for k in range(KC):
                            S.op("pe", lambda q, k=k, ts=ts, s=s, p=p, oc=oc: q.matmul(PS[p], WGU[s][:, k, oc * 128:(oc + 1) * 128], Xb[:, k, ts], start=(k == 0), stop=(k == KC - 1)),
                                 reads=[bWGU[s], bXb[k][t]], writes=[bPS[p]])
                        S.op("act", lambda q, t=t, p=p: q.copy(UF[:, HALO + t * TT:HALO + (t + 1) * TT], PS[p]), reads=[bPS[p]], writes=[bUF])
                    if sg == 0:
                        S.dma("sp", uctx[c], UF[:, TOK:TOK + HALO], reads=[bUF], writes=[db("u%d" % c)])
                    cur, bcur = UF, bUF
                    tmps = [(T1, bT1), (T2, bT2)]
                    for i in range(m + 1):
                        sh = 1 << i
                        lo = (2 << i) - 1
                        nxt, bn = tmps[i % 2]
                        S.op("dve", lambda q, nxt=nxt, cur=cur, lo=lo, sh=sh: q.tensor_tensor(nxt[:, lo:W], cur[:, lo:W], cur[:, lo - sh:W - sh], ALU.add),
                             reads=[bcur], writes=[bn])
                        cur, bcur = nxt, bn
                    S.op("dve", lambda q, cur=cur, oc=oc, w=w: q.scalar_tensor_tensor(Pm[:, oc, :], cur[:, HALO:W], 1.0 / w, UF[:, HALO:W], ALU.mult, ALU.subtract),
                         reads=[bcur, bUF], writes=bPm[oc])
                    if sg == 0:
                        S.op("dve", lambda q, cur=cur, m=m: q.tensor_tensor(TMP, cur[:, HALO:2 * HALO], PMI[:, m * HALO:(m + 1) * HALO], ALU.mult),
                             reads=[bcur, bPMI], writes=[bTMP])
                        S.op("dve", lambda q, oc=oc: q.tensor_tensor(Pm[:, oc, 0:HALO], TMP, UF[:, HALO:2 * HALO], ALU.subtract),
                             reads=[bTMP, bUF], writes=[bPm[oc][0]])
                guS.release(gpos)
                for oc in range(2):
                    c = 2 * m + oc
                    for t in range(NT):
                        ts = slice(t * TT, (t + 1) * TT)
                        p = (2 * oc + t) % 4
                        for ic in range(2):
                            S.op("pe", lambda q, ic=ic, oc=oc, ts=ts, p=p, m=m: q.matmul(PS[p], WGS4[:, m, ic, oc * 128:(oc + 1) * 128], Pm[:, ic, ts], start=(ic == 0), stop=(ic == 1)),
                                 reads=[bWGS, bPm[ic][t]], writes=[bPS[p]])
                        S.op("act", lambda q, oc=oc, ts=ts, p=p, c=c: q.activation(Y1[:, oc, ts], PS[p], AF.Identity, scale=PMI[:, 64 + c:65 + c]),
                             reads=[bPS[p], bPMI], writes=[bY1[oc][t]])
                for oc in range(2):
                    c = 2 * m + oc
                    sw, wpos = wdS.take(WD_MIX[layer] + c)
                    for t in range(NT):
                        ts = slice(t * TT, (t + 1) * TT)
                        for i in range(KC):
                            po = 4 + (i % 2)
                            S.op("pe", lambda q, i=i, oc=oc, ts=ts, sw=sw, po=po: q.matmul(PS[po], WD[sw][:, i * 128:(i + 1) * 128], Y1[:, oc, ts], start=True, stop=True),
                                 reads=[bWD[sw], bY1[oc][t]], writes=[bPS[po]])
                            add_into_x(i, t, po, c == 0)
                    wdS.release(wpos)
            barrier()

        def moba_mixer(layer, sg):
            barrier()
            lj = layer // 3
            ktctx, vctx, kmctx = ktctx_all[lj], vctx_all[lj], kmctx_all[lj]
            A.off = scratch_base
            QH = A.bf16(TOK)
            QL = A.bf16(TOK)
            QB = A.bf16(TOK)
            KMH = A.bf16(16)
            KML = A.bf16(16)
            KTA = A.bf16(2 * TOK)
            VP = A.bf16(32 * 256).rearrange("p (a f) -> p a f", a=32)
            KMA = A.f32(16)
            SELT = A.bf16(TOK)
            BTP = A.bf16(5 * TT).rearrange("p (r t) -> p r t", r=5)
            SEL = A.bf16(16 * 128)
            MK = A.f32(512)
            CJ = A.f32(256)
            JB = A.bf16(128)
            IDB = A.bf16(128)
            ONEB = A.bf16(128)
            RB31 = A.f32(8)
            GMA = A.f32(256)
            G2 = A.f32(256)
            GE = A.f32(256)
            MX = A.f32(16)
            SELBA = A.bf16(256)
            base0 = A.off
            if sg == 0:
                RBX = A.f32(8)
                RBH = A.bf16(8)
                RBL = A.bf16(8)
                OH = A.f32(NDELTA)
                OHB = A.bf16(NDELTA)
                TVS = A.bf16(NDELTA)
                assert A.off <= A.words, A.off
            A.off = base0
            NPT = 4
            PT = [A.bf16(TT) for _ in range(NPT)]
            OT = A.bf16(TT)
            REC = A.f32(TT)
            SUMP = A.f32(TT)
            SH = A.bf16(TT)
            SL = A.bf16(TT)
            assert A.off <= A.words, A.off
            bQ, bKT, bVP, bKM, bSELT, bBTP, bOT, bREC, bK, bGM, bSB = Buf(), Buf(), Buf(), Buf(), Buf(), Buf(), Buf(), Buf(), Buf(), Buf(), Buf()
            bPT = [Buf() for _ in range(NPT)]
            bSUM = Buf()
            PMASK = MK[:, 0:256].rearrange("p (g n) -> p g n", g=16)
            NOTOWN = MK[:, 256:512].rearrange("p (g n) -> p g n", g=16)
            S.dma("pool", SEL[0:16, :], asel, writes=[bK])
            S.dma("sp", MK, amask, writes=[bK])
            S.dma("sp", CJ, aconst, writes=[bK])
            S.dma("sp", RB31, arb31, writes=[bK])
            S.op("act", lambda q: q.copy(JB, CJ[:, 0:128]), reads=[bK], writes=[bK])
            S.op("act", lambda q: q.copy(IDB, CJ[:, 128:256]), reads=[bK], writes=[bK])
            S.op("dve", lambda q: q.memset(ONEB, 1.0), writes=[bK])
            if sg == 0:
                S.dma("sp", RBX[0:33, :], arbx, writes=[bK])
                S.dma("sp", OH[0:33, :], aoh, writes=[bK])
                S.op("act", lambda q: q.copy(RBH[0:33, :], RBX[0:33, :]), reads=[bK], writes=[bK])
                S.op("dve", lambda q: q.tensor_tensor(RBL[0:33, :], RBX[0:33, :], RBH[0:33, :], ALU.subtract), reads=[bK], writes=[bK])
                S.op("act", lambda q: q.copy(OHB[0:33, :], OH[0:33, :]), reads=[bK], writes=[bK])
                for c3 in range(3):
                    S.op("pe", lambda q, c3=c3: q.matmul(PS[c3][0:8, 0:384], RBH[0:33, 0:8], OHB[0:33, c3 * 384:(c3 + 1) * 384], start=True, stop=False),
                         reads=[bK], writes=[bPS[c3]])
                    S.op("pe", lambda q, c3=c3: q.matmul(PS[c3][0:8, 0:384], RBL[0:33, 0:8], OHB[0:33, c3 * 384:(c3 + 1) * 384], start=False, stop=True),
                         reads=[bK], writes=[bPS[c3]])
                    S.op("act", lambda q, c3=c3: q.copy(TVS[0:8, c3 * 384:(c3 + 1) * 384], PS[c3][0:8, 0:384]), reads=[bPS[c3]], writes=[bK])
                S.dma("sp", tvd, TVS[0:8, :], reads=[bK], writes=[db("tvd")])
                barrier()
            for hp in range(4):
                sv, pv = guS.take(GU_MIX[layer] + 3 * hp)
                if sg == 1:
                    S.dma("sp", VP[:, 0:16, :], vctx[hp].rearrange("p (a f) -> p a f", a=16), reads=[db("L%dvc%d" % (lj, hp))], writes=[bVP])
                for tt in range(16):
                    p = 6 + (tt % 2)
                    for k in range(KC):
                        S.op("pe", lambda q, tt=tt, k=k, p=p, sv=sv: q.matmul(PS[p][:, 0:256], Xb[:, k, tt * 128:(tt + 1) * 128], WGU[sv][:, k, :], start=(k == 0), stop=(k == KC - 1)),
                             reads=[bWGU[sv], bXb[k][tt // 4]], writes=[bPS[p]])
                    S.op("act", lambda q, tt=tt, p=p: q.copy(VP[:, sg * 16 + tt, :], PS[p][:, 0:256]), reads=[bPS[p]], writes=[bVP])
                guS.release(pv)
                if sg == 0:
                    S.dma("sp", vctx[hp].rearrange("p (a f) -> p a f", a=16), VP[:, 0:16, :], reads=[bVP], writes=[db("L%dvc%d" % (lj, hp))])
                for hh in range(2):
                    h = 2 * hp + hh
                    sa, pa = guS.take(GU_MIX[layer] + 3 * hp + 1 + hh)
                    sw, wpos = wdS.take(WD_MIX[layer] + h)
                    S.op("dve", lambda q: q.memset(KMA, 0.0), writes=[bKM])
                    if sg == 1:
                        S.dma("sp", KTA[:, 0:TOK], ktctx[h], reads=[db("L%dkt%d" % (lj, h))], writes=[bKT])
                        S.dma("sp", KMA[:, 0:8], kmctx[:, h * 8:(h + 1) * 8], reads=[db("L%dkm%d" % (lj, h))], writes=[bKM])
                    for t in range(NT):
                        ts = slice(t * TT, (t + 1) * TT)
                        for k in range(KC):
                            S.op("pe", lambda q, k=k, ts=ts, sa=sa: q.matmul(PS[6], WGU[sa][:, k, 0:128], Xb[:, k, ts], start=(k == 0), stop=(k == KC - 1)),
                                 reads=[bWGU[sa], bXb[k][t]], writes=[bPS[6]])
                        S.op("act", lambda q, ts=ts: q.copy(QH[:, ts], PS[6]), reads=[bPS[6]], writes=[bQ])
                        S.op("dve", lambda q, ts=ts: q.tensor_tensor(QL[:, ts], PS[6], QH[:, ts], ALU.subtract), reads=[bPS[6], bQ], writes=[bQ])
                        S.op("act", lambda q, ts=ts: q.activation(QB[:, ts], PS[6], AF.Identity, scale=float(128 ** -0.5)), reads=[bPS[6]], writes=[bQ])
                        for k in range(KC):
                            S.op("pe", lambda q, k=k, ts=ts, sa=sa: q.matmul(PS[7], WGU[sa][:, k, 128:256], Xb[:, k, ts], start=(k == 0), stop=(k == KC - 1)),
                                 reads=[bWGU[sa], bXb[k][t]], writes=[bPS[7]])
                        S.op("act", lambda q, t=t: q.copy(KTA[:, sg * TOK + t * TT:sg * TOK + (t + 1) * TT], PS[7]), reads=[bPS[7]], writes=[bKT])
                        S.op("dve", lambda q, t=t: q.tensor_reduce(KMA[:, sg * 8 + 2 * t:sg * 8 + 2 * t + 2], PS[7].rearrange("p (b l) -> p b l", b=2), AX.X, ALU.add),
                             reads=[bPS[7]], writes=[bKM])
                    guS.release(pa)
                    if sg == 0:
                        S.dma("sp", ktctx[h], KTA[:, 0:TOK], reads=[bKT], writes=[db("L%dkt%d" % (lj, h))])
                        S.dma("sp", kmctx[:, h * 8:(h + 1) * 8], KMA[:, 0:8], reads=[bKM], writes=[db("L%dkm%d" % (lj, h))])
                    for r in range(5):
                        src = bass.AP(tvd.tensor, h * NDELTA + 512 - 128 * r, [[1, 128], [1, TT]])
                        S.dma("sp", BTP[:, r, :], src, reads=[db("tvd")], writes=[bBTP])
                    S.op("act", lambda q: q.copy(KMH, KMA), reads=[bKM], writes=[bKM])
                    S.op("dve", lambda q: q.tensor_tensor(KML, KMA, KMH, ALU.subtract), reads=[bKM], writes=[bKM])
                    for qt in range(16):
                        qs = slice(qt * 128, (qt + 1) * 128)
                        go = PS[6][:, qt * 16:(qt + 1) * 16]
                        S.op("pe", lambda q, qs=qs, go=go: q.matmul(go, QH[:, qs], KMH, start=True, stop=False), reads=[bQ, bKM], writes=[bPS[6]])
                        S.op("pe", lambda q, qs=qs, go=go: q.matmul(go, QH[:, qs], KML, start=False, stop=False), reads=[bQ, bKM], writes=[bPS[6]])
                        S.op("pe", lambda q, qs=qs, go=go: q.matmul(go, QL[:, qs], KMH, start=False, stop=True), reads=[bQ, bKM], writes=[bPS[6]])
                    g4 = lambda a: a.rearrange("p (g two n) -> p g two n", g=8, two=2)
                    pm4 = PMASK[:, sg * 8:sg * 8 + 8, :].unsqueeze(2).to_broadcast([128, 8, 2, 16])
                    no4 = NOTOWN[:, sg * 8:sg * 8 + 8, :].unsqueeze(2).to_broadcast([128, 8, 2, 16])
                    g3 = lambda a: a.rearrange("p (t n) -> p t n", t=16)
                    bc = lambda a: a.unsqueeze(2).to_broadcast([128, 16, 16])
                    S.op("dve", lambda q: q.tensor_tensor(g4(GMA), g4(PS[6][:, 0:256]), pm4, ALU.add), reads=[bPS[6], bK], writes=[bGM])
                    S.op("dve", lambda q: q.tensor_reduce(MX, g3(GMA), AX.X, ALU.max), reads=[bGM], writes=[bSB])
                    S.op("dve", lambda q: q.tensor_tensor(g3(GE), g3(GMA), bc(MX), ALU.is_ge), reads=[bGM, bSB], writes=[bSB])
                    S.op("dve", lambda q: q.scalar_tensor_tensor(G2, GE, -1e30, GMA, ALU.mult, ALU.add), reads=[bGM, bSB], writes=[bSB])
                    S.op("dve", lambda q: q.tensor_reduce(MX, g3(G2), AX.X, ALU.max), reads=[bSB], writes=[bSB])
                    S.op("dve", lambda q: q.tensor_tensor(g3(GE), g3(G2), bc(MX), ALU.is_ge), reads=[bSB], writes=[bSB])
                    S.op("dve", lambda q: q.scalar_tensor_tensor(G2, GE, -1e30, G2, ALU.mult, ALU.add), reads=[bSB], writes=[bSB])
                    S.op("dve", lambda q: q.tensor_reduce(MX, g3(G2), AX.X, ALU.max), reads=[bSB], writes=[bSB])
                    S.op("dve", lambda q: q.tensor_scalar_max(MX, MX, -1e29), reads=[bSB], writes=[bSB])
                    S.op("dve", lambda q: q.tensor_tensor(g3(GE), g3(GMA), bc(MX), ALU.is_ge), reads=[bGM, bSB], writes=[bSB])
                    S.op("dve", lambda q: q.tensor_scalar(GE, GE, 30000.0, -30000.0, ALU.mult, ALU.add), reads=[bSB], writes=[bSB])
                    S.op("dve", lambda q: q.tensor_tensor(g4(SELBA), g4(GE), no4, ALU.mult), reads=[bSB, bK], writes=[bSB])
                    for qt in range(16):
                        S.op("pe", lambda q, qt=qt: q.matmul(PS[7][0:16, (qt % 4) * 128:(qt % 4 + 1) * 128], SELBA[:, qt * 16:(qt + 1) * 16], IDB, start=True, stop=True),
                             reads=[bSB, bK], writes=[bPS[7]])
                        if qt % 4 == 3:
                            T = qt // 4
                            S.op("act", lambda q, T=T: q.copy(SELT[0:16, T * TT:(T + 1) * TT], PS[7][0:16, :]), reads=[bPS[7]], writes=[bSELT])
                    pending = [None, []]
                    for T in range(NT):
                        gT = sg * 4 + T
                        ts = slice(T * TT, (T + 1) * TT)
                        nch = 4 * gT + 4
                        psb = [0, 1, 6, 7]

                        def stage1(c, ts=ts, gT=gT, h=h):
                            n = c // 2
                            r = c - 4 * gT
                            near = r >= -1
                            ps = psb[c % NPT]
                            pt = PT[c % NPT]
                            bpt = bPT[c % NPT]
                            S.op("pe", lambda q: q.matmul(PS[ps], KTA[:, c * 128:(c + 1) * 128], QB[:, ts], start=True, stop=False),
                                 reads=[bKT, bQ], writes=[bPS[ps]])
                            if r < 2:
                                S.op("pe", lambda q: q.matmul(PS[ps], SEL[0:16, n * 128:(n + 1) * 128], SELT[0:16, ts], start=False, stop=(not near)),
                                     reads=[bK, bSELT], writes=[bPS[ps]])
                            if near:
                                S.op("pe", lambda q: q.matmul(PS[ps], JB, BTP[:, r + 1, :], start=False, stop=True),
                                     reads=[bK, bBTP], writes=[bPS[ps]])
                                S.op("act", lambda q: q.activation(pt, PS[ps], AF.Exp), reads=[bPS[ps]], writes=[bpt])
                            else:
                                S.op("act", lambda q, h=h: q.activation(pt, PS[ps], AF.Exp, bias=RB31[:, h:h + 1]), reads=[bPS[ps], bK], writes=[bpt])

                        def stage2(c, hh=hh, nch=nch):
                            pt = PT[c % NPT]
                            bpt = bPT[c % NPT]
                            S.op("pe", lambda q: q.matmul(PS[2], VP[:, c, hh * 128:(hh + 1) * 128], pt, start=(c == 0), stop=(c == nch - 1)),
                                 reads=[bVP, bpt], writes=[bPS[2]])
                            if c == 0:
                                S.op("dve", lambda q: q.tensor_copy(SUMP, pt), reads=[bpt], writes=[bSUM])
                            else:
                                S.op("dve", lambda q: q.tensor_tensor(SUMP, SUMP, pt, ALU.add), reads=[bpt, bSUM], writes=[bSUM])

                        LOOK = 3
                        def tail_b():
                            S.op("pe", lambda q: q.matmul(PS[3], ONEB, SH, start=True, stop=False), reads=[bK, bREC], writes=[bPS[3]])
                            S.op("pe", lambda q: q.matmul(PS[3], ONEB, SL, start=False, stop=True), reads=[bK, bREC], writes=[bPS[3]])
                            S.op("dve", lambda q: q.reciprocal(REC, PS[3]), reads=[bPS[3]], writes=[bREC])
                            S.op("dve", lambda q: q.tensor_tensor(OT, PS[2], REC, ALU.mult), reads=[bPS[2], bREC], writes=[bOT])

                        def tail_d(i, T=T, sw=sw, h=h):
                            po = 4 + (i % 2)
                            S.op("pe", lambda q: q.matmul(PS[po], WD[sw][:, i * 128:(i + 1) * 128], OT, start=True, stop=True),
                                 reads=[bWD[sw], bOT], writes=[bPS[po]])
                            add_into_x(i, T, po, h == 0)

                        cb = 2
                        cd = 8
                        for c in range(nch + LOOK):
                            if c < nch:
                                stage1(c)
                            if c == cb and pending[0] is not None:
                                pending[0]()
                                pending[0] = None
                            if c >= cd and pending[1]:
                                pending[1].pop(0)()
                            if c >= LOOK:
                                stage2(c - LOOK)
                        while pending[1]:
                            pending[1].pop(0)()
                        S.op("act", lambda q: q.copy(SH, SUMP), reads=[bSUM], writes=[bREC])
                        S.op("dve", lambda q: q.tensor_tensor(SL, SUMP, SH, ALU.subtract), reads=[bSUM, bREC], writes=[bREC])
                        pending[0] = tail_b
                        pending[1] = [(lambda i=i, f=tail_d: f(i)) for i in range(KC)]
                    pending[0]()
                    while pending[1]:
                        pending[1].pop(0)()
                    wdS.release(wpos)
            barrier()

        def mlstm_mixer(layer, sg):
            barrier()
            A.off = scratch_base
            QT = A.bf16(2 * TOK).rearrange("p (c t) -> p c t", c=2)
            KT = A.bf16(2 * TOK).rearrange("p (c t) -> p c t", c=2)
            KM = A.bf16(16 * 256).rearrange("p (a f) -> p a f", a=16)
            VA = A.bf16(16 * 384).rearrange("p (a f) -> p a f", a=16)
            EBI = A.bf16(TOK)
            CF = A.f32(768).rearrange("p (c f) -> p c f", c=2)
            CB = A.bf16(768).rearrange("p (c f) -> p c f", c=2)
            WFR = A.bf16(KC * 128).rearrange("p (k f) -> p k f", k=KC)
            CGW = A.bf16(64).rearrange("p (k f) -> p k f", k=KC)
            CM = A.f32(80)
            CC = A.f32(384)
            IDB = A.bf16(128)
            O256 = A.bf16(128)
            GT = A.f32(128).rearrange("p (a g) -> p a g", a=16)
            LOGF = A.f32(64).rearrange("p (a g) -> p a g", a=16)
            BT_ = A.f32(64).rearrange("p (a g) -> p a g", a=16)
            EV = A.f32(64).rearrange("p (a g) -> p a g", a=16)
            NBG = A.f32(8)
            DEC = A.f32(32)
            TRIB = A.bf16(128)
            LHI = A.bf16(64)
            LLO = A.bf16(64)
            rbase = A.off
            PRE = A.f32(TOK + 4)
            ACC = A.f32(TOK)
            rend = A.off
            A.off = rbase
            WTS = [A.bf16(128), A.bf16(128)]
            DEN = A.f32(TT)
            HC = A.f32(2 * TT).rearrange("p (c t) -> p c t", c=2)
            HCB = A.bf16(2 * TT).rearrange("p (c t) -> p c t", c=2)
            HCQ = A.bf16(2 * TT).rearrange("p (c t) -> p c t", c=2)
            HN = A.bf16(2 * TT).rearrange("p (c t) -> p c t", c=2)
            SG = A.bf16(2 * TT).rearrange("p (c t) -> p c t", c=2)
            A.off = max(A.off, rend)
            assert A.off <= A.words, A.off
            bQT, bKT, bKM, bVA, bEBI, bCF, bCB, bWFR, bK = Buf(), Buf(), Buf(), Buf(), Buf(), Buf(), Buf(), Buf(), Buf()
            bG, bR1, bR2, bWT0, bDEN, bHC, bHCB, bHCQ, bHN, bSG, bDEC = Buf(), Buf(), Buf(), Buf(), Buf(), Buf(), Buf(), Buf(), Buf(), Buf(), Buf()
            bWT1 = Buf()
            TRI = CC[:, 0:128]
            CAUS = CC[:, 128:256]
            S.dma("pool", CGW, cgw.rearrange("p (k f) -> p k f", k=KC), writes=[bK])
            S.dma("sp", CM, cmisc, writes=[bK])
            S.dma("sp", CC, cconst, writes=[bK])
            S.op("act", lambda q: q.copy(IDB, CC[:, 256:384]), reads=[bK], writes=[bK])
            S.op("dve", lambda q: q.memset(O256, 1.0 / 256.0), writes=[bK])
            S.op("dve", lambda q: q.tensor_scalar_mul(NBG, CM[:, 72:80], -1.0), reads=[bK], writes=[bK])
            for tt in range(16):
                for k in range(KC):
                    S.op("pe", lambda q, tt=tt, k=k: q.matmul(PS[7][:, tt * 8:(tt + 1) * 8], Xb[:, k, tt * 128:(tt + 1) * 128], CGW[:, k, :], start=(k == 0), stop=(k == KC - 1)),
                         reads=[bK] + [bXb[k][tt // 4]], writes=[bPS[7]])
            S.op("dve", lambda q: q.tensor_tensor(GT, PS[7][:, 0:128].rearrange("p (a g) -> p a g", a=16), CM[:, 72:80].unsqueeze(1).to_broadcast([128, 16, 8]), ALU.add),
                 reads=[bPS[7], bK], writes=[bG])
            S.op("act", lambda q: q.activation(LOGF, GT[:, :, 4:8], AF.Exp, scale=-1.0), reads=[bG], writes=[bG])
            S.op("dve", lambda q: q.tensor_scalar_add(LOGF, LOGF, 1.0), reads=[bG], writes=[bG])
            S.op("act", lambda q: q.activation(LOGF, LOGF, AF.Ln), reads=[bG], writes=[bG])
            S.op("act", lambda q: q.copy(TRIB, TRI), reads=[bK], writes=[bK])
            S.op("act", lambda q: q.copy(LHI, LOGF.rearrange("p a g -> p (a g)")), reads=[bG], writes=[bG])
            S.op("dve", lambda q: q.tensor_tensor(LLO, LOGF.rearrange("p a g -> p (a g)"), LHI, ALU.subtract), reads=[bG], writes=[bG])
            S.op("pe", lambda q: q.matmul(PS[7][:, 128:192], TRIB, LHI, start=True, stop=False), reads=[bK, bG], writes=[bPS[7]])
            S.op("pe", lambda q: q.matmul(PS[7][:, 128:192], TRIB, LLO, start=False, stop=True), reads=[bK, bG], writes=[bPS[7]])
            S.op("dve", lambda q: q.tensor_tensor(BT_, GT[:, :, 0:4], PS[7][:, 128:192].rearrange("p (a g) -> p a g", a=16), ALU.add),
                 reads=[bPS[7], bG], writes=[bG])
            S.op("dve", lambda q: q.tensor_scalar_add(BT_, BT_, -math.log(16.0)), reads=[bG], writes=[bG])
            S.op("act", lambda q: q.activation(EV, BT_, AF.Exp), reads=[bG], writes=[bG])

            def conv_silu(h, isk, cc, slot, OUTT, bOUT):
                ci = (h * 2 + isk) * 2 + cc
                cw = (isk * 8 + h * 2 + cc) * 4
                if sg == 0:
                    S.op("dve", lambda q: q.memset(PRE[:, 0:4], 0.0), writes=[bR1])
                else:
                    S.dma("sp", PRE[:, 0:4], cctx[ci], reads=[db("cc%d" % ci)], writes=[bR1])
                for t in range(NT):
                    ts = slice(t * TT, (t + 1) * TT)
                    p = t % 3
                    for k in range(KC):
                        S.op("pe", lambda q, k=k, ts=ts, p=p: q.matmul(PS[p], WGU[slot][:, k, cc * 128:(cc + 1) * 128], Xb[:, k, ts], start=(k == 0), stop=(k == KC - 1)),
                             reads=[bWGU[slot], bXb[k][t]], writes=[bPS[p]])
                    S.op("act", lambda q, t=t, p=p: q.copy(PRE[:, 4 + t * TT:4 + (t + 1) * TT], PS[p]), reads=[bPS[p]], writes=[bR1])
                if sg == 0:
                    S.dma("sp", cctx[ci], PRE[:, TOK:TOK + 4], reads=[bR1], writes=[db("cc%d" % ci)])
                S.op("dve", lambda q: q.tensor_scalar_mul(ACC, PRE[:, 1:1 + TOK], CM[:, cw:cw + 1]), reads=[bR1, bK], writes=[bR2])
                for j in range(1, 4):
                    S.op("dve", lambda q, j=j: q.scalar_tensor_tensor(ACC, PRE[:, 1 + j:1 + j + TOK], CM[:, cw + j:cw + j + 1], ACC, ALU.mult, ALU.add),
                         reads=[bR1, bR2, bK], writes=[bR2])
                S.op("act", lambda q: q.activation(OUTT[:, cc, :], ACC, AF.Silu), reads=[bR2], writes=[bOUT])

            for h in range(4):
                sq, pq = guS.take(GU_MIX[layer] + 4 * h + 0)
                sk, pk = guS.take(GU_MIX[layer] + 4 * h + 1)
                sv, pv = guS.take(GU_MIX[layer] + 4 * h + 2)
                so, po_ = guS.take(GU_MIX[layer] + 4 * h + 3)
                w0, wp0 = wdS.take(WD_MIX[layer] + 2 * h)
                w1, wp1 = wdS.take(WD_MIX[layer] + 2 * h + 1)
                wsl = [w0, w1]
                S.alias_after([bR1, bR2], [bWT0, bWT1, bDEN, bHC, bHCB, bHCQ, bHN, bSG])
                S.dma("pool", WFR, cfrep[h].rearrange("p (k f) -> p k f", k=KC), writes=[bWFR])
                for t in range(NT):
                    ts = slice(t * TT, (t + 1) * TT)
                    p = t % 3
                    for k in range(KC):
                        S.op("pe", lambda q, k=k, ts=ts, p=p: q.matmul(PS[p], WFR[:, k, :], Xb[:, k, ts], start=(k == 0), stop=(k == KC - 1)),
                             reads=[bWFR, bXb[k][t]], writes=[bPS[p]])
                    S.op("act", lambda q, ts=ts, p=p, h=h: q.activation(PRE[:, ts], PS[p], AF.Exp, bias=NBG[:, 4 + h:5 + h], scale=-1.0),
                         reads=[bPS[p], bK], writes=[bR1])
                S.op("dve", lambda q: q.tensor_scalar_add(PRE[:, 0:TOK], PRE[:, 0:TOK], 1.0), reads=[bR1], writes=[bR1])
                S.op("act", lambda q: q.activation(PRE[:, 0:TOK], PRE[:, 0:TOK], AF.Ln), reads=[bR1], writes=[bR1])
                cur, bcur, nxt, bnxt = PRE[:, 0:TOK], bR1, ACC, bR2
                for i in range(7):
                    sh = 1 << i
                    c3 = cur.rearrange("p (c l) -> p c l", l=128)
                    n3 = nxt.rearrange("p (c l) -> p c l", l=128)
                    S.op("dve", lambda q, c3=c3, n3=n3, sh=sh: q.tensor_tensor(n3[:, :, sh:128], c3[:, :, sh:128], c3[:, :, 0:128 - sh], ALU.add),
                         reads=[bcur], writes=[bnxt])
                    S.op("act", lambda q, c3=c3, n3=n3, sh=sh: q.copy(n3[:, :, 0:sh], c3[:, :, 0:sh]), reads=[bcur], writes=[bnxt])
                    cur, bcur, nxt, bnxt = nxt, bnxt, cur, bcur
                S.op("act", lambda q, cur=cur: q.activation(EBI, cur, AF.Exp), reads=[bcur], writes=[bEBI])
                S.op("act", lambda q, cur=cur: q.activation(DEC[:, 0:16], cur.rearrange("p (c l) -> p c l", l=128)[:, :, 127], AF.Exp, scale=-1.0),
                     reads=[bcur], writes=[bDEC])
                for cc in range(2):
                    conv_silu(h, 0, cc, sq, QT, bQT)
                guS.release(pq)
                for cc in range(2):
                    conv_silu(h, 1, cc, sk, KT, bKT)
                guS.release(pk)
                for tt in range(16):
                    p = 3 + (tt % 2)
                    for cc in range(2):
                        S.op("pe", lambda q, tt=tt, cc=cc, p=p: q.matmul(PS[p][:, cc * 128:(cc + 1) * 128], KT[:, cc, tt * 128:(tt + 1) * 128], IDB, start=True, stop=True),
                             reads=[bKT, bK], writes=[bPS[p]])
                    S.op("act", lambda q, tt=tt, p=p: q.copy(KM[:, tt, :], PS[p][:, 0:256]), reads=[bPS[p]], writes=[bKM])
                for tt in range(16):
                    p = 5 + (tt % 2)
                    for k in range(KC):
                        S.op("pe", lambda q, tt=tt, k=k, p=p, sv=sv: q.matmul(PS[p][:, 0:256], Xb[:, k, tt * 128:(tt + 1) * 128], WGU[sv][:, k, :], start=(k == 0), stop=(k == KC - 1)),
                             reads=[bWGU[sv], bXb[k][tt // 4]], writes=[bPS[p]])
                    S.op("act", lambda q, tt=tt, p=p, h=h: q.activation(VA[:, tt, 0:256], PS[p][:, 0:256], AF.Identity, scale=EV[:, tt, h:h + 1]),
                         reads=[bPS[p], bG], writes=[bVA])
                    S.op("dve", lambda q, tt=tt, h=h: q.tensor_scalar(VA[:, tt, 256:384], ones, EV[:, tt, h:h + 1], 1024.0, ALU.mult, ALU.mult),
                         reads=[bones, bG], writes=[bVA])
                guS.release(pv)
                if sg == 0:
                    S.op("dve", lambda q: q.memset(CF, 0.0), writes=[bCF])
                else:
                    S.dma("sp", CF, sctx[h].rearrange("p (c f) -> p c f", c=2), reads=[db("sc%d" % h)], writes=[bCF])
                S.op("act", lambda q: q.copy(CB, CF), reads=[bCF], writes=[bCB])
                bWTS = [bWT0, bWT1]
                stage_bufs = [bWT0, bWT1, bDEN, bHC, bHCB, bHCQ, bHN, bSG]
                S.alias_after(stage_bufs, [bR1, bR2])

                def st_stage(tt):
                    WT = WTS[tt % 2]
                    for kc in range(2):
                        S.op("pe", lambda q, kc=kc: q.matmul(PS[3][:, 0:128], KT[:, kc, tt * 128:(tt + 1) * 128], QT[:, kc, tt * 128:(tt + 1) * 128], start=(kc == 0), stop=(kc == 1)),
                             reads=[bKT, bQT], writes=[bPS[3]])
                    S.op("dve", lambda q: q.tensor_tensor(WT, PS[3][:, 0:128], CAUS, ALU.mult), reads=[bPS[3], bK], writes=[bWTS[tt % 2]])

                st_stage(0)
                for tt in range(16):
                    T4 = tt // 4
                    col0 = (tt % 4) * 128
                    WT = WTS[tt % 2]
                    bWT = bWTS[tt % 2]
                    for vc in range(3):
                        S.op("pe", lambda q, tt=tt, vc=vc, col0=col0, WT=WT: q.matmul(PS[vc][:, col0:col0 + 128], VA[:, tt, vc * 128:(vc + 1) * 128], WT, start=True, stop=False),
                             reads=[bVA, bWT], writes=[bPS[vc]])
                    if tt + 1 < 16:
                        st_stage(tt + 1)
                    for vc in range(3):
                        for kc in range(2):
                            S.op("pe", lambda q, tt=tt, vc=vc, kc=kc, col0=col0: q.matmul(PS[vc][:, col0:col0 + 128], CB[:, kc, vc * 128:(vc + 1) * 128], QT[:, kc, tt * 128:(tt + 1) * 128], start=False, stop=(kc == 1)),
                                 reads=[bCB, bQT], writes=[bPS[vc]])
                    S.op("dve", lambda q, tt=tt: q.tensor_scalar_mul(CF, CF, DEC[:, tt:tt + 1]), reads=[bCF, bDEC], writes=[bCF])
                    for kc in range(2):
                        S.op("pe", lambda q, tt=tt, kc=kc: q.matmul(PS[4 + kc][:, 0:384], KM[:, tt, kc * 128:(kc + 1) * 128], VA[:, tt, :], start=True, stop=True),
                             reads=[bKM, bVA], writes=[bPS[4 + kc]])
                    for kc in range(2):
                        S.op("dve", lambda q, kc=kc, tt=tt: q.scalar_tensor_tensor(CF[:, kc, :], PS[4 + kc][:, 0:384], DEC[:, tt:tt + 1], CF[:, kc, :], ALU.mult, ALU.add),
                             reads=[bPS[4 + kc], bCF, bDEC], writes=[bCF])
                    S.op("dve", lambda q: q.tensor_copy(CB, CF), reads=[bCF], writes=[bCB])
                    if tt % 4 != 3:
                        continue
                    ts = slice(T4 * TT, (T4 + 1) * TT)
                    S.op("act", lambda q: q.activation(DEN, PS[2], AF.Abs), reads=[bPS[2]], writes=[bDEN])
                    S.op("act", lambda q: q.copy(HC[:, 0, :], PS[0]), reads=[bPS[0]], writes=[bHC])
                    S.op("dve", lambda q: q.tensor_copy(HC[:, 1, :], PS[1]), reads=[bPS[1]], writes=[bHC])
                    S.op("dve", lambda q, ts=ts: q.tensor_tensor(DEN, DEN, EBI[:, ts], ALU.max), reads=[bDEN, bEBI], writes=[bDEN])
                    S.op("dve", lambda q: q.reciprocal(DEN, DEN), reads=[bDEN], writes=[bDEN])
                    for cc in range(2):
                        p = 6 + cc
                        for k in range(KC):
                            S.op("pe", lambda q, k=k, ts=ts, p=p, cc=cc, so=so: q.matmul(PS[p], WGU[so][:, k, cc * 128:(cc + 1) * 128], Xb[:, k, ts], start=(k == 0), stop=(k == KC - 1)),
                                 reads=[bWGU[so], bXb[k][T4]], writes=[bPS[p]])
                        S.op("act", lambda q, cc=cc, p=p: q.activation(SG[:, cc, :], PS[p], AF.Sigmoid), reads=[bPS[p]], writes=[bSG])
                    for vc in range(2):
                        S.op("dve", lambda q, vc=vc: q.tensor_tensor(HC[:, vc, :], HC[:, vc, :], DEN, ALU.mult), reads=[bHC, bDEN], writes=[bHC])
                        S.op("dve", lambda q, vc=vc: q.tensor_tensor(HC[:, vc, :], HC[:, vc, :], SG[:, vc, :], ALU.mult), reads=[bHC, bSG], writes=[bHC])
                    S.op("act", lambda q: q.copy(HCB, HC), reads=[bHC], writes=[bHCB])
                    S.op("act", lambda q: q.activation(HCQ, HC, AF.Square), reads=[bHC], writes=[bHCQ])
                    for vc in range(2):
                        S.op("pe", lambda q, vc=vc: q.matmul(PS[6], O256, HCB[:, vc, :], start=(vc == 0), stop=(vc == 1)), reads=[bK, bHCB], writes=[bPS[6]])
                    for vc in range(2):
                        S.op("pe", lambda q, vc=vc: q.matmul(PS[7], O256, HCQ[:, vc, :], start=(vc == 0), stop=(vc == 1)), reads=[bK, bHCQ], writes=[bPS[7]])
                    S.op("act", lambda q: q.copy(MEAN, PS[6]), reads=[bPS[6]], writes=[bST])
                    S.op("dve", lambda q: q.tensor_tensor(NMR, MEAN, MEAN, ALU.mult), reads=[bST], writes=[bST])
                    S.op("dve", lambda q: q.tensor_tensor(RSTD, PS[7], NMR, ALU.subtract), reads=[bPS[7], bST], writes=[bST])
                    S.op("dve", lambda q: q.tensor_scalar_add(RSTD, RSTD, LN_EPS), reads=[bST], writes=[bST])
                    S.op("act", lambda q: q.sqrt(RSTD, RSTD), reads=[bST], writes=[bST])
                    S.op("dve", lambda q: q.reciprocal(RSTD, RSTD), reads=[bST], writes=[bST])
                    S.op("dve", lambda q: q.scalar_tensor_tensor(NMR, MEAN, -1.0, RSTD, ALU.mult, ALU.mult), reads=[bST], writes=[bST])
                    for vc in range(2):
                        S.op("dve", lambda q, vc=vc: q.tensor_tensor(HC[:, vc, :], HC[:, vc, :], RSTD, ALU.mult), reads=[bHC, bST], writes=[bHC])
                        S.op("dve", lambda q, vc=vc: q.tensor_tensor(HC[:, vc, :], HC[:, vc, :], NMR, ALU.add), reads=[bHC, bST], writes=[bHC])
                        S.op("act", lambda q, vc=vc, h=h: q.activation(HN[:, vc, :], HC[:, vc, :], AF.Identity, scale=CM[:, 64 + 2 * h + vc:65 + 2 * h + vc]),
                             reads=[bHC, bK], writes=[bHN])
                    for i in range(KC):
                        po = 4 + (i % 2)
                        for vc in range(2):
                            S.op("pe", lambda q, i=i, vc=vc, po=po, wsl=wsl: q.matmul(PS[po], WD[wsl[vc]][:, i * 128:(i + 1) * 128], HN[:, vc, :], start=(vc == 0), stop=(vc == 1)),
                                 reads=[bWD[wsl[vc]], bHN], writes=[bPS[po]])
                        add_into_x(i, T4, po, h == 0)
                guS.release(po_)
                wdS.release(wp0)
                wdS.release(wp1)
                if sg == 0:
                    S.dma("sp", sctx[h].rearrange("p (c f) -> p c f", c=2), CF, reads=[bCF], writes=[db("sc%d" % h)])
            barrier()

        for sg in range(NSEG):
            for k in range(KC):
                S.dma("sp", X[:, k, :], xT[sg][:, k * TOK:(k + 1) * TOK], writes=bX[k])
            for k in range(KC):
                for t in range(NT):
                    S.op("act", lambda q, k=k, t=t: q.copy(Xb[:, k, t * TT:(t + 1) * TT], X[:, k, t * TT:(t + 1) * TT]),
                         reads=[bX[k][t]], writes=[bXb[k][t]])
            for (kind, layer, f) in phases:
                if kind == "ffn":
                    ffn(layer, f)
                else:
                    [moba_mixer, pool_mixer, mlstm_mixer][layer % 3](layer, sg)
                    layer_norm(layer, 1)
            for k in range(KC):
                S.dma("sp", yT[sg][:, k * TOK:(k + 1) * TOK], X[:, k, :], reads=bX[k], is_output=True)
        S.finish()
        with nc.Block() as block:
            S.emit(block)
    return nc


def _gu_layout(w):
    n = w.shape[1] // 256
    return np.ascontiguousarray(w.reshape(KC, 128, n, 256).transpose(2, 1, 0, 3)).reshape(n, 128, KC * 256)


def prep_inputs(inputs, phases):
    f32 = lambda a: np.asarray(a, dtype=np.float32)
    x = f32(inputs["x"])
    xs = []
    for b in range(BATCH):
        a = x[b].reshape(NSEG, TOK, KC, 128).transpose(0, 3, 2, 1)
        xs.append(np.ascontiguousarray(a).reshape(NSEG, 128, KC * TOK))
    wgu_in = f32(inputs["ffn_w_gu"])
    g = wgu_in[..., :DFF].reshape(DEPTH, 2, KC, 128, NJ, 128)
    u = wgu_in[..., DFF:].reshape(DEPTH, 2, KC, 128, NJ, 128)
    gu = np.stack([g, u], axis=5)
    gu = np.ascontiguousarray(gu.transpose(0, 1, 4, 3, 2, 5, 6)).reshape(DEPTH * 2 * NJ, 128, KC * 256)
    pieces = [gu]
    a_in, b_in, c_in = f32(inputs["a_w_in"]), f32(inputs["b_w_in"]), f32(inputs["c_w_in"])
    for layer in range(DEPTH):
        mk, j = layer % 3, layer // 3
        if mk == 0:
            w = a_in[j]
            q, k, v = w[:, 0:D], w[:, D:2 * D], w[:, 2 * D:3 * D]
            cols = []
            for hp in range(4):
                cols.append(v[:, hp * 256:(hp + 1) * 256])
                for h in (2 * hp, 2 * hp + 1):
                    cols.append(np.concatenate([q[:, h * 128:(h + 1) * 128], k[:, h * 128:(h + 1) * 128]], axis=1))
            pieces.append(_gu_layout(np.concatenate(cols, axis=1)))
        elif mk == 1:
            pieces.append(_gu_layout(b_in[j]))
        else:
            w = c_in[j]
            cols = []
            for h in range(4):
                for part in range(4):
                    cols.append(w[:, part * D + h * 256: part * D + (h + 1) * 256])
            pieces.append(_gu_layout(np.concatenate(cols, axis=1)))
    wgu = np.concatenate(pieces, axis=0)
    assert wgu.shape[0] == NP_GU, wgu.shape
    wds = [f32(inputs["ffn_w_down"]).reshape(DEPTH * 2 * NJ, 128, D)]
    for layer in range(DEPTH):
        mk, j = layer % 3, layer // 3
        wo = [inputs["a_w_out"], inputs["b_w_out"], inputs["c_w_out"]][mk]
        wds.append(f32(wo)[j].reshape(8, 128, D))
    wd = np.ascontiguousarray(np.concatenate(wds, axis=0))
    lg = f32(inputs["ln_g"]).reshape(DEPTH, 3, KC, 128)
    lb = f32(inputs["ln_b"]).reshape(DEPTH, 3, KC, 128)
    lnp = np.stack([lg, lb], axis=2)
    lnp = np.ascontiguousarray(lnp.transpose(4, 0, 1, 2, 3)).reshape(128, DEPTH * 3 * 2 * KC)
    wg = f32(inputs["b_w_group"])[0].reshape(4, 2, 128, 256).transpose(2, 0, 1, 3)
    pwg = np.ascontiguousarray(wg).reshape(128, 2048)
    invc = np.zeros((4, HALO), np.float32)
    for m in range(4):
        invc[m] = 1.0 / np.minimum(np.arange(HALO) + 1.0, float(2 ** (m + 1)))
    pmisc = np.zeros((128, 72), np.float32)
    pmisc[:, :64] = invc.reshape(1, 64)
    pmisc[:, 64:72] = f32(inputs["b_scale"])[0].reshape(8, 128).T
    cw_in = c_in[0]
    gw = cw_in[:, 4 * D:4 * D + 8]
    cgw = np.ascontiguousarray(gw.reshape(KC, 128, 8).transpose(1, 0, 2)).reshape(128, 64)
    cfrep = np.empty((4, 128, KC, 128), np.float32)
    for h in range(4):
        cfrep[h] = np.broadcast_to(gw[:, 4 + h].reshape(KC, 128).T[:, :, None], (128, KC, 128))
    cfrep = cfrep.reshape(4, 128, KC * 128)
    cmisc = np.zeros((128, 80), np.float32)
    cmisc[:, 0:64] = f32(inputs["c_conv_w"])[0].reshape(4, 16, 128).transpose(2, 1, 0).reshape(128, 64)
    cmisc[:, 64:72] = f32(inputs["c_norm_g"])[0].reshape(8, 128).T
    cmisc[:, 72:80] = np.broadcast_to(f32(inputs["c_b_gates"])[0].reshape(1, 8), (128, 8))
    ii = np.arange(128)
    tri = (ii[:, None] <= ii[None, :]).astype(np.float32)
    cconst = np.concatenate([tri, tri, np.eye(128, dtype=np.float32)], axis=1)
    rb = f32(inputs["rel_bias"])
    arbx = np.concatenate([rb, np.full((1, 8), -30000.0, np.float32)], axis=0)
    delta = np.arange(NDELTA) - 511
    nn = np.maximum(delta, 0)
    nf = np.maximum(nn, 1).astype(np.float32)
    large = 16 + (np.log(nf / np.float32(16)) / np.float32(math.log(128 / 16)) * np.float32(16)).astype(np.int32)
    large = np.minimum(large, 31)
    bucket = np.where(nn < 16, nn, large)
    aoh = np.zeros((33, NDELTA), np.float32)
    for i in range(NDELTA):
        if delta[i] >= 0:
            aoh[bucket[i], i] = 1.0
        else:
            aoh[32, i] = 1.0
    asel = np.zeros((16, 16, 128), np.float32)
    for n in range(16):
        asel[n, n, :] = 1.0
    asel = asel.reshape(16, 2048)
    gbi = np.arange(16)
    pm = np.where(gbi[None, :] < gbi[:, None], 0.0, -1e30).astype(np.float32)
    no = (1.0 - np.eye(16)).astype(np.float32)
    amask = np.ascontiguousarray(np.broadcast_to(np.concatenate([pm.reshape(1, 256), no.reshape(1, 256)], axis=1), (128, 512)))
    aconst = np.concatenate([np.eye(128, dtype=np.float32)[::-1], np.eye(128, dtype=np.float32)], axis=1)
    aconst = np.ascontiguousarray(aconst)
    arb31 = np.ascontiguousarray(np.broadcast_to(rb[31].reshape(1, 8), (128, 8)))
    gu_ids, wd_ids = used_pieces(phases)
    if len(gu_ids) != NP_GU:
        wgu = np.ascontiguousarray(wgu[gu_ids])
    if len(wd_ids) != NP_WD:
        wd = np.ascontiguousarray(wd[wd_ids])
    common = {"wgu": wgu, "wd": wd, "lnp": lnp, "pwg": pwg, "pmisc": pmisc,
              "asel": asel, "amask": amask, "aconst": aconst, "arb31": arb31, "arbx": arbx, "aoh": aoh,
              "cgw": cgw, "cfrep": cfrep, "cmisc": cmisc, "cconst": cconst}
    return [dict(common, xT=xs[c]) for c in range(NCORES)]


def run(inputs, phases=None):
    phases = ALL_PHASES if phases is None else phases
    nc = build_program(phases)
    in_maps = prep_inputs(inputs, phases)
    res = run_bass_kernel_spmd(nc, in_maps, core_ids=list(range(NCORES)))
    out = np.empty((BATCH, SEQ, D), dtype=np.float32)
    for b in range(NCORES):
        yT = np.asarray(res.results[b]["yT"]).reshape(NSEG, 128, KC, TOK)
        out[b] = yT.transpose(0, 3, 2, 1).reshape(SEQ, D)
    return out


def kernel(**inputs):
    return run(inputs)
```
